# Optimizing a Trainium2 kernel written in Bass

```python
import math
import jax, jax.numpy as jnp
from jax import lax
import numpy as np

D_MODEL = 2048
BATCH = 16
SEQ = 2048
DEPTH = 4

N_MIXERS = 2
CONV_E = D_MODEL
CONV_WIDTH = 31
N_HEADS = 16
HEAD_DIM = D_MODEL // N_HEADS
N_KV_GROUPS = 4
GROUP_SIZE = N_HEADS // N_KV_GROUPS
CMP_LEN = 32
CMP_STRIDE = 16
SEL_LEN = 64
N_SELECT = 8
WINDOW = 512
QBLK = 128
NUM_BUCKETS = 32
MAX_DISTANCE = 128
Q_W = N_HEADS * HEAD_DIM
KV_W = N_KV_GROUPS * HEAD_DIM
GATE_W = 3 * N_HEADS
Z_W = Q_W
NSA_IN_W = Q_W + 6 * KV_W + GATE_W + Z_W
NEG = -1e30
FORCE = 1e6
EPS = 1e-6

kernel_name = "hybrid_conformer_nsa_gated_trunk"


def rmsnorm(x, g):
    xf = x.astype(jnp.float32)
    y = xf * lax.rsqrt(jnp.mean(xf * xf, axis=-1, keepdims=True) + EPS)
    return y.astype(x.dtype) * g


def layernorm(x, g, b):
    xf = x.astype(jnp.float32)
    mu = jnp.mean(xf, axis=-1, keepdims=True)
    var = jnp.mean(jnp.square(xf - mu), axis=-1, keepdims=True)
    y = (xf - mu) * lax.rsqrt(var + EPS)
    return y.astype(x.dtype) * g + b


def t5_bucket(dist):
    dist = jnp.maximum(dist, 0)
    max_exact = NUM_BUCKETS // 2
    d = jnp.maximum(dist, max_exact).astype(jnp.float32)
    large = max_exact + (jnp.log(d / max_exact) / math.log(MAX_DISTANCE / max_exact)
                         * (NUM_BUCKETS - max_exact)).astype(jnp.int32)
    large = jnp.minimum(large, NUM_BUCKETS - 1)
    return jnp.where(dist < max_exact, dist, large)


def conformer_mixer(h, w_in, dw_w, dw_b, ln_g, ln_b, w_out):
    u = h @ w_in
    a, b, z = jnp.split(u, 3, axis=-1)
    v = a * jax.nn.sigmoid(b)
    v = lax.conv_general_dilated(
        v, dw_w[:, None, :].astype(v.dtype), window_strides=(1,),
        padding=[(CONV_WIDTH - 1, 0)], dimension_numbers=('NWC', 'WIO', 'NWC'),
        feature_group_count=CONV_E) + dw_b
    v = layernorm(v, ln_g, ln_b)
    v = jax.nn.silu(v) * jax.nn.silu(z)
    return v @ w_out


def nsa_mixer(h, rel_bias, w_in, cmp_pos, ck_w1, ck_w2, cv_w1, cv_w2, w_out):
    B, S, _ = h.shape
    G, R, DK = N_KV_GROUPS, GROUP_SIZE, HEAD_DIM
    u = h @ w_in
    splits = [int(c) for c in np.cumsum([Q_W] + [KV_W] * 6 + [GATE_W])]
    q, kc, vc, ks, vs, kw, vw, gates, z = jnp.split(u, splits, axis=-1)
    q = q.reshape(B, S, G, R, DK)
    kc, vc, ks, vs, kw, vw = [t.reshape(B, S, G, DK) for t in (kc, vc, ks, vs, kw, vw)]
    gate = jax.nn.sigmoid(gates.astype(jnp.float32)).reshape(B, S, G, R, 3)
    scale = DK ** -0.5

    nc = (S - CMP_LEN) // CMP_STRIDE + 1
    tok_idx = np.arange(nc)[:, None] * CMP_STRIDE + np.arange(CMP_LEN)[None, :]
    cmp_end = jnp.asarray(tok_idx[:, -1], dtype=jnp.int32)

    def compress(t, w1, w2):
        blk = t[:, tok_idx] + cmp_pos[:, None, :]
        blk = blk.transpose(0, 1, 3, 2, 4).reshape(B, nc, G, CMP_LEN * DK)
        return jax.nn.silu(blk @ w1) @ w2

    Kc = compress(kc, ck_w1, ck_w2)
    Vc = compress(vc, cv_w1, cv_w2)

    ns = S // SEL_LEN
    n_sel = min(N_SELECT, ns)
    j0 = np.arange(nc)[:, None] * CMP_STRIDE
    s0 = np.arange(ns)[None, :] * SEL_LEN
    overlap = jnp.asarray(((j0 < s0 + SEL_LEN) & (j0 + CMP_LEN > s0)).astype(np.float32))
    blk_start = jnp.arange(ns, dtype=jnp.int32) * SEL_LEN
    Ks_blk = ks.reshape(B, ns, SEL_LEN, G, DK).transpose(0, 3, 1, 2, 4)
    Vs_blk = vs.reshape(B, ns, SEL_LEN, G, DK).transpose(0, 3, 1, 2, 4)
    bi = jnp.arange(B)[:, None, None, None]
    gi = jnp.arange(G)[None, None, :, None]
    tbl_g = rel_bias.reshape(NUM_BUCKETS, G, R).transpose(1, 0, 2)

    kw_pad = jnp.pad(kw, ((0, 0), (WINDOW, 0), (0, 0), (0, 0)))
    vw_pad = jnp.pad(vw, ((0, 0), (WINDOW, 0), (0, 0), (0, 0)))

    def head_bias(bucket):
        bb = rel_bias[bucket]
        return bb.reshape(bucket.shape + (G, R)).transpose(2, 3, 0, 1).astype(jnp.float32)

    def block_fn(i):
        t0 = i * QBLK
        qb = lax.dynamic_slice_in_dim(q, t0, QBLK, axis=1)
        gb = lax.dynamic_slice_in_dim(gate, t0, QBLK, axis=1)
        tq = t0 + jnp.arange(QBLK, dtype=jnp.int32)

        s_c = jnp.einsum('btgrd,bngd->bgrtn', qb, Kc).astype(jnp.float32) * scale
        dist_c = tq[:, None] - cmp_end[None, :]
        valid_c = dist_c >= 0
        s_c = jnp.where(valid_c, s_c + head_bias(t5_bucket(dist_c)), NEG)
        p_c = jnp.where(valid_c, jax.nn.softmax(s_c, axis=-1), 0.0)
        o_c = jnp.einsum('bgrtn,bngd->btgrd', p_c.astype(Vc.dtype), Vc)

        imp = jnp.einsum('bgrtn,ns->btgs', p_c, overlap)
        cur = tq // SEL_LEN
        sb = jnp.arange(ns, dtype=jnp.int32)[None, :]
        forced = (sb == 0) | (sb == cur[:, None]) | (sb == cur[:, None] - 1)
        causal_blk = blk_start[None, :] <= tq[:, None]
        imp = jnp.where(forced[None, :, None, :], FORCE, imp)
        imp = jnp.where(causal_blk[None, :, None, :], imp, NEG)
        _, idx = lax.top_k(imp, n_sel)

        K_sel = Ks_blk[bi, gi, idx].reshape(B, QBLK, G, n_sel * SEL_LEN, DK)
        V_sel = Vs_blk[bi, gi, idx].reshape(B, QBLK, G, n_sel * SEL_LEN, DK)
        kpos = (idx[..., None] * SEL_LEN + jnp.arange(SEL_LEN, dtype=jnp.int32)).reshape(B, QBLK, G, n_sel * SEL_LEN)
        dist_s = tq[None, :, None, None] - kpos
        bias_s = tbl_g[gi, t5_bucket(dist_s)].transpose(0, 2, 4, 1, 3).astype(jnp.float32)
        mask_s = (dist_s >= 0).transpose(0, 2, 1, 3)[:, :, None]
        s_s = jnp.einsum('btgrd,btgkd->bgrtk', qb, K_sel).astype(jnp.float32) * scale
        p_s = jax.nn.softmax(jnp.where(mask_s, s_s + bias_s, NEG), axis=-1)
        o_s = jnp.einsum('bgrtk,btgkd->btgrd', p_s.astype(V_sel.dtype), V_sel)

        kwb = lax.dynamic_slice_in_dim(kw_pad, t0, QBLK + WINDOW, axis=1)
        vwb = lax.dynamic_slice_in_dim(vw_pad, t0, QBLK + WINDOW, axis=1)
        kpos_w = t0 - WINDOW + jnp.arange(QBLK + WINDOW, dtype=jnp.int32)
        dist_w = tq[:, None] - kpos_w[None, :]
        mask_w = (dist_w >= 0) & (dist_w < WINDOW) & (kpos_w[None, :] >= 0)
        s_w = jnp.einsum('btgrd,bkgd->bgrtk', qb, kwb).astype(jnp.float32) * scale
        p_w = jax.nn.softmax(jnp.where(mask_w, s_w + head_bias(t5_bucket(dist_w)), NEG), axis=-1)
        o_w = jnp.einsum('bgrtk,bkgd->btgrd', p_w.astype(vwb.dtype), vwb)

        gb = gb.astype(o_c.dtype)
        return gb[..., 0:1] * o_c + gb[..., 1:2] * o_s + gb[..., 2:3] * o_w

    outs = lax.map(block_fn, jnp.arange(S // QBLK))
    o = outs.transpose(1, 0, 2, 3, 4, 5).reshape(B, S, Q_W)
    return (o * jax.nn.silu(z)) @ w_out


def setup_inputs(seed: int = 0) -> dict:
    key = jax.random.key(seed)
    keys = iter(jax.random.split(key, 128))

    def nrm(shape, scale):
        return jax.random.normal(next(keys), shape, jnp.float32) * scale

    inputs = {
        "x": nrm((BATCH, SEQ, D_MODEL), 1.0),
        "rel_bias": nrm((NUM_BUCKETS, N_HEADS), 0.5),
    }
    for i in range(DEPTH):
        p = f"l{i}_"
        inputs[p + "norm"] = 1.0 + nrm((D_MODEL,), 0.1)
        if i % N_MIXERS == 0:
            inputs[p + "w_in"] = nrm((D_MODEL, 3 * CONV_E), D_MODEL ** -0.5)
            inputs[p + "dw_w"] = nrm((CONV_WIDTH, CONV_E), CONV_WIDTH ** -0.5)
            inputs[p + "dw_b"] = nrm((CONV_E,), 0.02)
            inputs[p + "ln_g"] = 1.0 + nrm((CONV_E,), 0.1)
            inputs[p + "ln_b"] = nrm((CONV_E,), 0.02)
            inputs[p + "w_out"] = nrm((CONV_E, D_MODEL), CONV_E ** -0.5)
        else:
            inputs[p + "w_in"] = nrm((D_MODEL, NSA_IN_W), D_MODEL ** -0.5)
            inputs[p + "cmp_pos"] = nrm((CMP_LEN, HEAD_DIM), 0.5)
            inputs[p + "ck_w1"] = nrm((CMP_LEN * HEAD_DIM, HEAD_DIM), (CMP_LEN * HEAD_DIM) ** -0.5)
            inputs[p + "ck_w2"] = nrm((HEAD_DIM, HEAD_DIM), HEAD_DIM ** -0.5)
            inputs[p + "cv_w1"] = nrm((CMP_LEN * HEAD_DIM, HEAD_DIM), (CMP_LEN * HEAD_DIM) ** -0.5)
            inputs[p + "cv_w2"] = nrm((HEAD_DIM, HEAD_DIM), HEAD_DIM ** -0.5)
            inputs[p + "w_out"] = nrm((Q_W, D_MODEL), Q_W ** -0.5)
    inputs["final_norm"] = 1.0 + nrm((D_MODEL,), 0.1)
    return inputs


def reference(x, rel_bias,
              l0_norm, l0_w_in, l0_dw_w, l0_dw_b, l0_ln_g, l0_ln_b, l0_w_out,
              l1_norm, l1_w_in, l1_cmp_pos, l1_ck_w1, l1_ck_w2, l1_cv_w1, l1_cv_w2, l1_w_out,
              l2_norm, l2_w_in, l2_dw_w, l2_dw_b, l2_ln_g, l2_ln_b, l2_w_out,
              l3_norm, l3_w_in, l3_cmp_pos, l3_ck_w1, l3_ck_w2, l3_cv_w1, l3_cv_w2, l3_w_out,
              final_norm):
    layers = [
        (l0_norm, (l0_w_in, l0_dw_w, l0_dw_b, l0_ln_g, l0_ln_b, l0_w_out)),
        (l1_norm, (l1_w_in, l1_cmp_pos, l1_ck_w1, l1_ck_w2, l1_cv_w1, l1_cv_w2, l1_w_out)),
        (l2_norm, (l2_w_in, l2_dw_w, l2_dw_b, l2_ln_g, l2_ln_b, l2_w_out)),
        (l3_norm, (l3_w_in, l3_cmp_pos, l3_ck_w1, l3_ck_w2, l3_cv_w1, l3_cv_w2, l3_w_out)),
    ]
    for i in range(DEPTH):
        g, params = layers[i]
        h = rmsnorm(x, g)
        if i % N_MIXERS == 0:
            x = x + conformer_mixer(h, *params)
        else:
            x = x + nsa_mixer(h, rel_bias, *params)
    return rmsnorm(x, final_norm)
```

```python
import numpy as np
from contextlib import ExitStack
import concourse.bass as bass
import concourse.mybir as mybir
from concourse.bass_utils import run_bass_kernel_spmd

F32 = mybir.dt.float32
BF16 = mybir.dt.bfloat16
AF = mybir.ActivationFunctionType
ALU = mybir.AluOpType
AX = mybir.AxisListType

D = 2048
S = 2048
NT = S // 128
NSEQ = 2
NCORES = 8
NH = 16
NG = 4
DK = 128
EPS = 1e-6
NEGM = -30000.0
CONF_W = 3 * D
NSA_W = 2048 + 6 * 512 + 48 + 2048


class Owner:
    def __init__(self, K, name):
        self.sem = K.stack.enter_context(K.nc.semaphore(name))
        self.count = 0
        self.name = name


class Eng(Owner):
    def __init__(self, K, name, h):
        super().__init__(K, "e_" + name)
        self.h = h
        self.seen = {}
        self.is_pe = name == "pe"
        self.last_inc = True


class Buf:
    def __init__(self, ap, name=""):
        self.ap = ap
        self.name = name
        self.writers = {}
        self.readers = {}
        self.prev = {}
        self.excl = False

    def __getitem__(self, idx):
        return self.ap[idx]


def _merge(d, own, val):
    if d.get(own, 0) < val:
        d[own] = val


class Kern:
    def __init__(self, nc):
        self.nc = nc
        self.stack = ExitStack()
        self.pe = Eng(self, "pe", nc.tensor)
        self.act = Eng(self, "act", nc.scalar)
        self.dve = Eng(self, "dve", nc.vector)
        self.pool = Eng(self, "pool", nc.gpsimd)
        self.sp = Eng(self, "sp", nc.sync)
        self.engs = [self.pe, self.act, self.dve, self.pool, self.sp]
        self.slots = {}
        self.n_dram = 0

    def slot(self, name):
        if name not in self.slots:
            self.slots[name] = Owner(self, "d_" + name)
        return self.slots[name]

    def sb(self, st, name, shape, dt):
        self.n_dram += 1
        nm = f"{name}_{self.n_dram}"
        return Buf(st.enter_context(self.nc.sbuf_tensor(nm, list(shape), dt)), nm)

    def dram(self, name, shape, dt, kind=None):
        if kind is None:
            t = self.nc.dram_tensor(name, list(shape), dt)
        else:
            t = self.nc.dram_tensor(name, list(shape), dt, kind=kind)
        return t.ap()

    def _deps(self, eng, reads, writes):
        deps = {}

        def add(d, raw):
            for own, val in d.items():
                if own is eng and (eng.is_pe or not raw):
                    continue
                _merge(deps, own, val)

        for b in reads:
            add(b.writers, True)
            if b.excl:
                add(b.readers, False)
        for b in writes:
            add(b.writers, False)
            add(b.readers, False)
            add(b.prev, False)
        return [(o, v) for o, v in deps.items() if eng.seen.get(o, 0) < v]

    def _emit_waits(self, eng, need, fn):
        for o, v in need[1:]:
            eng.h.wait_ge(o.sem, v)
        ins = fn()
        if need:
            ins._wait_ge(need[0][0].sem, need[0][1])
        for o, v in need:
            eng.seen[o] = v
        return ins

    def _update(self, ev, reads, writes):
        own, val = ev
        for b in reads:
            _merge(b.readers, own, val)
        for b in writes:
            if b.readers:
                prev = dict(b.readers)
                for o, v in b.writers.items():
                    _merge(prev, o, v)
                b.prev = prev
                b.readers = {}
                b.writers = {}
            _merge(b.writers, own, val)

    def op(self, eng, fn, reads=(), writes=(), inc=True):
        need = self._deps(eng, reads, writes)
        ins = self._emit_waits(eng, need, fn)
        if inc:
            eng.count += 1
            ins.then_inc(eng.sem, 1)
            ev = (eng, eng.count)
            eng.last_inc = True
        else:
            assert eng.is_pe
            ev = (eng, eng.count + 1)
            eng.last_inc = False
        self._update(ev, reads, writes)
        return ins

    def dma(self, q, out, in_, slot, reads=(), writes=(), group=False, **kw):
        need = self._deps(q, reads, writes)
        if not group and slot.count > 0 and q.seen.get(slot, 0) < slot.count:
            need = [(o, v) for o, v in need if o is not slot] + [(slot, slot.count)]
        ins = self._emit_waits(q, need, lambda: q.h.dma_start(out=out, in_=in_, **kw))
        slot.count += 16
        ins.then_inc(slot.sem, 16)
        self._update((slot, slot.count), reads, writes)
        return ins

    def barrier(self):
        assert self.pe.last_inc
        owners = list(self.engs) + list(self.slots.values())
        for e in self.engs:
            for o in owners:
                if o is e or o.count == 0:
                    continue
                if e.seen.get(o, 0) < o.count:
                    e.h.wait_ge(o.sem, o.count)
                    e.seen[o] = o.count


class Prog:
    def __init__(self, layers=(0, 1, 2, 3), final_norm=True, nseq=NSEQ, dbg=99):
        self.dbg = dbg
        self.layers = list(layers)
        self.final_norm = final_norm
        self.nseq = nseq
        nc = bass.Bass("TRN2", target_bir_lowering=False)
        self.nc = nc
        self.K = Kern(nc)
        self.build()

    def declare_inputs(self):
        K = self.K
        ns = self.nseq
        I = {}

        def inp(name, shape, dt=F32):
            I[name] = K.dram(name, shape, dt, kind="ExternalInput")

        inp("x", [ns, S, D])
        inp("c_ident", [128, 128])
        for l in self.layers:
            p = f"l{l}_"
            inp(p + "norm", [1, D])
            inp(p + "w_out", [D, D])
            if l % 2 == 0:
                inp(p + "w_in", [D, CONF_W])
                inp(p + "dw_wT", [128, 16 * 31])
                inp(p + "dw_b", [128, 16])
                inp(p + "ln_g", [128, 16])
                inp(p + "ln_b", [128, 16])
            else:
                inp(p + "w_in", [D, NSA_W])
                inp(p + "posT", [128, 32])
                inp(p + "ck_w1", [4096, 128])
                inp(p + "ck_w2", [128, 128])
                inp(p + "cv_w1", [4096, 128])
                inp(p + "cv_w2", [128, 128])
        if any(l % 2 == 1 for l in self.layers):
            inp("rb_A", [128, 16, 128])
            inp("rb_B", [128, 16, 128])
            inp("rb_C", [248, 16, 128])
            inp("rb_far", [1, 16])
            inp("m_B", [128, 128])
            inp("m_C", [248, 128])
            inp("m_W0", [128, 128])
            inp("c_E", [128, 64 * 128])
            inp("c_keep", [128, 16 * 32])
            inp("c_add", [128, 16 * 32])
            inp("c_ov", [128, 34])
        inp("final_norm", [1, D])
        self.I = I
        self.out = K.dram("y", [ns, S, D], F32, kind="ExternalOutput")

    def build(self):
        K = self.K
        nc = self.nc
        ns = self.nseq
        self.declare_inputs()
        I = self.I
        with K.stack:
            st = K.stack
            self.ident_f = K.sb(st, "ident_f", [128, 128], F32)
            self.ident_b = K.sb(st, "ident_b", [128, 128], BF16)
            self.ones_f = K.sb(st, "ones_f", [128, 128], F32)
            self.gbc = K.sb(st, "gbc", [128, D], F32)
            self.gfin = K.sb(st, "gfin", [128, D], F32)
            self.small = K.sb(st, "small", [128, 64], F32)
            pt = st.enter_context(nc.psum_tensor("psum", [128, 8, 512], F32))
            self.pt = pt
            self.P = [Buf(pt[:, b, :], f"ps{b}") for b in range(8)]
            for b_ in self.P:
                b_.excl = True
            sl = K.slot("const")
            K.dma(K.sp, self.ident_f[:], I["c_ident"], sl, writes=[self.ident_f])
            K.op(K.act, lambda: nc.scalar.copy(out=self.ident_b[:], in_=self.ident_f[:]),
                 reads=[self.ident_f], writes=[self.ident_b])
            K.op(K.pool, lambda: nc.gpsimd.memset(self.ones_f[:], 1.0), writes=[self.ones_f])
            K.dma(K.sp, self.gfin[:], bass.AP(I["final_norm"].tensor, 0, [[0, 128], [1, D]]), sl,
                  writes=[self.gfin])
            K.op(K.pool, lambda: nc.gpsimd.memset(self.small[:, 0:1], EPS), writes=[self.small])
            nl = len(self.layers)
            self.X = [I["x"]]
            for i in range(nl - 1):
                self.X.append(K.dram(f"xs{i}", [ns, S, D], F32))
            self.X.append(self.out)
            self.Xb = [[[Buf(None, f"X{i}_{s}_{t}") for t in range(NT)] for s in range(ns)]
                       for i in range(nl + 1)]
            self.WT = K.dram("wt_s", [ns, NT, 128, 16, 128], BF16)
            self.WTb = [Buf(None, f"WT{s}") for s in range(ns)]
            self.Ys = K.dram("y_s", [ns, 16, 128, S], F32)
            self.Yb = [[Buf(None, f"Y{s}_{c}") for c in range(16)] for s in range(ns)]
            if any(l % 2 == 1 for l in self.layers):
                self.QT = K.dram("qt_s", [ns, NT, 128, 16, 128], BF16)
                self.SZ = K.dram("sz_s", [ns, NT, 128, 16, 128], BF16)
                self.KT = K.dram("kt_s", [ns, 4, 4, 128, S], BF16)
                self.VA = K.dram("va_s", [ns, 2, NT, 128, 520], BF16)
                self.TA = K.dram("ta_s", [128, 16, 128], F32)
                self.TB = K.dram("tb_s", [128, 16, 128], F32)
                self.TC = K.dram("tc_s", [248, 16, 128], F32)
                self.QTb = [Buf(None, f"QT{s}") for s in range(ns)]
                self.SZb = [Buf(None, f"SZ{s}") for s in range(ns)]
                self.KTb = [Buf(None, f"KT{s}") for s in range(ns)]
                self.VAb = [Buf(None, f"VA{s}") for s in range(ns)]
                self.Tb = Buf(None, "Ttab")
                self.nsa_setup()
            K.barrier()
            for li, l in enumerate(self.layers):
                last = (li == nl - 1) and self.final_norm
                sl = K.slot("const")
                K.dma(K.sp, self.gbc[:], bass.AP(I[f"l{l}_norm"].tensor, 0, [[0, 128], [1, D]]), sl,
                      writes=[self.gbc])
                if l % 2 == 0:
                    self.conformer_layer(li, l)
                else:
                    self.nsa_layer(li, l)
                self.g2_phase(li, l, last)
            K.barrier()

    def p1_phase(self, st, li, s, hT):
        K, nc = self.K, self.nc
        xr = [K.sb(st, f"p1x{i}", [128, D], F32) for i in range(2)]
        hb = [K.sb(st, f"p1h{i}", [128, D], BF16) for i in range(2)]
        junk = K.sb(st, "p1junk", [128, D], BF16)
        stat = [K.sb(st, f"p1s{i}", [128, 4], F32) for i in range(2)]
        X = self.X[li]
        xsl = [K.slot("x0"), K.slot("x1")]

        def load(tt):
            K.dma(K.sp, xr[tt % 2][:], X[s, tt * 128:(tt + 1) * 128, :], xsl[tt % 2],
                  reads=[self.Xb[li][s][tt]], writes=[xr[tt % 2]])

        load(0)
        for tt in range(NT):
            if tt + 1 < NT:
                load(tt + 1)
            x = xr[tt % 2]
            h = hb[tt % 2]
            sm = stat[tt % 2]
            K.op(K.act, lambda: nc.scalar.activation(out=junk[:], in_=x[:], func=AF.Square,
                                                     accum_out=sm[:, 0:1]),
                 reads=[x], writes=[junk, sm])
            K.op(K.act, lambda: nc.scalar.activation(out=sm[:, 1:2], in_=sm[:, 0:1], func=AF.Sqrt,
                                                     scale=1.0 / D, bias=self.eps_ap()),
                 reads=[sm, self.small], writes=[sm])
            K.op(K.dve, lambda: nc.vector.reciprocal(out=sm[:, 2:3], in_=sm[:, 1:2]),
                 reads=[sm], writes=[sm])
            K.op(K.dve, lambda: nc.vector.scalar_tensor_tensor(
                out=h[:], in0=x[:], scalar=sm[:, 2:3], in1=self.gbc[:], op0=ALU.mult, op1=ALU.mult),
                reads=[x, sm, self.gbc], writes=[h])
            pb = (tt % 2) * 2
            pv = self.pt[:, pb:pb + 2, :].bitcast(BF16)
            for c in range(16):
                bank = self.P[pb + c // 8]
                o = pv[:, c // 8, (c % 8) * 128:(c % 8 + 1) * 128]
                K.op(K.pe, lambda: nc.tensor.transpose(out=o, in_=h[:, c * 128:(c + 1) * 128],
                                                       identity=self.ident_b[:]),
                     reads=[h, self.ident_b], writes=[bank], inc=(c % 8 == 7))
            for half in range(2):
                src = pv[:, half, :].rearrange("p (c t) -> p c t", c=8)
                dst = hT[:, half * 8:(half + 1) * 8, tt * 128:(tt + 1) * 128]
                eng = K.act if half == 0 else K.dve
                if half == 0:
                    K.op(K.act, lambda: nc.scalar.copy(out=dst, in_=src), reads=[self.P[pb]], writes=[hT])
                else:
                    K.op(K.dve, lambda: nc.vector.tensor_copy(out=dst, in_=src), reads=[self.P[pb + 1]],
                         writes=[hT])

    def eps_ap(self):
        return self.small[:, 0:1]

    def load_w_cast(self, dst_buf, dst_ap, w_ap, col0, ncols, slot, group=False):
        K = self.K
        src = w_ap.rearrange("(k p) n -> p k n", p=128)[:, :, col0:col0 + ncols]
        K.dma(K.pool, dst_ap, src, slot, writes=[dst_buf], group=group)

    def g2_phase(self, li, l, last):
        K, nc = self.K, self.nc
        I = self.I
        with ExitStack() as st:
            wo = K.sb(st, "g2w", [128, 16, D], BF16)
            wsl = K.slot("wo")
            for j in range(4):
                self.load_w_cast(wo, wo[:, :, j * 512:(j + 1) * 512], I[f"l{l}_w_out"], j * 512, 512, wsl,
                                 group=(j > 0))
            xr = [K.sb(st, f"g2x{i}", [128, D], F32) for i in range(2)]
            xo = [K.sb(st, f"g2o{i}", [128, D], F32) for i in range(2)]
            wt = [K.sb(st, f"g2t{i}", [128, 16, 128], BF16) for i in range(2)]
            junk = K.sb(st, "g2junk", [128, D], BF16)
            stat = [K.sb(st, f"g2s{i}", [128, 4], F32) for i in range(2)]
            xsl = [K.slot("x0"), K.slot("x1")]
            tsl = [K.slot("t0"), K.slot("t1")]
            osl = [K.slot("o0"), K.slot("o1")]
            Xi, Xo = self.X[li], self.X[li + 1]
            steps = [(s, tt) for s in range(self.nseq) for tt in range(NT)]

            def load(i):
                s, tt = steps[i]
                K.dma(K.sp, xr[i % 2][:], Xi[s, tt * 128:(tt + 1) * 128, :], xsl[i % 2],
                      reads=[self.Xb[li][s][tt]], writes=[xr[i % 2]])
                K.dma(K.sp, wt[i % 2][:], self.WT[s, tt], tsl[i % 2], reads=[self.WTb[s]], writes=[wt[i % 2]])

            load(0)
            for i, (s, tt) in enumerate(steps):
                if i + 1 < len(steps):
                    load(i + 1)
                x, o, w = xr[i % 2], xo[i % 2], wt[i % 2]
                for db in range(4):
                    bank = self.P[(i * 4 + db) % 2]
                    for c in range(16):
                        K.op(K.pe, lambda: nc.tensor.matmul(bank[:], lhsT=w[:, c, :],
                                                            rhs=wo[:, c, db * 512:(db + 1) * 512],
                                                            start=(c == 0), stop=(c == 15)),
                             reads=[w, wo], writes=[bank], inc=(c == 15))
                    K.op(K.dve, lambda: nc.vector.tensor_tensor(out=o[:, db * 512:(db + 1) * 512], in0=bank[:],
                                                                in1=x[:, db * 512:(db + 1) * 512], op=ALU.add),
                         reads=[bank, x], writes=[o])
                if last:
                    sm = stat[i % 2]
                    K.op(K.act, lambda: nc.scalar.activation(out=junk[:], in_=o[:], func=AF.Square,
                                                             accum_out=sm[:, 0:1]),
                         reads=[o], writes=[junk, sm])
                    K.op(K.act, lambda: nc.scalar.activation(out=sm[:, 1:2], in_=sm[:, 0:1], func=AF.Sqrt,
                                                             scale=1.0 / D, bias=self.eps_ap()),
                         reads=[sm, self.small], writes=[sm])
                    K.op(K.dve, lambda: nc.vector.reciprocal(out=sm[:, 2:3], in_=sm[:, 1:2]),
                         reads=[sm], writes=[sm])
                    K.op(K.dve, lambda: nc.vector.scalar_tensor_tensor(
                        out=o[:], in0=o[:], scalar=sm[:, 2:3], in1=self.gfin[:], op0=ALU.mult, op1=ALU.mult),
                        reads=[o, sm, self.gfin], writes=[o])
                K.dma(K.sp, Xo[s, tt * 128:(tt + 1) * 128, :], o[:], osl[i % 2], reads=[o],
                      writes=[self.Xb[li + 1][s][tt]])
            K.barrier()

    def conformer_layer(self, li, l):
        K, nc = self.K, self.nc
        I = self.I
        p = f"l{l}_"
        W = I[p + "w_in"]
        for s in range(self.nseq):
            with ExitStack() as st:
                hT = K.sb(st, "hT", [128, 16, S], BF16)
                with ExitStack() as st1:
                    self.p1_phase(st1, li, s, hT)
                    K.barrier()
                dwT = K.sb(st, "dwT", [128, 16, 31], F32)
                dwb = K.sb(st, "dwb", [128, 16], F32)
                lng = K.sb(st, "lng", [128, 16], F32)
                lnb = K.sb(st, "lnb", [128, 16], F32)
                csl = K.slot("const")
                K.dma(K.sp, dwT[:], I[p + "dw_wT"].rearrange("p (c k) -> p c k", c=16), csl, writes=[dwT])
                K.dma(K.sp, dwb[:], I[p + "dw_b"], csl, writes=[dwb])
                K.dma(K.sp, lng[:], I[p + "ln_g"], csl, writes=[lng])
                K.dma(K.sp, lnb[:], I[p + "ln_b"], csl, writes=[lnb])
                S1 = K.sb(st, "S1", [128, S], F32)
                S2 = K.sb(st, "S2", [128, S], F32)
                wa = [K.sb(st, f"wa{i}", [128, 16, 128], BF16) for i in range(2)]
                wb = [K.sb(st, f"wb{i}", [128, 16, 128], BF16) for i in range(2)]
                wasl = [K.slot("wa0"), K.slot("wa1")]
                wbsl = [K.slot("wb0"), K.slot("wb1")]
                with ExitStack() as st2:
                    v = [K.sb(st2, f"v{i}", [128, 30 + S], BF16) for i in range(2)]
                    dg = [K.sb(st2, f"dg{i}", [128, 31, 128], BF16) for i in range(2)]
                    sg = [K.sb(st2, f"sg{i}", [128, 512], F32) for i in range(2)]
                    ys = [K.sb(st2, f"ys{i}", [128, 512], F32) for i in range(2)]
                    yq = [K.sb(st2, f"yq{i}", [128, 512], F32) for i in range(2)]
                    ysl = [K.slot("ys0"), K.slot("ys1")]
                    for i in range(2):
                        K.op(K.pool, lambda: nc.gpsimd.memset(v[i][:, 0:30], 0.0), writes=[v[i]])

                    def loadw(c):
                        self.load_w_cast(wa[c % 2], wa[c % 2][:], W, c * 128, 128, wasl[c % 2])
                        self.load_w_cast(wb[c % 2], wb[c % 2][:], W, D + c * 128, 128, wbsl[c % 2])

                    def gemm_glu(c, tb):
                        j = c * 4 + tb
                        pa, pb = self.P[(j % 2) * 3], self.P[(j % 2) * 3 + 1]
                        for (ps_, w_) in ((pa, wa[c % 2]), (pb, wb[c % 2])):
                            for k in range(16):
                                K.op(K.pe, lambda: nc.tensor.matmul(ps_[:], lhsT=w_[:, k, :],
                                                                    rhs=hT[:, k, tb * 512:(tb + 1) * 512],
                                                                    start=(k == 0), stop=(k == 15)),
                                     reads=[w_, hT], writes=[ps_], inc=(k == 15))
                        g_ = sg[j % 2]
                        K.op(K.act, lambda: nc.scalar.activation(out=g_[:], in_=pb[:], func=AF.Sigmoid),
                             reads=[pb], writes=[g_])
                        vv = v[c % 2]
                        K.op(K.dve, lambda: nc.vector.tensor_tensor(
                            out=vv[:, 30 + tb * 512:30 + (tb + 1) * 512], in0=pa[:], in1=g_[:], op=ALU.mult),
                            reads=[pa, g_], writes=[vv])

                    def conv(c, tb):
                        j = c * 4 + tb
                        py = self.P[(j % 2) * 3 + 2]
                        vv, dd = v[c % 2], dg[c % 2]
                        for k in range(31):
                            K.op(K.pe, lambda: nc.tensor.matmul(py[:], lhsT=dd[:, k, :],
                                                                rhs=vv[:, k + tb * 512:k + (tb + 1) * 512],
                                                                start=(k == 0), stop=(k == 30)),
                                 reads=[dd, vv], writes=[py], inc=(k == 30))
                        y_, q_ = ys[j % 2], yq[j % 2]
                        K.op(K.act, lambda: nc.scalar.activation(out=y_[:], in_=py[:], func=AF.Identity,
                                                                 bias=dwb[:, c:c + 1]),
                             reads=[py, dwb], writes=[y_])
                        K.op(K.act, lambda: nc.scalar.activation(out=q_[:], in_=py[:], func=AF.Square,
                                                                 bias=dwb[:, c:c + 1]),
                             reads=[py, dwb], writes=[q_])
                        sl_ = slice(tb * 512, (tb + 1) * 512)
                        if c == 0:
                            K.op(K.pool, lambda: nc.gpsimd.tensor_copy(out=S1[:, sl_], in_=y_[:]),
                                 reads=[y_], writes=[S1])
                            K.op(K.pool, lambda: nc.gpsimd.tensor_copy(out=S2[:, sl_], in_=q_[:]),
                                 reads=[q_], writes=[S2])
                        else:
                            K.op(K.pool, lambda: nc.gpsimd.tensor_tensor(out=S1[:, sl_], in0=S1[:, sl_],
                                                                         in1=y_[:], op=ALU.add),
                                 reads=[y_, S1], writes=[S1])
                            K.op(K.pool, lambda: nc.gpsimd.tensor_tensor(out=S2[:, sl_], in0=S2[:, sl_],
                                                                         in1=q_[:], op=ALU.add),
                                 reads=[q_, S2], writes=[S2])
                        K.dma(K.sp, self.Ys[s, c, :, sl_], y_[:], ysl[j % 2], reads=[y_], writes=[self.Yb[s][c]])

                    def mkdiag(c):
                        dd = dg[c % 2]
                        K.op(K.pool, lambda: nc.gpsimd.tensor_tensor(
                            out=dd[:],
                            in0=self.ident_b[:].unsqueeze(1).broadcast_to([128, 31, 128]),
                            in1=dwT[:, c, :].unsqueeze(2).broadcast_to([128, 31, 128]),
                            op=ALU.mult),
                            reads=[self.ident_b, dwT], writes=[dd])

                    steps = [(c, tb) for c in range(16) for tb in range(4)]
                    loadw(0)
                    mkdiag(0)
                    gemm_glu(0, 0)
                    for i, (c, tb) in enumerate(steps):
                        if tb == 0 and c + 1 < 16:
                            loadw(c + 1)
                            mkdiag(c + 1)
                        if i + 1 < len(steps):
                            gemm_glu(*steps[i + 1])
                        conv(c, tb)
                    for tb in range(4):
                        sl_ = slice(tb * 512, (tb + 1) * 512)
                        p1, p2 = self.P[6], self.P[7]
                        K.op(K.pe, lambda: nc.tensor.matmul(p1[:], lhsT=self.ones_f[:], rhs=S1[:, sl_],
                                                            start=True, stop=True),
                             reads=[self.ones_f, S1], writes=[p1])
                        K.op(K.pe, lambda: nc.tensor.matmul(p2[:], lhsT=self.ones_f[:], rhs=S2[:, sl_],
                                                            start=True, stop=True),
                             reads=[self.ones_f, S2], writes=[p2])
                        K.op(K.act, lambda: nc.scalar.mul(out=S1[:, sl_], in_=p1[:], mul=1.0 / D),
                             reads=[p1], writes=[S1])
                        t_ = sg[0]
                        K.op(K.dve, lambda: nc.vector.tensor_tensor(out=t_[:], in0=S1[:, sl_], in1=S1[:, sl_],
                                                                    op=ALU.mult),
                             reads=[S1], writes=[t_])
                        K.op(K.dve, lambda: nc.vector.scalar_tensor_tensor(
                            out=t_[:], in0=p2[:], scalar=1.0 / D, in1=t_[:], op0=ALU.mult, op1=ALU.subtract),
                            reads=[p2, t_], writes=[t_])
                        K.op(K.act, lambda: nc.scalar.activation(out=t_[:], in_=t_[:], func=AF.Sqrt,
                                                                 bias=self.eps_ap()),
                             reads=[t_, self.small], writes=[t_])
                        K.op(K.dve, lambda: nc.vector.reciprocal(out=S2[:, sl_], in_=t_[:]),
                             reads=[t_], writes=[S2])
                    K.barrier()
                with ExitStack() as st3:
                    yb = [K.sb(st3, f"yb{i}", [128, S], F32) for i in range(2)]
                    wTb = [K.sb(st3, f"wTb{i}", [128, S], BF16) for i in range(2)]
                    szb = [K.sb(st3, f"szb{i}", [128, 512], F32) for i in range(2)]
                    s1b = [K.sb(st3, f"s1b{i}", [128, 512], F32) for i in range(2)]
                    ybsl = [K.slot("yb0"), K.slot("yb1")]
                    wtsl = [K.slot("wt0"), K.slot("wt1")]

                    def load3(c):
                        self.load_w_cast(wa[c % 2], wa[c % 2][:], W, 2 * D + c * 128, 128, wasl[c % 2])
                        K.dma(K.sp, yb[c % 2][:], self.Ys[s, c], ybsl[c % 2], reads=[self.Yb[s][c]],
                              writes=[yb[c % 2]])

                    load3(0)
                    for c in range(16):
                        if c + 1 < 16:
                            load3(c + 1)
                        y_, w_ = yb[c % 2], wTb[c % 2]
                        for tb in range(4):
                            j = c * 4 + tb
                            sl_ = slice(tb * 512, (tb + 1) * 512)
                            pz = self.P[j % 2]
                            for k in range(16):
                                K.op(K.pe, lambda: nc.tensor.matmul(pz[:], lhsT=wa[c % 2][:, k, :], rhs=hT[:, k, sl_],
                                                                    start=(k == 0), stop=(k == 15)),
                                     reads=[wa[c % 2], hT], writes=[pz], inc=(k == 15))
                            z_, a_ = szb[j % 2], s1b[j % 2]
                            K.op(K.act, lambda: nc.scalar.activation(out=z_[:], in_=pz[:], func=AF.Silu),
                                 reads=[pz], writes=[z_])
                            K.op(K.dve, lambda: nc.vector.tensor_tensor(out=y_[:, sl_], in0=y_[:, sl_],
                                                                        in1=S1[:, sl_], op=ALU.subtract),
                                 reads=[y_, S1], writes=[y_])
                            K.op(K.dve, lambda: nc.vector.tensor_tensor(out=y_[:, sl_], in0=y_[:, sl_],
                                                                        in1=S2[:, sl_], op=ALU.mult),
                                 reads=[y_, S2], writes=[y_])
                            K.op(K.act, lambda: nc.scalar.activation(out=a_[:], in_=y_[:, sl_], func=AF.Silu,
                                                                     scale=lng[:, c:c + 1], bias=lnb[:, c:c + 1]),
                                 reads=[y_, lng, lnb], writes=[a_])
                            K.op(K.dve, lambda: nc.vector.tensor_tensor(out=w_[:, sl_], in0=a_[:], in1=z_[:],
                                                                        op=ALU.mult),
                                 reads=[a_, z_], writes=[w_])
                        dst = self.WT[s].rearrange("t p c j -> p t c j")[:, :, c, :]
                        K.dma(K.sp, dst, w_[:].rearrange("p (t j) -> p t j", t=NT), wtsl[c % 2], reads=[w_],
                              writes=[self.WTb[s]])
                    K.barrier()

    def nsa_setup(self):
        K, nc, I = self.K, self.nc, self.I
        with ExitStack() as st:
            far = K.sb(st, "far", [128, 16], F32)
            sl = K.slot("const")
            K.dma(K.sp, far[:], bass.AP(I["rb_far"].tensor, 0, [[0, 128], [1, 16]]), sl, writes=[far])
            jobs = [("rb_A", None, self.TA, 0, 128), ("rb_B", "m_B", self.TB, 0, 128),
                    ("rb_C", "m_C", self.TC, 0, 128), ("rb_C", "m_C", self.TC, 128, 120)]
            for ji, (rn, mn, dst, r0, nr) in enumerate(jobs):
                raw = K.sb(st, f"raw{ji}", [128, 16, 128], F32)
                K.dma(K.sp, raw[0:nr], I[rn][r0:r0 + nr], K.slot(f"x{ji % 2}"), writes=[raw])
                K.op(K.dve, lambda: nc.vector.tensor_tensor(
                    out=raw[0:nr], in0=raw[0:nr], in1=far[0:nr].unsqueeze(2).broadcast_to([nr, 16, 128]),
                    op=ALU.subtract), reads=[raw, far], writes=[raw])
                if mn is not None:
                    mk = K.sb(st, f"mk{ji}", [128, 128], F32)
                    K.dma(K.sp, mk[0:nr], I[mn][r0:r0 + nr], K.slot(f"t{ji % 2}"), writes=[mk])
                    K.op(K.dve, lambda: nc.vector.tensor_tensor(
                        out=raw[0:nr], in0=raw[0:nr], in1=mk[0:nr].unsqueeze(1).broadcast_to([nr, 16, 128]),
                        op=ALU.add), reads=[raw, mk], writes=[raw])
                K.dma(K.sp, dst[r0:r0 + nr], raw[0:nr], K.slot(f"o{ji % 2}"), reads=[raw], writes=[self.Tb])
            K.barrier()

    def nsa_layer(self, li, l):
        K, nc, I = self.K, self.nc, self.I
        p = f"l{l}_"
        W = I[p + "w_in"]
        for s in range(self.nseq):
            with ExitStack() as so:
                gates = K.sb(so, "gates", [128, NT, 48], F32)
                with ExitStack() as st:
                    hT = K.sb(st, "hT", [128, 16, S], BF16)
                    with ExitStack() as st1:
                        self.p1_phase(st1, li, s, hT)
                        K.barrier()
                    if self.dbg >= 2:
                        self.n2_phase(st, s, W, hT, gates)
                    K.barrier()
                with ExitStack() as st:
                    KcT = K.sb(st, "KcT", [128, 4, 128], BF16)
                    VcO = K.sb(st, "VcO", [128, 4, 162], BF16)
                    with ExitStack() as st3:
                        if self.dbg >= 3:
                            self.n3_phase(st3, s, p, KcT, VcO)
                        K.barrier()
                    if self.dbg >= 4:
                        self.n4_phase(st, s, KcT, VcO, gates)
                    K.barrier()

    def n2_phase(self, st, s, W, hT, gates):
        K, nc = self.K, self.nc
        wst = [K.sb(st, f"wst{i}", [128, 16, 128], BF16) for i in range(3)]
        wsl = [K.slot(f"wa{i}") for i in range(2)] + [K.slot("wb0")]
        stg = [K.sb(st, f"stg{i}", [128, S], BF16) for i in range(2)]
        gsl = [K.slot("ys0"), K.slot("ys1")]
        chunks = []
        for h in range(16):
            chunks.append((h * 128, "q", h))
        for wi, base in enumerate((2048, 3072, 4096, 2560)):
            for g in range(4):
                chunks.append((base + g * 128, "k", (wi, g)))
        for c in range(16):
            chunks.append((5168 + c * 128, "z", c))

        def loadw(i):
            self.load_w_cast(wst[i % 3], wst[i % 3][:], W, chunks[i][0], 128, wsl[i % 3])

        loadw(0)
        loadw(1)
        j = 0
        for i, (col0, kind, idx) in enumerate(chunks):
            if i + 2 < len(chunks):
                loadw(i + 2)
            w_ = wst[i % 3]
            g_ = stg[i % 2]
            for tb in range(4):
                ps_ = self.P[j % 4]
                j += 1
                sl_ = slice(tb * 512, (tb + 1) * 512)
                for k in range(16):
                    K.op(K.pe, lambda: nc.tensor.matmul(ps_[:], lhsT=w_[:, k, :], rhs=hT[:, k, sl_],
                                                        start=(k == 0), stop=(k == 15)),
                         reads=[w_, hT], writes=[ps_], inc=(k == 15))
                if kind == "q":
                    K.op(K.act, lambda: nc.scalar.mul(out=g_[:, sl_], in_=ps_[:], mul=float(DK) ** -0.5),
                         reads=[ps_], writes=[g_])
                elif kind == "z":
                    K.op(K.act, lambda: nc.scalar.activation(out=g_[:, sl_], in_=ps_[:], func=AF.Silu),
                         reads=[ps_], writes=[g_])
                else:
                    K.op(K.dve, lambda: nc.vector.tensor_copy(out=g_[:, sl_], in_=ps_[:]),
                         reads=[ps_], writes=[g_])
            if kind == "q":
                dst = self.QT[s].rearrange("t p h j -> p t h j")[:, :, idx, :]
                K.dma(K.sp, dst, g_[:].rearrange("p (t j) -> p t j", t=NT), gsl[i % 2], reads=[g_],
                      writes=[self.QTb[s]])
            elif kind == "z":
                dst = self.SZ[s].rearrange("t p h j -> p t h j")[:, :, idx, :]
                K.dma(K.sp, dst, g_[:].rearrange("p (t j) -> p t j", t=NT), gsl[i % 2], reads=[g_],
                      writes=[self.SZb[s]])
            else:
                wi, g = idx
                K.dma(K.sp, self.KT[s, wi, g], g_[:], gsl[i % 2], reads=[g_], writes=[self.KTb[s]])
        wmv = [K.sb(st, f"wmv{i}", [128, 16, 512], BF16) for i in range(2)]
        wg = K.sb(st, "wg", [128, 16, 48], BF16)
        vst = [K.sb(st, f"vst{i}", [128, 4, 130], BF16) for i in range(2)]
        msl = [K.slot("wb1"), K.slot("wo")]
        vsl = [K.slot("yb0"), K.slot("yb1")]
        for i in range(2):
            K.op(K.pool, lambda: nc.gpsimd.memset(vst[i][:, :, 128:129], 1.0), writes=[vst[i]])
            K.op(K.pool, lambda: nc.gpsimd.memset(vst[i][:, :, 129:130], 0.0), writes=[vst[i]])
        for vi, base in enumerate((3584, 4608)):
            self.load_w_cast(wmv[vi], wmv[vi][:], W, base, 512, msl[vi])
        self.load_w_cast(wg, wg[:], W, 5120, 48, K.slot("const"))
        for vi in range(2):
            for tt in range(NT):
                ps_ = self.P[j % 4]
                j += 1
                for k in range(16):
                    K.op(K.pe, lambda: nc.tensor.matmul(ps_[:], lhsT=hT[:, k, tt * 128:(tt + 1) * 128],
                                                        rhs=wmv[vi][:, k, :], start=(k == 0), stop=(k == 15)),
                         reads=[wmv[vi], hT], writes=[ps_], inc=(k == 15))
                v_ = vst[tt % 2]
                eng = K.act if tt % 2 == 0 else K.dve
                src = ps_[:].rearrange("p (g d) -> p g d", g=4)
                if tt % 2 == 0:
                    K.op(K.act, lambda: nc.scalar.copy(out=v_[:, :, 0:128], in_=src), reads=[ps_], writes=[v_])
                else:
                    K.op(K.dve, lambda: nc.vector.tensor_copy(out=v_[:, :, 0:128], in_=src), reads=[ps_],
                         writes=[v_])
                K.dma(K.sp, self.VA[s, vi, tt], v_[:].rearrange("p g d -> p (g d)"), vsl[tt % 2], reads=[v_],
                      writes=[self.VAb[s]])
        for tt in range(NT):
            ps_ = self.P[j % 4]
            j += 1
            for k in range(16):
                K.op(K.pe, lambda: nc.tensor.matmul(ps_[:, 0:48], lhsT=hT[:, k, tt * 128:(tt + 1) * 128],
                                                    rhs=wg[:, k, :], start=(k == 0), stop=(k == 15)),
                     reads=[wg, hT], writes=[ps_], inc=(k == 15))
            K.op(K.act, lambda: nc.scalar.activation(out=gates[:, tt, :], in_=ps_[:, 0:48], func=AF.Sigmoid),
                 reads=[ps_], writes=[gates])

    def n3_phase(self, st, s, p, KcT, VcO):
        K, nc, I = self.K, self.nc, self.I
        srcT = [K.sb(st, f"cT{i}", [128, 4, S], BF16) for i in range(2)]
        w1 = [K.sb(st, f"w1{i}", [128, 32, 128], BF16) for i in range(2)]
        w2 = [K.sb(st, f"w2{i}", [128, 128], BF16) for i in range(2)]
        HT = [K.sb(st, f"HT{i}", [128, 4, 127], BF16) for i in range(2)]
        posb = K.sb(st, "posb", [128, 32], BF16)
        cvec = K.sb(st, "cvec", [128, 2], F32)
        csl = K.slot("const")
        K.dma(K.sp, srcT[0][:], self.KT[s, 0].rearrange("g p t -> p g t"), K.slot("x0"), reads=[self.KTb[s]],
              writes=[srcT[0]])
        K.dma(K.sp, srcT[1][:], self.KT[s, 3].rearrange("g p t -> p g t"), K.slot("x1"), reads=[self.KTb[s]],
              writes=[srcT[1]])
        for i, nm in enumerate(("ck", "cv")):
            K.dma(K.pool, w1[i][:], I[p + nm + "_w1"].rearrange("(l d) j -> d l j", d=128), K.slot(f"wa{i}"),
                  writes=[w1[i]])
            K.dma(K.pool, w2[i][:], I[p + nm + "_w2"], K.slot(f"wb{i}"), writes=[w2[i]])
        K.dma(K.pool, posb[:], I[p + "posT"], csl, writes=[posb])
        K.op(K.pool, lambda: nc.gpsimd.memset(KcT[:], 0.0), writes=[KcT])
        K.op(K.pool, lambda: nc.gpsimd.memset(VcO[:], 0.0), writes=[VcO])
        for g in range(4):
            K.dma(K.pool, VcO[:, g, 128:162], I["c_ov"], csl, writes=[VcO])
        for i in range(2):
            pc = self.P[4 + i]
            for l_ in range(32):
                K.op(K.pe, lambda: nc.tensor.matmul(pc[:, 0:1], lhsT=w1[i][:, l_, :], rhs=posb[:, l_:l_ + 1],
                                                    start=(l_ == 0), stop=(l_ == 31)),
                     reads=[w1[i], posb], writes=[pc], inc=(l_ == 31))
            K.op(K.act, lambda: nc.scalar.copy(out=cvec[:, i:i + 1], in_=pc[:, 0:1]), reads=[pc], writes=[cvec])
            ph = self.P[i]
            for l_ in range(32):
                K.op(K.pe, lambda: nc.tensor.matmul(ph[:, 0:508].rearrange("p (g n) -> p g n", g=4),
                                                    lhsT=w1[i][:, l_, :],
                                                    rhs=srcT[i][:, :, l_:l_ + 16 * 126 + 1:16],
                                                    start=(l_ == 0), stop=(l_ == 31)),
                     reads=[w1[i], srcT[i]], writes=[ph], inc=(l_ == 31))
            K.op(K.act, lambda: nc.scalar.activation(out=HT[i][:], in_=ph[:, 0:508].rearrange("p (g n) -> p g n", g=4),
                                                     func=AF.Silu, bias=cvec[:, i:i + 1]),
                 reads=[ph, cvec], writes=[HT[i]])
        pk = self.P[2]
        K.op(K.pe, lambda: nc.tensor.matmul(pk[:, 0:508], lhsT=w2[0][:], rhs=HT[0][:].rearrange("p g n -> p (g n)"),
                                            start=True, stop=True),
             reads=[w2[0], HT[0]], writes=[pk])
        K.op(K.act, lambda: nc.scalar.copy(out=KcT[:, :, 0:127], in_=pk[:, 0:508].rearrange("p (g n) -> p g n", g=4)),
             reads=[pk], writes=[KcT])
        pv = self.P[3]
        for g in range(4):
            K.op(K.pe, lambda: nc.tensor.matmul(pv[0:127, g * 128:(g + 1) * 128], lhsT=HT[1][:, g, :], rhs=w2[1][:],
                                                start=True, stop=True),
                 reads=[w2[1], HT[1]], writes=[pv], inc=(g == 3))
        K.op(K.dve, lambda: nc.vector.tensor_copy(out=VcO[0:127, :, 0:128],
                                                  in_=pv[0:127, :].rearrange("p (g d) -> p g d", g=4)),
             reads=[pv], writes=[VcO])

    def n4_phase(self, st, s, KcT, VcO, gates):
        K, nc, I = self.K, self.nc, self.I
        ksT = K.sb(st, "ksT", [128, 4, S], BF16)
        kwT = K.sb(st, "kwT", [128, 4, S], BF16)
        vsA = K.sb(st, "vsA", [128, NT, 520], BF16)
        vwA = K.sb(st, "vwA", [128, NT, 520], BF16)
        Eb = K.sb(st, "Eb", [128, 64, 128], BF16)
        TA = K.sb(st, "TA", [128, 16, 128], F32)
        TB = K.sb(st, "TB", [128, 16, 128], F32)
        keep = K.sb(st, "keep", [128, NT, 32], F32)
        addm = K.sb(st, "addm", [128, NT, 32], F32)
        M0 = K.sb(st, "M0", [128, 128], BF16)
        csl = K.slot("const")
        K.dma(K.sp, ksT[:], self.KT[s, 1].rearrange("g p t -> p g t"), K.slot("x0"), reads=[self.KTb[s]], writes=[ksT])
        K.dma(K.sp, kwT[:], self.KT[s, 2].rearrange("g p t -> p g t"), K.slot("x1"), reads=[self.KTb[s]], writes=[kwT])
        K.dma(K.sp, vsA[:], self.VA[s, 0].rearrange("t p c -> p t c"), K.slot("t0"), reads=[self.VAb[s]], writes=[vsA])
        K.dma(K.sp, vwA[:], self.VA[s, 1].rearrange("t p c -> p t c"), K.slot("t1"), reads=[self.VAb[s]], writes=[vwA])
        for j in range(4):
            K.dma(K.pool, Eb[:, j * 16:(j + 1) * 16, :],
                  I["c_E"].rearrange("p (e k) -> p e k", k=128)[:, j * 16:(j + 1) * 16, :], K.slot("wa0"), writes=[Eb],
                  group=(j > 0))
        K.dma(K.pool, M0[:], I["m_W0"], K.slot("wa1"), writes=[M0])
        K.dma(K.sp, TA[:], self.TA, csl, reads=[self.Tb], writes=[TA])
        K.dma(K.sp, TB[:], self.TB, csl, reads=[self.Tb], writes=[TB])
        K.dma(K.sp, keep[:], I["c_keep"].rearrange("p (t s) -> p t s", s=32), csl, writes=[keep])
        K.dma(K.sp, addm[:], I["c_add"].rearrange("p (t s) -> p t s", s=32), csl, writes=[addm])
        qr = [K.sb(st, f"qr{i}", [128, 16, 128], BF16) for i in range(2)]
        zr = [K.sb(st, f"zr{i}", [128, 16, 128], BF16) for i in range(2)]
        cbr = [K.sb(st, f"cbr{i}", [128, 16, 128], F32) for i in range(2)]
        pTr = [K.sb(st, f"pT{i}", [128, 512], BF16) for i in range(3)]
        scr = [K.sb(st, f"sc{i}", [128, 512], F32) for i in range(2)]
        otile = K.sb(st, "otile", [128, D], F32)
        ob = K.sb(st, "ob", [128, D], BF16)
        oT = [K.sb(st, f"oT{i}", [128, 16, 128], BF16) for i in range(2)]
        impn = K.sb(st, "impn", [128, 4, 32], F32)
        t8 = K.sb(st, "t8", [128, 4, 8], F32)
        selm = K.sb(st, "selm", [128, 4, 32], F32)
        selT = K.sb(st, "selT", [128, 128], BF16)
        rcs = [K.sb(st, f"rc{i}", [128, 8], F32) for i in range(4)]
        qsl = [K.slot("ys0"), K.slot("ys1")]
        zsl = [K.slot("yb0"), K.slot("yb1")]
        bsl = [K.slot("wb0"), K.slot("wb1")]
        osl = [K.slot("o0"), K.slot("o1")]
        pmisc = self.pt[:, 6:8, :]
        Pm = [self.P[6], self.P[7]]
        po_sel = self.pt[:, 2:4, :]
        po_win = self.pt[:, 4:6, :]
        Psel = [self.P[2], self.P[3]]
        Pwin = [self.P[4], self.P[5]]

        def load(tt):
            K.dma(K.sp, qr[tt % 2][:], self.QT[s, tt], qsl[tt % 2], reads=[self.QTb[s]], writes=[qr[tt % 2]])
            K.dma(K.sp, zr[tt % 2][:], self.SZ[s, tt], zsl[tt % 2], reads=[self.SZb[s]], writes=[zr[tt % 2]])
            K.dma(K.sp, cbr[tt % 2][:], self.TC[120 - 8 * tt:248 - 8 * tt], bsl[tt % 2], reads=[self.Tb],
                  writes=[cbr[tt % 2]])

        cnt = {"u": 0, "rc": 0}

        def coef_and_acc(po_view, Pb, g, tt, branch, first):
            rc = rcs[cnt["rc"] % 4]
            cnt["rc"] += 1
            W_ = 162 if branch == 0 else 129
            sums = po_view[:, :, 128:128 + 2 * W_:W_]
            K.op(K.dve, lambda: nc.vector.tensor_scalar(out=rc[:, 0:4].rearrange("p (a b) -> p a b", a=2), in0=sums,
                                                        scalar1=1e-30, scalar2=None, op0=ALU.max),
                 reads=Pb, writes=[rc])
            K.op(K.dve, lambda: nc.vector.reciprocal(out=rc[:, 0:4], in_=rc[:, 0:4]), reads=[rc], writes=[rc])
            gcol = (4 * g) * 3 + branch
            K.op(K.dve, lambda: nc.vector.tensor_tensor(out=rc[:, 4:8], in0=rc[:, 0:4],
                                                        in1=gates[:, tt, gcol:gcol + 10:3], op=ALU.mult),
                 reads=[rc, gates], writes=[rc])
            for r in range(4):
                h = 4 * g + r
                src = po_view[:, r // 2, (r % 2) * W_:(r % 2) * W_ + 128]
                dst = otile[:, h * 128:(h + 1) * 128]
                if first:
                    K.op(K.act, lambda: nc.scalar.activation(out=dst, in_=src, func=AF.Copy, scale=rc[:, 4 + r:5 + r]),
                         reads=[Pb[r // 2], rc], writes=[otile])
                else:
                    K.op(K.dve, lambda: nc.vector.scalar_tensor_tensor(out=dst, in0=src, scalar=rc[:, 4 + r:5 + r],
                                                                       in1=dst, op0=ALU.mult, op1=ALU.add),
                         reads=[Pb[r // 2], rc, otile], writes=[otile])
            return rc

        if self.dbg < 5:
            return
        load(0)
        for tt in range(NT):
            if tt + 1 < NT:
                load(tt + 1)
            q_, z_, cb_ = qr[tt % 2], zr[tt % 2], cbr[tt % 2]
            for g in range(4):
                u = cnt["u"]
                cnt["u"] += 1
                ps_ = self.P[u % 2]
                K.op(K.pe, lambda: nc.tensor.matmul(ps_[:], lhsT=KcT[:, g, :], rhs=q_[:, 4 * g:4 * g + 4, :],
                                                    start=True, stop=True),
                     reads=[KcT, q_], writes=[ps_])
                sc_ = scr[u % 2]
                K.op(K.dve, lambda: nc.vector.tensor_tensor(out=sc_[:], in0=ps_[:],
                                                            in1=cb_[:, 4 * g:4 * g + 4, :].rearrange("p h j -> p (h j)"),
                                                            op=ALU.add),
                     reads=[ps_, cb_], writes=[sc_])
                pT_ = pTr[u % 3]
                K.op(K.act, lambda: nc.scalar.activation(out=pT_[:], in_=sc_[:], func=AF.Exp),
                     reads=[sc_], writes=[pT_])
                if self.dbg < 5.2:
                    continue
                for r in range(4):
                    K.op(K.pe, lambda: nc.tensor.matmul(pmisc[:, r // 2, (r % 2) * 162:(r % 2) * 162 + 162],
                                                        lhsT=pT_[:, r * 128:(r + 1) * 128], rhs=VcO[:, g, :],
                                                        start=True, stop=True),
                         reads=[pT_, VcO], writes=[Pm[r // 2]], inc=(r == 3))
                if self.dbg < 5.3:
                    continue
                rc = coef_and_acc(pmisc, Pm, g, tt, 0, True)
                if self.dbg < 5.4:
                    continue
                for r in range(4):
                    src = pmisc[:, r // 2, (r % 2) * 162 + 130:(r % 2) * 162 + 162]
                    if r == 0:
                        K.op(K.dve, lambda: nc.vector.tensor_scalar(out=impn[:, g, :], in0=src, scalar1=rc[:, 0:1],
                                                                    scalar2=None, op0=ALU.mult),
                             reads=[Pm[0], rc], writes=[impn])
                    else:
                        K.op(K.dve, lambda: nc.vector.scalar_tensor_tensor(
                            out=impn[:, g, :], in0=src, scalar=rc[:, r:r + 1], in1=impn[:, g, :],
                            op0=ALU.mult, op1=ALU.add),
                            reads=[Pm[r // 2], rc, impn], writes=[impn])
            if self.dbg < 6:
                continue
            K.op(K.dve, lambda: nc.vector.tensor_tensor(out=impn[:], in0=impn[:],
                                                        in1=keep[:, tt, :].unsqueeze(1).broadcast_to([128, 4, 32]),
                                                        op=ALU.mult),
                 reads=[impn, keep], writes=[impn])
            K.op(K.dve, lambda: nc.vector.tensor_tensor(out=impn[:], in0=impn[:],
                                                        in1=addm[:, tt, :].unsqueeze(1).broadcast_to([128, 4, 32]),
                                                        op=ALU.add),
                 reads=[impn, addm], writes=[impn])
            for g in range(4):
                K.op(K.dve, lambda: nc.vector.max(out=t8[:, g, :], in_=impn[:, g, :]), reads=[impn], writes=[t8])
            for g in range(4):
                K.op(K.dve, lambda: nc.vector.tensor_scalar(out=selm[:, g, :], in0=impn[:, g, :],
                                                            scalar1=t8[:, g, 7:8], scalar2=-1.0,
                                                            op0=ALU.is_ge, op1=ALU.add),
                     reads=[impn, t8], writes=[selm])
            K.op(K.pe, lambda: nc.tensor.transpose(out=self.P[6][:, 0:128], in_=selm[:].rearrange("p g s -> p (g s)"),
                                                   identity=self.ident_f[:]),
                 reads=[selm, self.ident_f], writes=[self.P[6]])
            K.op(K.act, lambda: nc.scalar.copy(out=selT[:], in_=self.P[6][:, 0:128]), reads=[self.P[6]], writes=[selT])
            if self.dbg < 7:
                continue
            for g in range(4):
                units = [("s", kc) for kc in range(tt + 1)] + [("w", kc) for kc in range(max(0, tt - 4), tt + 1)]
                first_seen = {"s": True, "w": True}
                last_kc = {"s": tt, "w": tt}

                def qk(un):
                    br, kc = un
                    u = cnt["u"]
                    cnt["u"] += 1
                    ps_ = self.P[u % 2]
                    kT = ksT if br == "s" else kwT
                    extra = (br == "s" and kc < tt) or (br == "w" and kc == tt - 4)
                    K.op(K.pe, lambda: nc.tensor.matmul(ps_[:], lhsT=kT[:, g, kc * 128:(kc + 1) * 128],
                                                        rhs=q_[:, 4 * g:4 * g + 4, :], start=True, stop=not extra),
                         reads=[kT, q_], writes=[ps_], inc=not extra)
                    if br == "s" and kc < tt:
                        K.op(K.pe, lambda: nc.tensor.matmul(ps_[:], lhsT=Eb[:, g * 16 + kc, :],
                                                            rhs=selT[:].unsqueeze(1).broadcast_to([128, 4, 128]),
                                                            start=False, stop=True),
                             reads=[Eb, selT], writes=[ps_])
                    elif br == "w" and kc == tt - 4:
                        K.op(K.pe, lambda: nc.tensor.matmul(ps_[:], lhsT=self.ident_b[:],
                                                            rhs=M0[:].unsqueeze(1).broadcast_to([128, 4, 128]),
                                                            start=False, stop=True),
                             reads=[self.ident_b, M0], writes=[ps_])
                    return u

                def rest(un, u):
                    br, kc = un
                    ps_ = self.P[u % 2]
                    pT_ = pTr[u % 3]
                    tab = TB if kc == tt else (TA if kc == tt - 1 else None)
                    if tab is not None:
                        sc_ = scr[u % 2]
                        K.op(K.dve, lambda: nc.vector.tensor_tensor(
                            out=sc_[:], in0=ps_[:], in1=tab[:, 4 * g:4 * g + 4, :].rearrange("p h j -> p (h j)"),
                            op=ALU.add), reads=[ps_, tab], writes=[sc_])
                        K.op(K.act, lambda: nc.scalar.activation(out=pT_[:], in_=sc_[:], func=AF.Exp),
                             reads=[sc_], writes=[pT_])
                    else:
                        K.op(K.act, lambda: nc.scalar.activation(out=pT_[:], in_=ps_[:], func=AF.Exp),
                             reads=[ps_], writes=[pT_])
                    po_view, Pb, vA = (po_sel, Psel, vsA) if br == "s" else (po_win, Pwin, vwA)
                    first = first_seen[br]
                    first_seen[br] = False
                    last = kc == last_kc[br]
                    for r in range(4):
                        K.op(K.pe, lambda: nc.tensor.matmul(
                            po_view[:, r // 2, (r % 2) * 129:(r % 2) * 129 + 129],
                            lhsT=pT_[:, r * 128:(r + 1) * 128], rhs=vA[:, kc, g * 130:g * 130 + 129],
                            start=(first and r % 2 == 0), stop=last, skip_group_check=True),
                            reads=[pT_, vA], writes=[Pb[r // 2]], inc=(r == 3))

                uprev = qk(units[0])
                for n, un in enumerate(units):
                    unext = qk(units[n + 1]) if n + 1 < len(units) else None
                    rest(un, uprev)
                    uprev = unext
                coef_and_acc(po_sel, Psel, g, tt, 1, False)
                coef_and_acc(po_win, Pwin, g, tt, 2, False)
            if self.dbg < 8:
                continue
            K.op(K.act, lambda: nc.scalar.copy(out=ob[:], in_=otile[:]), reads=[otile], writes=[ob])
            pv = pmisc.bitcast(BF16)
            for c in range(16):
                o = pv[:, c // 8, (c % 8) * 128:(c % 8 + 1) * 128]
                K.op(K.pe, lambda: nc.tensor.transpose(out=o, in_=ob[:, c * 128:(c + 1) * 128], identity=self.ident_b[:]),
                     reads=[ob, self.ident_b], writes=[Pm[c // 8]], inc=(c % 8 == 7))
            o_ = oT[tt % 2]
            for half in range(2):
                K.op(K.dve, lambda: nc.vector.tensor_tensor(
                    out=o_[:, half * 8:(half + 1) * 8, :], in0=pv[:, half, :].rearrange("p (c t) -> p c t", c=8),
                    in1=z_[:, half * 8:(half + 1) * 8, :], op=ALU.mult),
                    reads=[Pm[half], z_], writes=[o_])
            K.dma(K.sp, self.WT[s, tt], o_[:], osl[tt % 2], reads=[o_], writes=[self.WTb[s]])


def host_consts():
    c = {}
    c["c_ident"] = np.eye(128, dtype=np.float32)
    return c


def layer_inputs(inputs, layers):
    m = {}
    for l in layers:
        p = f"l{l}_"
        m[p + "norm"] = np.ascontiguousarray(inputs[p + "norm"].reshape(1, D))
        m[p + "w_out"] = inputs[p + "w_out"]
        m[p + "w_in"] = inputs[p + "w_in"]
        if l % 2 == 0:
            dw = inputs[p + "dw_w"].reshape(31, 16, 128).transpose(2, 1, 0)
            m[p + "dw_wT"] = np.ascontiguousarray(dw.reshape(128, 16 * 31))
            for nm in ("dw_b", "ln_g", "ln_b"):
                m[p + nm] = np.ascontiguousarray(inputs[p + nm].reshape(16, 128).T)
        else:
            m[p + "posT"] = np.ascontiguousarray(inputs[p + "cmp_pos"].T)
            for nm in ("ck_w1", "ck_w2", "cv_w1", "cv_w2"):
                m[p + nm] = inputs[p + nm]
    m["final_norm"] = np.ascontiguousarray(inputs["final_norm"].reshape(1, D))
    return m


_PROG_CACHE = {}


def run(inputs, layers=(0, 1, 2, 3), final_norm=True, ncores=NCORES, nseq=NSEQ, x_override=None, trace=False, dbg=99):
    key = (tuple(layers), final_norm, nseq, dbg)
    if key not in _PROG_CACHE:
        _PROG_CACHE[key] = Prog(layers, final_norm, nseq, dbg)
    prog = _PROG_CACHE[key]
    shared = dict(host_consts())
    shared.update(layer_inputs(inputs, layers))
    if any(l % 2 == 1 for l in layers):
        shared.update(nsa_host_tables(inputs["rel_bias"]))
    x = inputs["x"] if x_override is None else x_override
    in_maps = []
    for c in range(ncores):
        m = dict(shared)
        m["x"] = np.ascontiguousarray(x[c * nseq:(c + 1) * nseq])
        in_maps.append(m)
    res = run_bass_kernel_spmd(prog.nc, in_maps, core_ids=list(range(ncores)), **({'trace': True} if trace else {}))
    if trace:
        print('exec_time_ns', res.exec_time_ns)
    return np.concatenate([r["y"] for r in res.results], axis=0)


def _t5_bucket_np(dist):
    dist = np.maximum(dist, 0)
    d = np.maximum(dist, 16).astype(np.float32)
    large = 16 + (np.log(d / np.float32(16)) / np.float32(np.log(8.0)) * np.float32(16)).astype(np.int32)
    large = np.minimum(large, 31)
    return np.where(dist < 16, dist, large)


def nsa_host_tables(rel_bias):
    m = {}
    k = np.arange(128)[:, None]
    t = np.arange(128)[None, :]
    dA = t - k + 128
    dB = t - k
    mp = np.arange(248)[:, None]
    dC = t - 16 * (mp - 120) - 31
    m["rb_A"] = np.ascontiguousarray(rel_bias[_t5_bucket_np(dA)].transpose(0, 2, 1))
    m["rb_B"] = np.ascontiguousarray(rel_bias[_t5_bucket_np(dB)].transpose(0, 2, 1))
    m["rb_C"] = np.ascontiguousarray(rel_bias[_t5_bucket_np(dC)].transpose(0, 2, 1))
    m["rb_far"] = np.ascontiguousarray(rel_bias[31:32, :])
    m["m_B"] = np.where(dB >= 0, 0.0, NEGM).astype(np.float32)
    m["m_C"] = np.where(dC >= 0, 0.0, NEGM).astype(np.float32)
    m["m_W0"] = np.where(k > t, 0.0, NEGM).astype(np.float32)
    E = np.zeros((4, 32, 4, 16, 128), np.float32)
    for g in range(4):
        for kc in range(16):
            E[g, 2 * kc, g, kc, 0:64] = -NEGM
            E[g, 2 * kc + 1, g, kc, 64:128] = -NEGM
    m["c_E"] = np.ascontiguousarray(E.reshape(128, 64 * 128))
    tl = np.arange(128)[:, None, None]
    ti = np.arange(16)[None, :, None]
    sb = np.arange(32)[None, None, :]
    tabs = ti * 128 + tl
    cur = tabs // 64
    forced = (sb == 0) | (sb == cur) | (sb == cur - 1)
    causal = sb * 64 <= tabs
    keep = (causal & ~forced).astype(np.float32)
    add = np.where(causal, np.where(forced, 1e6, 0.0), -1e30).astype(np.float32)
    m["c_keep"] = np.ascontiguousarray(keep.reshape(128, 512))
    m["c_add"] = np.ascontiguousarray(add.reshape(128, 512))
    ov = np.zeros((128, 34), np.float32)
    n = np.arange(127)[:, None]
    s0 = np.arange(32)[None, :] * 64
    j0 = n * 16
    ov[:127, 0] = 1.0
    ov[:127, 2:34] = ((j0 < s0 + 64) & (j0 + 32 > s0)).astype(np.float32)
    m["c_ov"] = ov
    return m


def kernel(**inputs):
    inputs = {k: np.asarray(v) for k, v in inputs.items()}
    return run(inputs)
```

```python
import numpy as np
from contextlib import ExitStack
import concourse.bass as bass
import concourse.mybir as mybir
from concourse.bass_utils import run_bass_kernel_spmd

F32 = mybir.dt.float32
BF16 = mybir.dt.bfloat16
AF = mybir.ActivationFunctionType
ALU = mybir.AluOpType
AX = mybir.AxisListType

D = 2048
S = 2048
NT = S // 128
NSEQ = 2
NCORES = 8
NH = 16
NG = 4
DK = 128
EPS = 1e-6
NEGM = -30000.0
CONF_W = 3 * D
NSA_W = 2048 + 6 * 512 + 48 + 2048


class Owner:
    def __init__(self, K, name):
        self.sem = K.stack.enter_context(K.nc.semaphore(name))
        self.count = 0
        self.name = name


class Eng(Owner):
    def __init__(self, K, name, h):
        super().__init__(K, "e_" + name)
        self.h = h
        self.seen = {}
        self.is_pe = name == "pe"
        self.last_inc = True


class Buf:
    def __init__(self, ap, name=""):
        self.ap = ap
        self.name = name
        self.writers = {}
        self.readers = {}
        self.prev = {}
        self.excl = False

    def __getitem__(self, idx):
        return self.ap[idx]


def _merge(d, own, val):
    if d.get(own, 0) < val:
        d[own] = val


class Kern:
    def __init__(self, nc):
        self.nc = nc
        self.stack = ExitStack()
        self.pe = Eng(self, "pe", nc.tensor)
        self.act = Eng(self, "act", nc.scalar)
        self.dve = Eng(self, "dve", nc.vector)
        self.pool = Eng(self, "pool", nc.gpsimd)
        self.sp = Eng(self, "sp", nc.sync)
        self.engs = [self.pe, self.act, self.dve, self.pool, self.sp]
        self.slots = {}
        self.n_dram = 0

    def slot(self, name):
        if name not in self.slots:
            self.slots[name] = Owner(self, "d_" + name)
        return self.slots[name]

    def sb(self, st, name, shape, dt):
        self.n_dram += 1
        nm = f"{name}_{self.n_dram}"
        return Buf(st.enter_context(self.nc.sbuf_tensor(nm, list(shape), dt)), nm)

    def dram(self, name, shape, dt, kind=None):
        if kind is None:
            t = self.nc.dram_tensor(name, list(shape), dt)
        else:
            t = self.nc.dram_tensor(name, list(shape), dt, kind=kind)
        return t.ap()

    def _deps(self, eng, reads, writes):
        deps = {}

        def add(d, raw):
            for own, val in d.items():
                if own is eng and (eng.is_pe or not raw):
                    continue
                _merge(deps, own, val)

        for b in reads:
            add(b.writers, True)
            if b.excl:
                add(b.readers, False)
        for b in writes:
            add(b.writers, False)
            add(b.readers, False)
            add(b.prev, False)
        return [(o, v) for o, v in deps.items() if eng.seen.get(o, 0) < v]

    def _emit_waits(self, eng, need, fn):
        for o, v in need[1:]:
            eng.h.wait_ge(o.sem, v)
        ins = fn()
        if need:
            ins._wait_ge(need[0][0].sem, need[0][1])
        for o, v in need:
            eng.seen[o] = v
        return ins

    def _update(self, ev, reads, writes):
        own, val = ev
        for b in reads:
            _merge(b.readers, own, val)
        for b in writes:
            if b.readers:
                prev = dict(b.readers)
                for o, v in b.writers.items():
                    _merge(prev, o, v)
                b.prev = prev
                b.readers = {}
                b.writers = {}
            _merge(b.writers, own, val)

    def op(self, eng, fn, reads=(), writes=(), inc=True):
        need = self._deps(eng, reads, writes)
        ins = self._emit_waits(eng, need, fn)
        if inc:
            eng.count += 1
            ins.then_inc(eng.sem, 1)
            ev = (eng, eng.count)
            eng.last_inc = True
        else:
            assert eng.is_pe
            ev = (eng, eng.count + 1)
            eng.last_inc = False
        self._update(ev, reads, writes)
        return ins

    def dma(self, q, out, in_, slot, reads=(), writes=(), group=False, **kw):
        need = self._deps(q, reads, writes)
        if not group and slot.count > 0 and q.seen.get(slot, 0) < slot.count:
            need = [(o, v) for o, v in need if o is not slot] + [(slot, slot.count)]
        ins = self._emit_waits(q, need, lambda: q.h.dma_start(out=out, in_=in_, **kw))
        slot.count += 16
        ins.then_inc(slot.sem, 16)
        self._update((slot, slot.count), reads, writes)
        return ins

    def barrier(self):
        assert self.pe.last_inc
        owners = list(self.engs) + list(self.slots.values())
        for e in self.engs:
            for o in owners:
                if o is e or o.count == 0:
                    continue
                if e.seen.get(o, 0) < o.count:
                    e.h.wait_ge(o.sem, o.count)
                    e.seen[o] = o.count


class Prog:
    def __init__(self, layers=(0, 1, 2, 3), final_norm=True, nseq=NSEQ, dbg=99):
        self.dbg = dbg
        self.layers = list(layers)
        self.final_norm = final_norm
        self.nseq = nseq
        nc = bass.Bass("TRN2", target_bir_lowering=False)
        self.nc = nc
        self.K = Kern(nc)
        self.build()

    def declare_inputs(self):
        K = self.K
        ns = self.nseq
        I = {}

        def inp(name, shape, dt=F32):
            I[name] = K.dram(name, shape, dt, kind="ExternalInput")

        inp("x", [ns, S, D])
        inp("c_ident", [128, 128])
        for l in self.layers:
            p = f"l{l}_"
            inp(p + "norm", [1, D])
            inp(p + "w_out", [D, D])
            if l % 2 == 0:
                inp(p + "w_in", [D, CONF_W])
                inp(p + "dw_wT", [128, 16 * 31])
                inp(p + "dw_b", [128, 16])
                inp(p + "ln_g", [128, 16])
                inp(p + "ln_b", [128, 16])
            else:
                inp(p + "w_in", [D, NSA_W])
                inp(p + "posT", [128, 32])
                inp(p + "ck_w1", [4096, 128])
                inp(p + "ck_w2", [128, 128])
                inp(p + "cv_w1", [4096, 128])
                inp(p + "cv_w2", [128, 128])
        if any(l % 2 == 1 for l in self.layers):
            inp("rb_A", [128, 16, 128])
            inp("rb_B", [128, 16, 128])
            inp("rb_C", [248, 16, 128])
            inp("rb_far", [1, 16])
            inp("m_B", [128, 128])
            inp("m_C", [248, 128])
            inp("m_W0", [128, 128])
            inp("c_E", [128, 64 * 128])
            inp("c_keep", [128, 16 * 32])
            inp("c_add", [128, 16 * 32])
            inp("c_ov", [128, 34])
        inp("final_norm", [1, D])
        self.I = I
        self.out = K.dram("y", [ns, S, D], F32, kind="ExternalOutput")

    def build(self):
        K = self.K
        nc = self.nc
        ns = self.nseq
        self.declare_inputs()
        I = self.I
        with K.stack:
            st = K.stack
            self.ident_f = K.sb(st, "ident_f", [128, 128], F32)
            self.ident_b = K.sb(st, "ident_b", [128, 128], BF16)
            self.ones_f = K.sb(st, "ones_f", [128, 128], F32)
            self.gbc = K.sb(st, "gbc", [128, D], F32)
            self.gfin = K.sb(st, "gfin", [128, D], F32)
            self.small = K.sb(st, "small", [128, 64], F32)
            pt = st.enter_context(nc.psum_tensor("psum", [128, 8, 512], F32))
            self.pt = pt
            self.P = [Buf(pt[:, b, :], f"ps{b}") for b in range(8)]
            for b_ in self.P:
                b_.excl = True
            sl = K.slot("const")
            K.dma(K.sp, self.ident_f[:], I["c_ident"], sl, writes=[self.ident_f])
            K.op(K.act, lambda: nc.scalar.copy(out=self.ident_b[:], in_=self.ident_f[:]),
                 reads=[self.ident_f], writes=[self.ident_b])
            K.op(K.pool, lambda: nc.gpsimd.memset(self.ones_f[:], 1.0), writes=[self.ones_f])
            K.dma(K.sp, self.gfin[:], bass.AP(I["final_norm"].tensor, 0, [[0, 128], [1, D]]), sl,
                  writes=[self.gfin])
            K.op(K.pool, lambda: nc.gpsimd.memset(self.small[:, 0:1], EPS), writes=[self.small])
            nl = len(self.layers)
            self.X = [I["x"]]
            for i in range(nl - 1):
                self.X.append(K.dram(f"xs{i}", [ns, S, D], F32))
            self.X.append(self.out)
            self.Xb = [[[Buf(None, f"X{i}_{s}_{t}") for t in range(NT)] for s in range(ns)]
                       for i in range(nl + 1)]
            self.WT = K.dram("wt_s", [ns, NT, 128, 16, 128], BF16)
            self.WTb = [Buf(None, f"WT{s}") for s in range(ns)]
            self.Ys = K.dram("y_s", [ns, 16, 128, S], F32)
            self.Yb = [[Buf(None, f"Y{s}_{c}") for c in range(16)] for s in range(ns)]
            if any(l % 2 == 1 for l in self.layers):
                self.QT = K.dram("qt_s", [ns, NT, 128, 16, 128], BF16)
                self.SZ = K.dram("sz_s", [ns, NT, 128, 16, 128], BF16)
                self.KT = K.dram("kt_s", [ns, 4, 4, 128, S], BF16)
                self.VA = K.dram("va_s", [ns, 2, NT, 128, 520], BF16)
                self.TA = K.dram("ta_s", [128, 16, 128], F32)
                self.TB = K.dram("tb_s", [128, 16, 128], F32)
                self.TC = K.dram("tc_s", [248, 16, 128], F32)
                self.QTb = [Buf(None, f"QT{s}") for s in range(ns)]
                self.SZb = [Buf(None, f"SZ{s}") for s in range(ns)]
                self.KTb = [Buf(None, f"KT{s}") for s in range(ns)]
                self.VAb = [Buf(None, f"VA{s}") for s in range(ns)]
                self.Tb = Buf(None, "Ttab")
                self.nsa_setup()
            K.barrier()
            for li, l in enumerate(self.layers):
                last = (li == nl - 1) and self.final_norm
                sl = K.slot("const")
                K.dma(K.sp, self.gbc[:], bass.AP(I[f"l{l}_norm"].tensor, 0, [[0, 128], [1, D]]), sl,
                      writes=[self.gbc])
                if l % 2 == 0:
                    self.conformer_layer(li, l)
                else:
                    self.nsa_layer(li, l)
                self.g2_phase(li, l, last)
            K.barrier()

    def p1_phase(self, st, li, s, hT):
        K, nc = self.K, self.nc
        xr = [K.sb(st, f"p1x{i}", [128, D], F32) for i in range(2)]
        hb = [K.sb(st, f"p1h{i}", [128, D], BF16) for i in range(2)]
        junk = K.sb(st, "p1junk", [128, D], BF16)
        stat = [K.sb(st, f"p1s{i}", [128, 4], F32) for i in range(2)]
        X = self.X[li]
        xsl = [K.slot("x0"), K.slot("x1")]

        def load(tt):
            K.dma(K.sp, xr[tt % 2][:], X[s, tt * 128:(tt + 1) * 128, :], xsl[tt % 2],
                  reads=[self.Xb[li][s][tt]], writes=[xr[tt % 2]])

        load(0)
        for tt in range(NT):
            if tt + 1 < NT:
                load(tt + 1)
            x = xr[tt % 2]
            h = hb[tt % 2]
            sm = stat[tt % 2]
            K.op(K.act, lambda: nc.scalar.activation(out=junk[:], in_=x[:], func=AF.Square,
                                                     accum_out=sm[:, 0:1]),
                 reads=[x], writes=[junk, sm])
            K.op(K.act, lambda: nc.scalar.activation(out=sm[:, 1:2], in_=sm[:, 0:1], func=AF.Sqrt,
                                                     scale=1.0 / D, bias=self.eps_ap()),
                 reads=[sm, self.small], writes=[sm])
            K.op(K.dve, lambda: nc.vector.reciprocal(out=sm[:, 2:3], in_=sm[:, 1:2]),
                 reads=[sm], writes=[sm])
            K.op(K.dve, lambda: nc.vector.scalar_tensor_tensor(
                out=h[:], in0=x[:], scalar=sm[:, 2:3], in1=self.gbc[:], op0=ALU.mult, op1=ALU.mult),
                reads=[x, sm, self.gbc], writes=[h])
            pb = (tt % 2) * 2
            pv = self.pt[:, pb:pb + 2, :].bitcast(BF16)
            for c in range(16):
                bank = self.P[pb + c // 8]
                o = pv[:, c // 8, (c % 8) * 128:(c % 8 + 1) * 128]
                K.op(K.pe, lambda: nc.tensor.transpose(out=o, in_=h[:, c * 128:(c + 1) * 128],
                                                       identity=self.ident_b[:]),
                     reads=[h, self.ident_b], writes=[bank], inc=(c % 8 == 7))
            for half in range(2):
                src = pv[:, half, :].rearrange("p (c t) -> p c t", c=8)
                dst = hT[:, half * 8:(half + 1) * 8, tt * 128:(tt + 1) * 128]
                eng = K.act if half == 0 else K.dve
                if half == 0:
                    K.op(K.act, lambda: nc.scalar.copy(out=dst, in_=src), reads=[self.P[pb]], writes=[hT])
                else:
                    K.op(K.dve, lambda: nc.vector.tensor_copy(out=dst, in_=src), reads=[self.P[pb + 1]],
                         writes=[hT])

    def eps_ap(self):
        return self.small[:, 0:1]

    def load_w_cast(self, dst_buf, dst_ap, w_ap, col0, ncols, slot, group=False):
        K = self.K
        src = w_ap.rearrange("(k p) n -> p k n", p=128)[:, :, col0:col0 + ncols]
        K.dma(K.pool, dst_ap, src, slot, writes=[dst_buf], group=group)

    def g2_phase(self, li, l, last):
        K, nc = self.K, self.nc
        I = self.I
        with ExitStack() as st:
            wo = K.sb(st, "g2w", [128, 16, D], BF16)
            wsl = K.slot("wo")
            for j in range(4):
                self.load_w_cast(wo, wo[:, :, j * 512:(j + 1) * 512], I[f"l{l}_w_out"], j * 512, 512, wsl,
                                 group=(j > 0))
            xr = [K.sb(st, f"g2x{i}", [128, D], F32) for i in range(2)]
            xo = [K.sb(st, f"g2o{i}", [128, D], F32) for i in range(2)]
            wt = [K.sb(st, f"g2t{i}", [128, 16, 128], BF16) for i in range(2)]
            junk = K.sb(st, "g2junk", [128, D], BF16)
            stat = [K.sb(st, f"g2s{i}", [128, 4], F32) for i in range(2)]
            xsl = [K.slot("x0"), K.slot("x1")]
            tsl = [K.slot("t0"), K.slot("t1")]
            osl = [K.slot("o0"), K.slot("o1")]
            Xi, Xo = self.X[li], self.X[li + 1]
            steps = [(s, tt) for s in range(self.nseq) for tt in range(NT)]

            def load(i):
                s, tt = steps[i]
                K.dma(K.sp, xr[i % 2][:], Xi[s, tt * 128:(tt + 1) * 128, :], xsl[i % 2],
                      reads=[self.Xb[li][s][tt]], writes=[xr[i % 2]])
                K.dma(K.sp, wt[i % 2][:], self.WT[s, tt], tsl[i % 2], reads=[self.WTb[s]], writes=[wt[i % 2]])

            load(0)
            for i, (s, tt) in enumerate(steps):
                if i + 1 < len(steps):
                    load(i + 1)
                x, o, w = xr[i % 2], xo[i % 2], wt[i % 2]
                for db in range(4):
                    bank = self.P[(i * 4 + db) % 2]
                    for c in range(16):
                        K.op(K.pe, lambda: nc.tensor.matmul(bank[:], lhsT=w[:, c, :],
                                                            rhs=wo[:, c, db * 512:(db + 1) * 512],
                                                            start=(c == 0), stop=(c == 15)),
                             reads=[w, wo], writes=[bank], inc=(c == 15))
                    K.op(K.dve, lambda: nc.vector.tensor_tensor(out=o[:, db * 512:(db + 1) * 512], in0=bank[:],
                                                                in1=x[:, db * 512:(db + 1) * 512], op=ALU.add),
                         reads=[bank, x], writes=[o])
                if last:
                    sm = stat[i % 2]
                    K.op(K.act, lambda: nc.scalar.activation(out=junk[:], in_=o[:], func=AF.Square,
                                                             accum_out=sm[:, 0:1]),
                         reads=[o], writes=[junk, sm])
                    K.op(K.act, lambda: nc.scalar.activation(out=sm[:, 1:2], in_=sm[:, 0:1], func=AF.Sqrt,
                                                             scale=1.0 / D, bias=self.eps_ap()),
                         reads=[sm, self.small], writes=[sm])
                    K.op(K.dve, lambda: nc.vector.reciprocal(out=sm[:, 2:3], in_=sm[:, 1:2]),
                         reads=[sm], writes=[sm])
                    K.op(K.dve, lambda: nc.vector.scalar_tensor_tensor(
                        out=o[:], in0=o[:], scalar=sm[:, 2:3], in1=self.gfin[:], op0=ALU.mult, op1=ALU.mult),
                        reads=[o, sm, self.gfin], writes=[o])
                K.dma(K.sp, Xo[s, tt * 128:(tt + 1) * 128, :], o[:], osl[i % 2], reads=[o],
                      writes=[self.Xb[li + 1][s][tt]])
            K.barrier()

    def conformer_layer(self, li, l):
        K, nc = self.K, self.nc
        I = self.I
        p = f"l{l}_"
        W = I[p + "w_in"]
        for s in range(self.nseq):
            with ExitStack() as st:
                hT = K.sb(st, "hT", [128, 16, S], BF16)
                with ExitStack() as st1:
                    self.p1_phase(st1, li, s, hT)
                    K.barrier()
                dwT = K.sb(st, "dwT", [128, 16, 31], F32)
                dwb = K.sb(st, "dwb", [128, 16], F32)
                lng = K.sb(st, "lng", [128, 16], F32)
                lnb = K.sb(st, "lnb", [128, 16], F32)
                csl = K.slot("const")
                K.dma(K.sp, dwT[:], I[p + "dw_wT"].rearrange("p (c k) -> p c k", c=16), csl, writes=[dwT])
                K.dma(K.sp, dwb[:], I[p + "dw_b"], csl, writes=[dwb])
                K.dma(K.sp, lng[:], I[p + "ln_g"], csl, writes=[lng])
                K.dma(K.sp, lnb[:], I[p + "ln_b"], csl, writes=[lnb])
                S1 = K.sb(st, "S1", [128, S], F32)
                S2 = K.sb(st, "S2", [128, S], F32)
                wa = [K.sb(st, f"wa{i}", [128, 16, 128], BF16) for i in range(2)]
                wb = [K.sb(st, f"wb{i}", [128, 16, 128], BF16) for i in range(2)]
                wasl = [K.slot("wa0"), K.slot("wa1")]
                wbsl = [K.slot("wb0"), K.slot("wb1")]
                with ExitStack() as st2:
                    v = [K.sb(st2, f"v{i}", [128, 30 + S], BF16) for i in range(2)]
                    dg = [K.sb(st2, f"dg{i}", [128, 31, 128], BF16) for i in range(2)]
                    sg = [K.sb(st2, f"sg{i}", [128, 512], F32) for i in range(2)]
                    ys = [K.sb(st2, f"ys{i}", [128, 512], F32) for i in range(2)]
                    yq = [K.sb(st2, f"yq{i}", [128, 512], F32) for i in range(2)]
                    ysl = [K.slot("ys0"), K.slot("ys1")]
                    for i in range(2):
                        K.op(K.pool, lambda: nc.gpsimd.memset(v[i][:, 0:30], 0.0), writes=[v[i]])

                    def loadw(c):
                        self.load_w_cast(wa[c % 2], wa[c % 2][:], W, c * 128, 128, wasl[c % 2])
                        self.load_w_cast(wb[c % 2], wb[c % 2][:], W, D + c * 128, 128, wbsl[c % 2])

                    def gemm_glu(c, tb):
                        j = c * 4 + tb
                        pa, pb = self.P[(j % 2) * 3], self.P[(j % 2) * 3 + 1]
                        for (ps_, w_) in ((pa, wa[c % 2]), (pb, wb[c % 2])):
                            for k in range(16):
                                K.op(K.pe, lambda: nc.tensor.matmul(ps_[:], lhsT=w_[:, k, :],
                                                                    rhs=hT[:, k, tb * 512:(tb + 1) * 512],
                                                                    start=(k == 0), stop=(k == 15)),
                                     reads=[w_, hT], writes=[ps_], inc=(k == 15))
                        g_ = sg[j % 2]
                        K.op(K.act, lambda: nc.scalar.activation(out=g_[:], in_=pb[:], func=AF.Sigmoid),
                             reads=[pb], writes=[g_])
                        vv = v[c % 2]
                        K.op(K.dve, lambda: nc.vector.tensor_tensor(
                            out=vv[:, 30 + tb * 512:30 + (tb + 1) * 512], in0=pa[:], in1=g_[:], op=ALU.mult),
                            reads=[pa, g_], writes=[vv])

                    def conv(c, tb):
                        j = c * 4 + tb
                        py = self.P[(j % 2) * 3 + 2]
                        vv, dd = v[c % 2], dg[c % 2]
                        for k in range(31):
                            K.op(K.pe, lambda: nc.tensor.matmul(py[:], lhsT=dd[:, k, :],
                                                                rhs=vv[:, k + tb * 512:k + (tb + 1) * 512],
                                                                start=(k == 0), stop=(k == 30)),
                                 reads=[dd, vv], writes=[py], inc=(k == 30))
                        y_, q_ = ys[j % 2], yq[j % 2]
                        K.op(K.act, lambda: nc.scalar.activation(out=y_[:], in_=py[:], func=AF.Identity,
                                                                 bias=dwb[:, c:c + 1]),
                             reads=[py, dwb], writes=[y_])
                        K.op(K.act, lambda: nc.scalar.activation(out=q_[:], in_=py[:], func=AF.Square,
                                                                 bias=dwb[:, c:c + 1]),
                             reads=[py, dwb], writes=[q_])
                        sl_ = slice(tb * 512, (tb + 1) * 512)
                        if c == 0:
                            K.op(K.pool, lambda: nc.gpsimd.tensor_copy(out=S1[:, sl_], in_=y_[:]),
                                 reads=[y_], writes=[S1])
                            K.op(K.pool, lambda: nc.gpsimd.tensor_copy(out=S2[:, sl_], in_=q_[:]),
                                 reads=[q_], writes=[S2])
                        else:
                            K.op(K.pool, lambda: nc.gpsimd.tensor_tensor(out=S1[:, sl_], in0=S1[:, sl_],
                                                                         in1=y_[:], op=ALU.add),
                                 reads=[y_, S1], writes=[S1])
                            K.op(K.pool, lambda: nc.gpsimd.tensor_tensor(out=S2[:, sl_], in0=S2[:, sl_],
                                                                         in1=q_[:], op=ALU.add),
                                 reads=[q_, S2], writes=[S2])
                        K.dma(K.sp, self.Ys[s, c, :, sl_], y_[:], ysl[j % 2], reads=[y_], writes=[self.Yb[s][c]])

                    def mkdiag(c):
                        dd = dg[c % 2]
                        K.op(K.pool, lambda: nc.gpsimd.tensor_tensor(
                            out=dd[:],
                            in0=self.ident_b[:].unsqueeze(1).broadcast_to([128, 31, 128]),
                            in1=dwT[:, c, :].unsqueeze(2).broadcast_to([128, 31, 128]),
                            op=ALU.mult),
                            reads=[self.ident_b, dwT], writes=[dd])

                    steps = [(c, tb) for c in range(16) for tb in range(4)]
                    loadw(0)
                    mkdiag(0)
                    gemm_glu(0, 0)
                    for i, (c, tb) in enumerate(steps):
                        if tb == 0 and c + 1 < 16:
                            loadw(c + 1)
                            mkdiag(c + 1)
                        if i + 1 < len(steps):
                            gemm_glu(*steps[i + 1])
                        conv(c, tb)
                    for tb in range(4):
                        sl_ = slice(tb * 512, (tb + 1) * 512)
                        p1, p2 = self.P[6], self.P[7]
                        K.op(K.pe, lambda: nc.tensor.matmul(p1[:], lhsT=self.ones_f[:], rhs=S1[:, sl_],
                                                            start=True, stop=True),
                             reads=[self.ones_f, S1], writes=[p1])
                        K.op(K.pe, lambda: nc.tensor.matmul(p2[:], lhsT=self.ones_f[:], rhs=S2[:, sl_],
                                                            start=True, stop=True),
                             reads=[self.ones_f, S2], writes=[p2])
                        K.op(K.act, lambda: nc.scalar.mul(out=S1[:, sl_], in_=p1[:], mul=1.0 / D),
                             reads=[p1], writes=[S1])
                        t_ = sg[0]
                        K.op(K.dve, lambda: nc.vector.tensor_tensor(out=t_[:], in0=S1[:, sl_], in1=S1[:, sl_],
                                                                    op=ALU.mult),
                             reads=[S1], writes=[t_])
                        K.op(K.dve, lambda: nc.vector.scalar_tensor_tensor(
                            out=t_[:], in0=p2[:], scalar=1.0 / D, in1=t_[:], op0=ALU.mult, op1=ALU.subtract),
                            reads=[p2, t_], writes=[t_])
                        K.op(K.act, lambda: nc.scalar.activation(out=t_[:], in_=t_[:], func=AF.Sqrt,
                                                                 bias=self.eps_ap()),
                             reads=[t_, self.small], writes=[t_])
                        K.op(K.dve, lambda: nc.vector.reciprocal(out=S2[:, sl_], in_=t_[:]),
                             reads=[t_], writes=[S2])
                    K.barrier()
                with ExitStack() as st3:
                    yb = [K.sb(st3, f"yb{i}", [128, S], F32) for i in range(2)]
                    wTb = [K.sb(st3, f"wTb{i}", [128, S], BF16) for i in range(2)]
                    szb = [K.sb(st3, f"szb{i}", [128, 512], F32) for i in range(2)]
                    s1b = [K.sb(st3, f"s1b{i}", [128, 512], F32) for i in range(2)]
                    ybsl = [K.slot("yb0"), K.slot("yb1")]
                    wtsl = [K.slot("wt0"), K.slot("wt1")]

                    def load3(c):
                        self.load_w_cast(wa[c % 2], wa[c % 2][:], W, 2 * D + c * 128, 128, wasl[c % 2])
                        K.dma(K.sp, yb[c % 2][:], self.Ys[s, c], ybsl[c % 2], reads=[self.Yb[s][c]],
                              writes=[yb[c % 2]])

                    load3(0)
                    for c in range(16):
                        if c + 1 < 16:
                            load3(c + 1)
                        y_, w_ = yb[c % 2], wTb[c % 2]
                        for tb in range(4):
                            j = c * 4 + tb
                            sl_ = slice(tb * 512, (tb + 1) * 512)
                            pz = self.P[j % 2]
                            for k in range(16):
                                K.op(K.pe, lambda: nc.tensor.matmul(pz[:], lhsT=wa[c % 2][:, k, :], rhs=hT[:, k, sl_],
                                                                    start=(k == 0), stop=(k == 15)),
                                     reads=[wa[c % 2], hT], writes=[pz], inc=(k == 15))
                            z_, a_ = szb[j % 2], s1b[j % 2]
                            K.op(K.act, lambda: nc.scalar.activation(out=z_[:], in_=pz[:], func=AF.Silu),
                                 reads=[pz], writes=[z_])
                            K.op(K.dve, lambda: nc.vector.tensor_tensor(out=y_[:, sl_], in0=y_[:, sl_],
                                                                        in1=S1[:, sl_], op=ALU.subtract),
                                 reads=[y_, S1], writes=[y_])
                            K.op(K.dve, lambda: nc.vector.tensor_tensor(out=y_[:, sl_], in0=y_[:, sl_],
                                                                        in1=S2[:, sl_], op=ALU.mult),
                                 reads=[y_, S2], writes=[y_])
                            K.op(K.act, lambda: nc.scalar.activation(out=a_[:], in_=y_[:, sl_], func=AF.Silu,
                                                                     scale=lng[:, c:c + 1], bias=lnb[:, c:c + 1]),
                                 reads=[y_, lng, lnb], writes=[a_])
                            K.op(K.dve, lambda: nc.vector.tensor_tensor(out=w_[:, sl_], in0=a_[:], in1=z_[:],
                                                                        op=ALU.mult),
                                 reads=[a_, z_], writes=[w_])
                        dst = self.WT[s].rearrange("t p c j -> p t c j")[:, :, c, :]
                        K.dma(K.sp, dst, w_[:].rearrange("p (t j) -> p t j", t=NT), wtsl[c % 2], reads=[w_],
                              writes=[self.WTb[s]])
                    K.barrier()

    def nsa_setup(self):
        K, nc, I = self.K, self.nc, self.I
        with ExitStack() as st:
            far = K.sb(st, "far", [128, 16], F32)
            sl = K.slot("const")
            K.dma(K.sp, far[:], bass.AP(I["rb_far"].tensor, 0, [[0, 128], [1, 16]]), sl, writes=[far])
            jobs = [("rb_A", None, self.TA, 0, 128), ("rb_B", "m_B", self.TB, 0, 128),
                    ("rb_C", "m_C", self.TC, 0, 128), ("rb_C", "m_C", self.TC, 128, 120)]
            for ji, (rn, mn, dst, r0, nr) in enumerate(jobs):
                raw = K.sb(st, f"raw{ji}", [128, 16, 128], F32)
                K.dma(K.sp, raw[0:nr], I[rn][r0:r0 + nr], K.slot(f"x{ji % 2}"), writes=[raw])
                K.op(K.dve, lambda: nc.vector.tensor_tensor(
                    out=raw[0:nr], in0=raw[0:nr], in1=far[0:nr].unsqueeze(2).broadcast_to([nr, 16, 128]),
                    op=ALU.subtract), reads=[raw, far], writes=[raw])
                if mn is not None:
                    mk = K.sb(st, f"mk{ji}", [128, 128], F32)
                    K.dma(K.sp, mk[0:nr], I[mn][r0:r0 + nr], K.slot(f"t{ji % 2}"), writes=[mk])
                    K.op(K.dve, lambda: nc.vector.tensor_tensor(
                        out=raw[0:nr], in0=raw[0:nr], in1=mk[0:nr].unsqueeze(1).broadcast_to([nr, 16, 128]),
                        op=ALU.add), reads=[raw, mk], writes=[raw])
                K.dma(K.sp, dst[r0:r0 + nr], raw[0:nr], K.slot(f"o{ji % 2}"), reads=[raw], writes=[self.Tb])
            K.barrier()

    def nsa_layer(self, li, l):
        K, nc, I = self.K, self.nc, self.I
        p = f"l{l}_"
        W = I[p + "w_in"]
        for s in range(self.nseq):
            with ExitStack() as so:
                gates = K.sb(so, "gates", [128, NT, 48], F32)
                with ExitStack() as st:
                    hT = K.sb(st, "hT", [128, 16, S], BF16)
                    with ExitStack() as st1:
                        self.p1_phase(st1, li, s, hT)
                        K.barrier()
                    if self.dbg >= 2:
                        self.n2_phase(st, s, W, hT, gates)
                    K.barrier()
                with ExitStack() as st:
                    KcT = K.sb(st, "KcT", [128, 4, 128], BF16)
                    VcO = K.sb(st, "VcO", [128, 4, 162], BF16)
                    with ExitStack() as st3:
                        if self.dbg >= 3:
                            self.n3_phase(st3, s, p, KcT, VcO)
                        K.barrier()
                    if self.dbg >= 4:
                        self.n4_phase(st, s, KcT, VcO, gates)
                    K.barrier()

    def n2_phase(self, st, s, W, hT, gates):
        K, nc = self.K, self.nc
        wst = [K.sb(st, f"wst{i}", [128, 16, 128], BF16) for i in range(3)]
        wsl = [K.slot(f"wa{i}") for i in range(2)] + [K.slot("wb0")]
        stg = [K.sb(st, f"stg{i}", [128, S], BF16) for i in range(2)]
        gsl = [K.slot("ys0"), K.slot("ys1")]
        chunks = []
        for h in range(16):
            chunks.append((h * 128, "q", h))
        for wi, base in enumerate((2048, 3072, 4096, 2560)):
            for g in range(4):
                chunks.append((base + g * 128, "k", (wi, g)))
        for c in range(16):
            chunks.append((5168 + c * 128, "z", c))

        def loadw(i):
            self.load_w_cast(wst[i % 3], wst[i % 3][:], W, chunks[i][0], 128, wsl[i % 3])

        loadw(0)
        loadw(1)
        j = 0
        for i, (col0, kind, idx) in enumerate(chunks):
            if i + 2 < len(chunks):
                loadw(i + 2)
            w_ = wst[i % 3]
            g_ = stg[i % 2]
            for tb in range(4):
                ps_ = self.P[j % 4]
                j += 1
                sl_ = slice(tb * 512, (tb + 1) * 512)
                for k in range(16):
                    K.op(K.pe, lambda: nc.tensor.matmul(ps_[:], lhsT=w_[:, k, :], rhs=hT[:, k, sl_],
                                                        start=(k == 0), stop=(k == 15)),
                         reads=[w_, hT], writes=[ps_], inc=(k == 15))
                if kind == "q":
                    K.op(K.act, lambda: nc.scalar.mul(out=g_[:, sl_], in_=ps_[:], mul=float(DK) ** -0.5),
                         reads=[ps_], writes=[g_])
                elif kind == "z":
                    K.op(K.act, lambda: nc.scalar.activation(out=g_[:, sl_], in_=ps_[:], func=AF.Silu),
                         reads=[ps_], writes=[g_])
                else:
                    K.op(K.dve, lambda: nc.vector.tensor_copy(out=g_[:, sl_], in_=ps_[:]),
                         reads=[ps_], writes=[g_])
            if kind == "q":
                dst = self.QT[s].rearrange("t p h j -> p t h j")[:, :, idx, :]
                K.dma(K.sp, dst, g_[:].rearrange("p (t j) -> p t j", t=NT), gsl[i % 2], reads=[g_],
                      writes=[self.QTb[s]])
            elif kind == "z":
                dst = self.SZ[s].rearrange("t p h j -> p t h j")[:, :, idx, :]
                K.dma(K.sp, dst, g_[:].rearrange("p (t j) -> p t j", t=NT), gsl[i % 2], reads=[g_],
                      writes=[self.SZb[s]])
            else:
                wi, g = idx
                K.dma(K.sp, self.KT[s, wi, g], g_[:], gsl[i % 2], reads=[g_], writes=[self.KTb[s]])
        wmv = [K.sb(st, f"wmv{i}", [128, 16, 512], BF16) for i in range(2)]
        wg = K.sb(st, "wg", [128, 16, 48], BF16)
        vst = [K.sb(st, f"vst{i}", [128, 4, 130], BF16) for i in range(2)]
        msl = [K.slot("wb1"), K.slot("wo")]
        vsl = [K.slot("yb0"), K.slot("yb1")]
        for i in range(2):
            K.op(K.pool, lambda: nc.gpsimd.memset(vst[i][:, :, 128:129], 1.0), writes=[vst[i]])
            K.op(K.pool, lambda: nc.gpsimd.memset(vst[i][:, :, 129:130], 0.0), writes=[vst[i]])
        for vi, base in enumerate((3584, 4608)):
            self.load_w_cast(wmv[vi], wmv[vi][:], W, base, 512, msl[vi])
        self.load_w_cast(wg, wg[:], W, 5120, 48, K.slot("const"))
        for vi in range(2):
            for tt in range(NT):
                ps_ = self.P[j % 4]
                j += 1
                for k in range(16):
                    K.op(K.pe, lambda: nc.tensor.matmul(ps_[:], lhsT=hT[:, k, tt * 128:(tt + 1) * 128],
                                                        rhs=wmv[vi][:, k, :], start=(k == 0), stop=(k == 15)),
                         reads=[wmv[vi], hT], writes=[ps_], inc=(k == 15))
                v_ = vst[tt % 2]
                eng = K.act if tt % 2 == 0 else K.dve
                src = ps_[:].rearrange("p (g d) -> p g d", g=4)
                if tt % 2 == 0:
                    K.op(K.act, lambda: nc.scalar.copy(out=v_[:, :, 0:128], in_=src), reads=[ps_], writes=[v_])
                else:
                    K.op(K.dve, lambda: nc.vector.tensor_copy(out=v_[:, :, 0:128], in_=src), reads=[ps_],
                         writes=[v_])
                K.dma(K.sp, self.VA[s, vi, tt], v_[:].rearrange("p g d -> p (g d)"), vsl[tt % 2], reads=[v_],
                      writes=[self.VAb[s]])
        for tt in range(NT):
            ps_ = self.P[j % 4]
            j += 1
            for k in range(16):
                K.op(K.pe, lambda: nc.tensor.matmul(ps_[:, 0:48], lhsT=hT[:, k, tt * 128:(tt + 1) * 128],
                                                    rhs=wg[:, k, :], start=(k == 0), stop=(k == 15)),
                     reads=[wg, hT], writes=[ps_], inc=(k == 15))
            K.op(K.act, lambda: nc.scalar.activation(out=gates[:, tt, :], in_=ps_[:, 0:48], func=AF.Sigmoid),
                 reads=[ps_], writes=[gates])

    def n3_phase(self, st, s, p, KcT, VcO):
        K, nc, I = self.K, self.nc, self.I
        srcT = [K.sb(st, f"cT{i}", [128, 4, S], BF16) for i in range(2)]
        w1 = [K.sb(st, f"w1{i}", [128, 32, 128], BF16) for i in range(2)]
        w2 = [K.sb(st, f"w2{i}", [128, 128], BF16) for i in range(2)]
        HT = [K.sb(st, f"HT{i}", [128, 4, 127], BF16) for i in range(2)]
        posb = K.sb(st, "posb", [128, 32], BF16)
        cvec = K.sb(st, "cvec", [128, 2], F32)
        csl = K.slot("const")
        K.dma(K.sp, srcT[0][:], self.KT[s, 0].rearrange("g p t -> p g t"), K.slot("x0"), reads=[self.KTb[s]],
              writes=[srcT[0]])
        K.dma(K.sp, srcT[1][:], self.KT[s, 3].rearrange("g p t -> p g t"), K.slot("x1"), reads=[self.KTb[s]],
              writes=[srcT[1]])
        for i, nm in enumerate(("ck", "cv")):
            K.dma(K.pool, w1[i][:], I[p + nm + "_w1"].rearrange("(l d) j -> d l j", d=128), K.slot(f"wa{i}"),
                  writes=[w1[i]])
            K.dma(K.pool, w2[i][:], I[p + nm + "_w2"], K.slot(f"wb{i}"), writes=[w2[i]])
        K.dma(K.pool, posb[:], I[p + "posT"], csl, writes=[posb])
        K.op(K.pool, lambda: nc.gpsimd.memset(KcT[:], 0.0), writes=[KcT])
        K.op(K.pool, lambda: nc.gpsimd.memset(VcO[:], 0.0), writes=[VcO])
        for g in range(4):
            K.dma(K.pool, VcO[:, g, 128:162], I["c_ov"], csl, writes=[VcO])
        for i in range(2):
            pc = self.P[4 + i]
            for l_ in range(32):
                K.op(K.pe, lambda: nc.tensor.matmul(pc[:, 0:1], lhsT=w1[i][:, l_, :], rhs=posb[:, l_:l_ + 1],
                                                    start=(l_ == 0), stop=(l_ == 31)),
                     reads=[w1[i], posb], writes=[pc], inc=(l_ == 31))
            K.op(K.act, lambda: nc.scalar.copy(out=cvec[:, i:i + 1], in_=pc[:, 0:1]), reads=[pc], writes=[cvec])
            ph = self.P[i]
            for l_ in range(32):
                K.op(K.pe, lambda: nc.tensor.matmul(ph[:, 0:508].rearrange("p (g n) -> p g n", g=4),
                                                    lhsT=w1[i][:, l_, :],
                                                    rhs=srcT[i][:, :, l_:l_ + 16 * 126 + 1:16],
                                                    start=(l_ == 0), stop=(l_ == 31)),
                     reads=[w1[i], srcT[i]], writes=[ph], inc=(l_ == 31))
            K.op(K.act, lambda: nc.scalar.activation(out=HT[i][:], in_=ph[:, 0:508].rearrange("p (g n) -> p g n", g=4),
                                                     func=AF.Silu, bias=cvec[:, i:i + 1]),
                 reads=[ph, cvec], writes=[HT[i]])
        pk = self.P[2]
        K.op(K.pe, lambda: nc.tensor.matmul(pk[:, 0:508], lhsT=w2[0][:], rhs=HT[0][:].rearrange("p g n -> p (g n)"),
                                            start=True, stop=True),
             reads=[w2[0], HT[0]], writes=[pk])
        K.op(K.act, lambda: nc.scalar.copy(out=KcT[:, :, 0:127], in_=pk[:, 0:508].rearrange("p (g n) -> p g n", g=4)),
             reads=[pk], writes=[KcT])
        pv = self.P[3]
        for g in range(4):
            K.op(K.pe, lambda: nc.tensor.matmul(pv[0:127, g * 128:(g + 1) * 128], lhsT=HT[1][:, g, :], rhs=w2[1][:],
                                                start=True, stop=True),
                 reads=[w2[1], HT[1]], writes=[pv], inc=(g == 3))
        K.op(K.dve, lambda: nc.vector.tensor_copy(out=VcO[0:127, :, 0:128],
                                                  in_=pv[0:127, :].rearrange("p (g d) -> p g d", g=4)),
             reads=[pv], writes=[VcO])

    def n4_phase(self, st, s, KcT, VcO, gates):
        K, nc, I = self.K, self.nc, self.I
        ksT = K.sb(st, "ksT", [128, 4, S], BF16)
        kwT = K.sb(st, "kwT", [128, 4, S], BF16)
        vsA = K.sb(st, "vsA", [128, NT, 520], BF16)
        vwA = K.sb(st, "vwA", [128, NT, 520], BF16)
        Eb = K.sb(st, "Eb", [128, 64, 128], BF16)
        TA = K.sb(st, "TA", [128, 16, 128], F32)
        TB = K.sb(st, "TB", [128, 16, 128], F32)
        keep = K.sb(st, "keep", [128, NT, 32], F32)
        addm = K.sb(st, "addm", [128, NT, 32], F32)
        M0 = K.sb(st, "M0", [128, 128], BF16)
        csl = K.slot("const")
        K.dma(K.sp, ksT[:], self.KT[s, 1].rearrange("g p t -> p g t"), K.slot("x0"), reads=[self.KTb[s]], writes=[ksT])
        K.dma(K.sp, kwT[:], self.KT[s, 2].rearrange("g p t -> p g t"), K.slot("x1"), reads=[self.KTb[s]], writes=[kwT])
        K.dma(K.sp, vsA[:], self.VA[s, 0].rearrange("t p c -> p t c"), K.slot("t0"), reads=[self.VAb[s]], writes=[vsA])
        K.dma(K.sp, vwA[:], self.VA[s, 1].rearrange("t p c -> p t c"), K.slot("t1"), reads=[self.VAb[s]], writes=[vwA])
        for j in range(4):
            K.dma(K.pool, Eb[:, j * 16:(j + 1) * 16, :],
                  I["c_E"].rearrange("p (e k) -> p e k", k=128)[:, j * 16:(j + 1) * 16, :], K.slot("wa0"), writes=[Eb],
                  group=(j > 0))
        K.dma(K.pool, M0[:], I["m_W0"], K.slot("wa1"), writes=[M0])
        K.dma(K.sp, TA[:], self.TA, csl, reads=[self.Tb], writes=[TA])
        K.dma(K.sp, TB[:], self.TB, csl, reads=[self.Tb], writes=[TB])
        K.dma(K.sp, keep[:], I["c_keep"].rearrange("p (t s) -> p t s", s=32), csl, writes=[keep])
        K.dma(K.sp, addm[:], I["c_add"].rearrange("p (t s) -> p t s", s=32), csl, writes=[addm])
        qr = [K.sb(st, f"qr{i}", [128, 16, 128], BF16) for i in range(2)]
        zr = [K.sb(st, f"zr{i}", [128, 16, 128], BF16) for i in range(2)]
        cbr = [K.sb(st, f"cbr{i}", [128, 16, 128], F32) for i in range(2)]
        NPT = 4
        pTr = [K.sb(st, f"pT{i}", [128, 512], BF16) for i in range(NPT)]
        pTc = [K.sb(st, f"pTc{i}", [128, 512], BF16) for i in range(2)]
        scr = [K.sb(st, f"sc{i}", [128, 512], F32) for i in range(2)]
        posb = [K.sb(st, f"posb{i}", [128, 2, 2, 129], F32) for i in range(4)]
        ptmp = K.sb(st, "ptmp", [128, 2, 2, 128], F32)
        otiles = [K.sb(st, f"otile{i}", [128, D], F32) for i in range(2)]
        ob = K.sb(st, "ob", [128, D], BF16)
        oT = [K.sb(st, f"oT{i}", [128, 16, 128], BF16) for i in range(1)]
        impns = [K.sb(st, f"impn{i}", [128, 4, 32], F32) for i in range(2)]
        t8s = [K.sb(st, f"t8{i}", [128, 4, 8], F32) for i in range(2)]
        selms = [K.sb(st, f"selm{i}", [128, 4, 32], F32) for i in range(2)]
        selTs = [K.sb(st, f"selT{i}", [128, 128], BF16) for i in range(2)]
        rcs = [K.sb(st, f"rc{i}", [128, 8], F32) for i in range(6)]
        qsl = [K.slot("ys0"), K.slot("ys1")]
        zsl = [K.slot("yb0"), K.slot("yb1")]
        bsl = [K.slot("wb0"), K.slot("wb1")]
        osl = [K.slot("o0"), K.slot("o1")]
        SCB = [self.P[0], self.P[1], self.P[7]]
        Pm = self.P[6]
        pm = self.pt[:, 6, :]
        po_sel = self.pt[:, 2:4, :]
        po_win = self.pt[:, 4:6, :]
        Psel = [self.P[2], self.P[3]]
        Pwin = [self.P[4], self.P[5]]

        def load_qc(tt):
            K.dma(K.sp, qr[tt % 2][:], self.QT[s, tt], qsl[tt % 2], reads=[self.QTb[s]], writes=[qr[tt % 2]])
            K.dma(K.sp, cbr[tt % 2][:], self.TC[120 - 8 * tt:248 - 8 * tt], bsl[tt % 2], reads=[self.Tb],
                  writes=[cbr[tt % 2]])

        def load_z(tt):
            K.dma(K.sp, zr[tt % 2][:], self.SZ[s, tt], zsl[tt % 2], reads=[self.SZb[s]], writes=[zr[tt % 2]])

        cnt = {"u": 0, "rc": 0, "pb": 0}

        def cmp_half(g, half, T):
            otile = otiles[T % 2]
            impn = impns[T % 2]
            rc = rcs[cnt["rc"] % 6]
            cnt["rc"] += 1
            K.op(K.dve, lambda: nc.vector.tensor_scalar(out=rc[:, 0:2], in0=pm[:, 128:291:162], scalar1=1e-30,
                                                        scalar2=None, op0=ALU.max),
                 reads=[Pm], writes=[rc])
            K.op(K.dve, lambda: nc.vector.reciprocal(out=rc[:, 0:2], in_=rc[:, 0:2]), reads=[rc], writes=[rc])
            h0 = 4 * g + 2 * half
            K.op(K.dve, lambda: nc.vector.tensor_tensor(out=rc[:, 4:6], in0=rc[:, 0:2],
                                                        in1=gates[:, T, h0 * 3:h0 * 3 + 4:3], op=ALU.mult),
                 reads=[rc, gates], writes=[rc])
            for q2 in range(2):
                h = h0 + q2
                K.op(K.act, lambda: nc.scalar.activation(out=otile[:, h * 128:(h + 1) * 128],
                                                         in_=pm[:, q2 * 162:q2 * 162 + 128], func=AF.Copy,
                                                         scale=rc[:, 4 + q2:5 + q2]),
                     reads=[Pm, rc], writes=[otile])
            for q2 in range(2):
                src = pm[:, q2 * 162 + 130:q2 * 162 + 162]
                if half == 0 and q2 == 0:
                    K.op(K.dve, lambda: nc.vector.tensor_scalar(out=impn[:, g, :], in0=src, scalar1=rc[:, 0:1],
                                                                scalar2=None, op0=ALU.mult),
                         reads=[Pm, rc], writes=[impn])
                else:
                    K.op(K.dve, lambda: nc.vector.scalar_tensor_tensor(
                        out=impn[:, g, :], in0=src, scalar=rc[:, q2:q2 + 1], in1=impn[:, g, :],
                        op0=ALU.mult, op1=ALU.add), reads=[Pm, rc, impn], writes=[impn])

        def fin_branch(po_view, Pb, g, T, branch):
            otile = otiles[T % 2]
            pb_ = posb[cnt["pb"] % 4]
            cnt["pb"] += 1
            rc = rcs[cnt["rc"] % 6]
            cnt["rc"] += 1
            K.op(K.dve, lambda: nc.vector.tensor_copy(
                out=pb_[:], in_=po_view[:, :, 0:258].rearrange("p a (b c) -> p a b c", c=129)),
                reads=Pb, writes=[pb_])
            K.op(K.dve, lambda: nc.vector.reciprocal(out=rc[:, 0:4].rearrange("p (a b) -> p a b", a=2),
                                                     in_=pb_[:, :, :, 128]),
                 reads=[pb_], writes=[rc])
            gcol = (4 * g) * 3 + branch
            K.op(K.dve, lambda: nc.vector.tensor_tensor(out=rc[:, 4:8], in0=rc[:, 0:4],
                                                        in1=gates[:, T, gcol:gcol + 10:3], op=ALU.mult),
                 reads=[rc, gates], writes=[rc])
            K.op(K.pool, lambda: nc.gpsimd.tensor_tensor(
                out=ptmp[:], in0=pb_[:, :, :, 0:128],
                in1=rc[:, 4:8].rearrange("p (a b) -> p a b", a=2).unsqueeze(3).broadcast_to([128, 2, 2, 128]),
                op=ALU.mult), reads=[pb_, rc], writes=[ptmp])
            og = otile[:, 4 * g * 128:(4 * g + 4) * 128].rearrange("p (a b d) -> p a b d", a=2, b=2)
            K.op(K.pool, lambda: nc.gpsimd.tensor_tensor(out=og, in0=og, in1=ptmp[:], op=ALU.add),
                 reads=[otile, ptmp], writes=[otile])

        if self.dbg < 5:
            return

        def topk_stages(T):
            impn, t8, selm, selT = impns[T % 2], t8s[T % 2], selms[T % 2], selTs[T % 2]

            def E():
                K.op(K.dve, lambda: nc.vector.tensor_tensor(out=impn[:], in0=impn[:],
                                                            in1=keep[:, T, :].unsqueeze(1).broadcast_to([128, 4, 32]),
                                                            op=ALU.mult),
                     reads=[impn, keep], writes=[impn])
                K.op(K.dve, lambda: nc.vector.tensor_tensor(out=impn[:], in0=impn[:],
                                                            in1=addm[:, T, :].unsqueeze(1).broadcast_to([128, 4, 32]),
                                                            op=ALU.add),
                     reads=[impn, addm], writes=[impn])

            def F():
                for g in range(4):
                    K.op(K.dve, lambda: nc.vector.max(out=t8[:, g, :], in_=impn[:, g, :]), reads=[impn], writes=[t8])

            def G():
                for g in range(4):
                    K.op(K.dve, lambda: nc.vector.tensor_scalar(out=selm[:, g, :], in0=impn[:, g, :],
                                                                scalar1=t8[:, g, 7:8], scalar2=-1.0,
                                                                op0=ALU.is_ge, op1=ALU.add),
                         reads=[impn, t8], writes=[selm])

            def H():
                K.op(K.pe, lambda: nc.tensor.transpose(out=pm[:, 0:128], in_=selm[:].rearrange("p g s -> p (g s)"),
                                                       identity=self.ident_f[:]),
                     reads=[selm, self.ident_f], writes=[Pm])

            def I_():
                K.op(K.act, lambda: nc.scalar.copy(out=selT[:], in_=pm[:, 0:128]), reads=[Pm], writes=[selT])

            return [E, F, G, H, I_]

        def final_stages(T):
            otile = otiles[T % 2]
            z_ = zr[T % 2]
            pv = pm.bitcast(BF16)
            o_ = oT[0]

            def cp():
                K.op(K.act, lambda: nc.scalar.copy(out=ob[:], in_=otile[:]), reads=[otile], writes=[ob])

            def tr(half):
                for c8 in range(8):
                    c = half * 8 + c8
                    K.op(K.pe, lambda: nc.tensor.transpose(out=pv[:, c8 * 128:(c8 + 1) * 128],
                                                           in_=ob[:, c * 128:(c + 1) * 128], identity=self.ident_b[:]),
                         reads=[ob, self.ident_b], writes=[Pm], inc=(c8 == 7))

            def mu(half):
                K.op(K.dve, lambda: nc.vector.tensor_tensor(
                    out=o_[:, half * 8:(half + 1) * 8, :], in0=pv[:].rearrange("p (c t) -> p c t", c=8),
                    in1=z_[:, half * 8:(half + 1) * 8, :], op=ALU.mult),
                    reads=[Pm, z_], writes=[o_])
                if half == 1:
                    K.dma(K.sp, self.WT[s, T], o_[:], osl[T % 2], reads=[o_], writes=[self.WTb[s]])

            return [cp, lambda: tr(0), lambda: mu(0), lambda: tr(1), lambda: mu(1)]

        def run_tile(T, with_units, Tc, fifo):
            allu = []
            if with_units:
                for g in range(4):
                    for kc in range(T + 1):
                        allu.append((g, "s", kc))
                    for kc in range(max(0, T - 4), T + 1):
                        allu.append((g, "w", kc))
            if Tc is not None:
                n0 = len(allu)
                start = min(n0, max(6, n0 // 4))
                gap = max(1, (n0 - start) // 5)
                for g in range(4):
                    allu.insert(min(len(allu), start + g * (gap + 1)), (g, "c", None))
            N = len(allu)
            first_seen = {}
            ustate = {}
            cdone = {"n": 0}

            def qk(n):
                g, br, kc = allu[n]
                u = cnt["u"]
                cnt["u"] += 1
                ustate[n] = u
                ps_ = SCB[u % 3]
                if br == "c":
                    q_ = qr[Tc % 2]
                    K.op(K.pe, lambda: nc.tensor.matmul(ps_[:], lhsT=KcT[:, g, :], rhs=q_[:, 4 * g:4 * g + 4, :],
                                                        start=True, stop=True),
                         reads=[KcT, q_], writes=[ps_])
                    return
                q_ = qr[T % 2]
                selT = selTs[T % 2]
                kT = ksT if br == "s" else kwT
                extra = (br == "s" and kc < T) or (br == "w" and kc == T - 4)
                K.op(K.pe, lambda: nc.tensor.matmul(ps_[:], lhsT=kT[:, g, kc * 128:(kc + 1) * 128],
                                                    rhs=q_[:, 4 * g:4 * g + 4, :], start=True, stop=not extra),
                     reads=[kT, q_], writes=[ps_], inc=not extra)
                if br == "s" and kc < T:
                    K.op(K.pe, lambda: nc.tensor.matmul(ps_[:], lhsT=Eb[:, g * 16 + kc, :],
                                                        rhs=selT[:].unsqueeze(1).broadcast_to([128, 4, 128]),
                                                        start=False, stop=True),
                         reads=[Eb, selT], writes=[ps_])
                elif br == "w" and kc == T - 4:
                    K.op(K.pe, lambda: nc.tensor.matmul(ps_[:], lhsT=self.ident_b[:],
                                                        rhs=M0[:].unsqueeze(1).broadcast_to([128, 4, 128]),
                                                        start=False, stop=True),
                         reads=[self.ident_b, M0], writes=[ps_])

            def rest_cmp(g, u):
                while len(fifo) > 4:
                    fifo.pop(0)()
                ps_, sc_, pT_ = SCB[u % 3], scr[u % 2], pTc[g % 2]
                cb_ = cbr[Tc % 2]
                K.op(K.dve, lambda: nc.vector.tensor_tensor(
                    out=sc_[:], in0=ps_[:], in1=cb_[:, 4 * g:4 * g + 4, :].rearrange("p h j -> p (h j)"),
                    op=ALU.add), reads=[ps_, cb_], writes=[sc_])
                K.op(K.act, lambda: nc.scalar.activation(out=pT_[:], in_=sc_[:], func=AF.Exp),
                     reads=[sc_], writes=[pT_])

                def pv(half):
                    for q2 in range(2):
                        r = half * 2 + q2
                        K.op(K.pe, lambda: nc.tensor.matmul(pm[:, q2 * 162:q2 * 162 + 162],
                                                            lhsT=pT_[:, r * 128:(r + 1) * 128], rhs=VcO[:, g, :],
                                                            start=True, stop=True),
                             reads=[pT_, VcO], writes=[Pm], inc=(q2 == 1))

                fifo.append(lambda: pv(0))
                fifo.append(lambda: cmp_half(g, 0, Tc))
                fifo.append(lambda: pv(1))
                fifo.append(lambda: cmp_half(g, 1, Tc))
                cdone["n"] += 1
                if cdone["n"] == 4:
                    fifo.extend(topk_stages(Tc))

            def rest(n):
                g, br, kc = allu[n]
                u = ustate.pop(n)
                if br == "c":
                    rest_cmp(g, u)
                    return
                ps_ = SCB[u % 3]
                pT_ = pTr[u % NPT]
                tab = TB if kc == T else (TA if kc == T - 1 else None)
                if tab is not None:
                    sc_ = scr[u % 2]
                    K.op(K.dve, lambda: nc.vector.tensor_tensor(
                        out=sc_[:], in0=ps_[:], in1=tab[:, 4 * g:4 * g + 4, :].rearrange("p h j -> p (h j)"),
                        op=ALU.add), reads=[ps_, tab], writes=[sc_])
                    K.op(K.act, lambda: nc.scalar.activation(out=pT_[:], in_=sc_[:], func=AF.Exp),
                         reads=[sc_], writes=[pT_])
                else:
                    K.op(K.act, lambda: nc.scalar.activation(out=pT_[:], in_=ps_[:], func=AF.Exp),
                         reads=[ps_], writes=[pT_])
                po_view, Pb, vA = (po_sel, Psel, vsA) if br == "s" else (po_win, Pwin, vwA)
                first = (g, br) not in first_seen
                first_seen[(g, br)] = True
                last = kc == T
                for r in range(4):
                    K.op(K.pe, lambda: nc.tensor.matmul(
                        po_view[:, r // 2, (r % 2) * 129:(r % 2) * 129 + 129],
                        lhsT=pT_[:, r * 128:(r + 1) * 128], rhs=vA[:, kc, g * 130:g * 130 + 129],
                        start=(first and r % 2 == 0), stop=last, skip_group_check=True),
                        reads=[pT_, vA], writes=[Pb[r // 2]], inc=(r == 3))
                if last:
                    fin_branch(po_view, Pb, g, T, 1 if br == "s" else 2)

            for n in range(N):
                if n == 0:
                    qk(0)
                    if N > 1:
                        qk(1)
                if n + 2 < N:
                    qk(n + 2)
                rest(n)
                if fifo:
                    fifo.pop(0)()
            while fifo:
                fifo.pop(0)()

        load_qc(0)
        run_tile(0, False, 0, [])
        for tt in range(NT):
            if tt + 1 < NT:
                load_qc(tt + 1)
            load_z(tt)
            fifo = final_stages(tt - 1) if tt > 0 else []
            run_tile(tt, True, tt + 1 if tt + 1 < NT else None, fifo)
        for f in final_stages(NT - 1):
            f()


def host_consts():
    c = {}
    c["c_ident"] = np.eye(128, dtype=np.float32)
    return c


def layer_inputs(inputs, layers):
    m = {}
    for l in layers:
        p = f"l{l}_"
        m[p + "norm"] = np.ascontiguousarray(inputs[p + "norm"].reshape(1, D))
        m[p + "w_out"] = inputs[p + "w_out"]
        m[p + "w_in"] = inputs[p + "w_in"]
        if l % 2 == 0:
            dw = inputs[p + "dw_w"].reshape(31, 16, 128).transpose(2, 1, 0)
            m[p + "dw_wT"] = np.ascontiguousarray(dw.reshape(128, 16 * 31))
            for nm in ("dw_b", "ln_g", "ln_b"):
                m[p + nm] = np.ascontiguousarray(inputs[p + nm].reshape(16, 128).T)
        else:
            m[p + "posT"] = np.ascontiguousarray(inputs[p + "cmp_pos"].T)
            for nm in ("ck_w1", "ck_w2", "cv_w1", "cv_w2"):
                m[p + nm] = inputs[p + nm]
    m["final_norm"] = np.ascontiguousarray(inputs["final_norm"].reshape(1, D))
    return m


_PROG_CACHE = {}


def run(inputs, layers=(0, 1, 2, 3), final_norm=True, ncores=NCORES, nseq=NSEQ, x_override=None, trace=False, dbg=99):
    key = (tuple(layers), final_norm, nseq, dbg)
    if key not in _PROG_CACHE:
        _PROG_CACHE[key] = Prog(layers, final_norm, nseq, dbg)
    prog = _PROG_CACHE[key]
    shared = dict(host_consts())
    shared.update(layer_inputs(inputs, layers))
    if any(l % 2 == 1 for l in layers):
        shared.update(nsa_host_tables(inputs["rel_bias"]))
    x = inputs["x"] if x_override is None else x_override
    in_maps = []
    for c in range(ncores):
        m = dict(shared)
        m["x"] = np.ascontiguousarray(x[c * nseq:(c + 1) * nseq])
        in_maps.append(m)
    res = run_bass_kernel_spmd(prog.nc, in_maps, core_ids=list(range(ncores)), **({'trace': True} if trace else {}))
    if trace:
        print('exec_time_ns', res.exec_time_ns)
    return np.concatenate([r["y"] for r in res.results], axis=0)


def _t5_bucket_np(dist):
    dist = np.maximum(dist, 0)
    d = np.maximum(dist, 16).astype(np.float32)
    large = 16 + (np.log(d / np.float32(16)) / np.float32(np.log(8.0)) * np.float32(16)).astype(np.int32)
    large = np.minimum(large, 31)
    return np.where(dist < 16, dist, large)


def nsa_host_tables(rel_bias):
    m = {}
    k = np.arange(128)[:, None]
    t = np.arange(128)[None, :]
    dA = t - k + 128
    dB = t - k
    mp = np.arange(248)[:, None]
    dC = t - 16 * (mp - 120) - 31
    m["rb_A"] = np.ascontiguousarray(rel_bias[_t5_bucket_np(dA)].transpose(0, 2, 1))
    m["rb_B"] = np.ascontiguousarray(rel_bias[_t5_bucket_np(dB)].transpose(0, 2, 1))
    m["rb_C"] = np.ascontiguousarray(rel_bias[_t5_bucket_np(dC)].transpose(0, 2, 1))
    m["rb_far"] = np.ascontiguousarray(rel_bias[31:32, :])
    m["m_B"] = np.where(dB >= 0, 0.0, NEGM).astype(np.float32)
    m["m_C"] = np.where(dC >= 0, 0.0, NEGM).astype(np.float32)
    m["m_W0"] = np.where(k > t, 0.0, NEGM).astype(np.float32)
    E = np.zeros((4, 32, 4, 16, 128), np.float32)
    for g in range(4):
        for kc in range(16):
            E[g, 2 * kc, g, kc, 0:64] = -NEGM
            E[g, 2 * kc + 1, g, kc, 64:128] = -NEGM
    m["c_E"] = np.ascontiguousarray(E.reshape(128, 64 * 128))
    tl = np.arange(128)[:, None, None]
    ti = np.arange(16)[None, :, None]
    sb = np.arange(32)[None, None, :]
    tabs = ti * 128 + tl
    cur = tabs // 64
    forced = (sb == 0) | (sb == cur) | (sb == cur - 1)
    causal = sb * 64 <= tabs
    keep = (causal & ~forced).astype(np.float32)
    add = np.where(causal, np.where(forced, 1e6, 0.0), -1e30).astype(np.float32)
    m["c_keep"] = np.ascontiguousarray(keep.reshape(128, 512))
    m["c_add"] = np.ascontiguousarray(add.reshape(128, 512))
    ov = np.zeros((128, 34), np.float32)
    n = np.arange(127)[:, None]
    s0 = np.arange(32)[None, :] * 64
    j0 = n * 16
    ov[:127, 0] = 1.0
    ov[:127, 2:34] = ((j0 < s0 + 64) & (j0 + 32 > s0)).astype(np.float32)
    m["c_ov"] = ov
    return m


def kernel(**inputs):
    inputs = {k: np.asarray(v) for k, v in inputs.items()}
    return run(inputs)
```

```python
import numpy as np
from contextlib import ExitStack
import concourse.bass as bass
import concourse.mybir as mybir
from concourse.bass_utils import run_bass_kernel_spmd

F32 = mybir.dt.float32
BF16 = mybir.dt.bfloat16
AF = mybir.ActivationFunctionType
ALU = mybir.AluOpType
AX = mybir.AxisListType

D = 2048
S = 2048
NT = S // 128
NSEQ = 2
NCORES = 8
NH = 16
NG = 4
DK = 128
EPS = 1e-6
NEGM = -30000.0
CONF_W = 3 * D
NSA_W = 2048 + 6 * 512 + 48 + 2048


class Owner:
    def __init__(self, K, name):
        self.sem = K.stack.enter_context(K.nc.semaphore(name))
        self.count = 0
        self.name = name


class Eng(Owner):
    def __init__(self, K, name, h):
        super().__init__(K, "e_" + name)
        self.h = h
        self.seen = {}
        self.is_pe = name == "pe"
        self.last_inc = True


class Buf:
    def __init__(self, ap, name=""):
        self.ap = ap
        self.name = name
        self.writers = {}
        self.readers = {}
        self.prev = {}
        self.excl = False

    def __getitem__(self, idx):
        return self.ap[idx]


def _merge(d, own, val):
    if d.get(own, 0) < val:
        d[own] = val


class Kern:
    def __init__(self, nc):
        self.nc = nc
        self.stack = ExitStack()
        self.pe = Eng(self, "pe", nc.tensor)
        self.act = Eng(self, "act", nc.scalar)
        self.dve = Eng(self, "dve", nc.vector)
        self.pool = Eng(self, "pool", nc.gpsimd)
        self.sp = Eng(self, "sp", nc.sync)
        self.engs = [self.pe, self.act, self.dve, self.pool, self.sp]
        self.slots = {}
        self.n_dram = 0

    def slot(self, name):
        if name not in self.slots:
            self.slots[name] = Owner(self, "d_" + name)
        return self.slots[name]

    def sb(self, st, name, shape, dt):
        self.n_dram += 1
        nm = f"{name}_{self.n_dram}"
        return Buf(st.enter_context(self.nc.sbuf_tensor(nm, list(shape), dt)), nm)

    def dram(self, name, shape, dt, kind=None):
        if kind is None:
            t = self.nc.dram_tensor(name, list(shape), dt)
        else:
            t = self.nc.dram_tensor(name, list(shape), dt, kind=kind)
        return t.ap()

    def _deps(self, eng, reads, writes):
        deps = {}

        def add(d, raw):
            for own, val in d.items():
                if own is eng and (eng.is_pe or not raw or val > eng.count):
                    continue
                _merge(deps, own, val)

        for b in reads:
            add(b.writers, True)
            if b.excl:
                add(b.readers, False)
        for b in writes:
            add(b.writers, False)
            add(b.readers, False)
            add(b.prev, False)
        return [(o, v) for o, v in deps.items() if eng.seen.get(o, 0) < v]

    def _emit_waits(self, eng, need, fn):
        for o, v in need[1:]:
            eng.h.wait_ge(o.sem, v)
        ins = fn()
        if need:
            ins._wait_ge(need[0][0].sem, need[0][1])
        for o, v in need:
            eng.seen[o] = v
        return ins

    def _update(self, ev, reads, writes):
        own, val = ev
        for b in reads:
            _merge(b.readers, own, val)
        for b in writes:
            if b.readers:
                prev = dict(b.readers)
                for o, v in b.writers.items():
                    _merge(prev, o, v)
                b.prev = prev
                b.readers = {}
                b.writers = {}
            _merge(b.writers, own, val)

    def op(self, eng, fn, reads=(), writes=(), inc=True):
        need = self._deps(eng, reads, writes)
        ins = self._emit_waits(eng, need, fn)
        if inc:
            eng.count += 1
            ins.then_inc(eng.sem, 1)
            ev = (eng, eng.count)
            eng.last_inc = True
        else:
            ev = (eng, eng.count + 1)
            eng.last_inc = False
        self._update(ev, reads, writes)
        return ins

    def dma(self, q, out, in_, slot, reads=(), writes=(), group=False, **kw):
        need = self._deps(q, reads, writes)
        if not group and slot.count > 0 and q.seen.get(slot, 0) < slot.count:
            need = [(o, v) for o, v in need if o is not slot] + [(slot, slot.count)]
        ins = self._emit_waits(q, need, lambda: q.h.dma_start(out=out, in_=in_, **kw))
        slot.count += 16
        ins.then_inc(slot.sem, 16)
        self._update((slot, slot.count), reads, writes)
        return ins

    def barrier(self):
        assert all(e.last_inc for e in self.engs)
        owners = list(self.engs) + list(self.slots.values())
        for e in self.engs:
            for o in owners:
                if o is e or o.count == 0:
                    continue
                if e.seen.get(o, 0) < o.count:
                    e.h.wait_ge(o.sem, o.count)
                    e.seen[o] = o.count


class Prog:
    def __init__(self, layers=(0, 1, 2, 3), final_norm=True, nseq=NSEQ, dbg=99):
        self.dbg = dbg
        self.layers = list(layers)
        self.final_norm = final_norm
        self.nseq = nseq
        nc = bass.Bass("TRN2", target_bir_lowering=False)
        self.nc = nc
        self.K = Kern(nc)
        self.build()

    def declare_inputs(self):
        K = self.K
        ns = self.nseq
        I = {}

        def inp(name, shape, dt=F32):
            I[name] = K.dram(name, shape, dt, kind="ExternalInput")

        inp("x", [ns, S, D])
        inp("c_ident", [128, 128])
        for l in self.layers:
            p = f"l{l}_"
            inp(p + "norm", [1, D])
            inp(p + "w_out", [D, D])
            if l % 2 == 0:
                inp(p + "w_in", [D, CONF_W])
                inp(p + "dw_wT", [128, 16 * 31])
                inp(p + "dw_b", [128, 16])
                inp(p + "ln_g", [128, 16])
                inp(p + "ln_b", [128, 16])
            else:
                inp(p + "w_in", [D, NSA_W])
                inp(p + "posT", [128, 32])
                inp(p + "ck_w1", [4096, 128])
                inp(p + "ck_w2", [128, 128])
                inp(p + "cv_w1", [4096, 128])
                inp(p + "cv_w2", [128, 128])
        if any(l % 2 == 1 for l in self.layers):
            inp("rb_A", [128, 16, 128])
            inp("rb_B", [128, 16, 128])
            inp("rb_C", [248, 16, 128])
            inp("rb_far", [1, 16])
            inp("m_B", [128, 128])
            inp("m_C", [248, 128])
            inp("m_W0", [128, 128])
            inp("c_E", [128, 64 * 128])
            inp("c_keep", [128, 16 * 32])
            inp("c_add", [128, 16 * 32])
            inp("c_ov", [128, 34])
        inp("final_norm", [1, D])
        self.I = I
        self.out = K.dram("y", [ns, S, D], F32, kind="ExternalOutput")

    def build(self):
        K = self.K
        nc = self.nc
        ns = self.nseq
        self.declare_inputs()
        I = self.I
        with K.stack:
            st = K.stack
            self.ident_f = K.sb(st, "ident_f", [128, 128], F32)
            self.ident_b = K.sb(st, "ident_b", [128, 128], BF16)
            self.ones_f = K.sb(st, "ones_f", [128, 128], F32)
            self.gbc = K.sb(st, "gbc", [128, D], F32)
            self.gfin = K.sb(st, "gfin", [128, D], F32)
            self.small = K.sb(st, "small", [128, 64], F32)
            pt = st.enter_context(nc.psum_tensor("psum", [128, 8, 512], F32))
            self.pt = pt
            self.P = [Buf(pt[:, b, :], f"ps{b}") for b in range(8)]
            for b_ in self.P:
                b_.excl = True
            sl = K.slot("const")
            K.dma(K.sp, self.ident_f[:], I["c_ident"], sl, writes=[self.ident_f])
            K.op(K.act, lambda: nc.scalar.copy(out=self.ident_b[:], in_=self.ident_f[:]),
                 reads=[self.ident_f], writes=[self.ident_b])
            K.op(K.pool, lambda: nc.gpsimd.memset(self.ones_f[:], 1.0), writes=[self.ones_f])
            K.dma(K.sp, self.gfin[:], bass.AP(I["final_norm"].tensor, 0, [[0, 128], [1, D]]), sl,
                  writes=[self.gfin])
            K.op(K.pool, lambda: nc.gpsimd.memset(self.small[:, 0:1], EPS), writes=[self.small])
            nl = len(self.layers)
            self.X = [I["x"]]
            for i in range(nl - 1):
                self.X.append(K.dram(f"xs{i}", [ns, S, D], F32))
            self.X.append(self.out)
            self.Xb = [[[Buf(None, f"X{i}_{s}_{t}") for t in range(NT)] for s in range(ns)]
                       for i in range(nl + 1)]
            self.WT = K.dram("wt_s", [ns, NT, 128, 16, 128], BF16)
            self.WTb = [Buf(None, f"WT{s}") for s in range(ns)]
            self.Ys = K.dram("y_s", [ns, 16, 128, S], F32)
            self.Yb = [[Buf(None, f"Y{s}_{c}") for c in range(16)] for s in range(ns)]
            if any(l % 2 == 1 for l in self.layers):
                self.QT = K.dram("qt_s", [ns, NT, 128, 16, 128], BF16)
                self.SZ = K.dram("sz_s", [ns, NT, 128, 16, 128], BF16)
                self.KT = K.dram("kt_s", [ns, 4, 4, 128, S], BF16)
                self.VA = K.dram("va_s", [ns, 2, NT, 128, 520], BF16)
                self.TA = K.dram("ta_s", [128, 16, 128], F32)
                self.TB = K.dram("tb_s", [128, 16, 128], F32)
                self.TC = K.dram("tc_s", [248, 16, 128], F32)
                self.QTb = [Buf(None, f"QT{s}") for s in range(ns)]
                self.SZb = [Buf(None, f"SZ{s}") for s in range(ns)]
                self.KTb = [Buf(None, f"KT{s}") for s in range(ns)]
                self.VAb = [Buf(None, f"VA{s}") for s in range(ns)]
                self.Tb = Buf(None, "Ttab")
                self.nsa_setup()
            K.barrier()
            for li, l in enumerate(self.layers):
                last = (li == nl - 1) and self.final_norm
                sl = K.slot("const")
                K.dma(K.sp, self.gbc[:], bass.AP(I[f"l{l}_norm"].tensor, 0, [[0, 128], [1, D]]), sl,
                      writes=[self.gbc])
                if l % 2 == 0:
                    self.conformer_layer(li, l)
                else:
                    self.nsa_layer(li, l)
                self.g2_phase(li, l, last)
            K.barrier()

    def p1_phase(self, st, li, s, hT):
        K, nc = self.K, self.nc
        xr = [K.sb(st, f"p1x{i}", [128, D], F32) for i in range(3)]
        hb = [K.sb(st, f"p1h{i}", [128, D], BF16) for i in range(2)]
        junk = K.sb(st, "p1junk", [128, D], BF16)
        stat = [K.sb(st, f"p1s{i}", [128, 4], F32) for i in range(2)]
        X = self.X[li]
        xsl = [K.slot("x0"), K.slot("x1"), K.slot("t0")]
        hTw = [Buf(None, f"hTw{i}") for i in range(2 * NT)]

        def load(tt):
            K.dma(K.sp, xr[tt % 3][:], X[s, tt * 128:(tt + 1) * 128, :], xsl[tt % 3],
                  reads=[self.Xb[li][s][tt]], writes=[xr[tt % 3]])

        def stage_a(tt):
            x, h, sm = xr[tt % 3], hb[tt % 2], stat[tt % 2]
            K.op(K.act, lambda: nc.scalar.activation(out=junk[:], in_=x[:], func=AF.Square,
                                                     accum_out=sm[:, 0:1]),
                 reads=[x], writes=[junk, sm])
            K.op(K.act, lambda: nc.scalar.activation(out=sm[:, 1:2], in_=sm[:, 0:1], func=AF.Sqrt,
                                                     scale=1.0 / D, bias=self.eps_ap()),
                 reads=[sm, self.small], writes=[sm])
            K.op(K.dve, lambda: nc.vector.reciprocal(out=sm[:, 2:3], in_=sm[:, 1:2]),
                 reads=[sm], writes=[sm])
            K.op(K.dve, lambda: nc.vector.scalar_tensor_tensor(
                out=h[:], in0=x[:], scalar=sm[:, 2:3], in1=self.gbc[:], op0=ALU.mult, op1=ALU.mult),
                reads=[x, sm, self.gbc], writes=[h])

        def stage_b(tt):
            h = hb[tt % 2]
            pb = (tt % 2) * 2
            pv = self.pt[:, pb:pb + 2, :].bitcast(BF16)
            for c in range(16):
                bank = self.P[pb + c // 8]
                o = pv[:, c // 8, (c % 8) * 128:(c % 8 + 1) * 128]
                K.op(K.pe, lambda: nc.tensor.transpose(out=o, in_=h[:, c * 128:(c + 1) * 128],
                                                       identity=self.ident_b[:]),
                     reads=[h, self.ident_b], writes=[bank], inc=(c % 8 == 7))
            for half in range(2):
                src = pv[:, half, :].rearrange("p (c t) -> p c t", c=8)
                dst = hT[:, half * 8:(half + 1) * 8, tt * 128:(tt + 1) * 128]
                if half == 0:
                    K.op(K.act, lambda: nc.scalar.copy(out=dst, in_=src), reads=[self.P[pb]],
                         writes=[hTw[2 * tt]])
                else:
                    K.op(K.dve, lambda: nc.vector.tensor_copy(out=dst, in_=src), reads=[self.P[pb + 1]],
                         writes=[hTw[2 * tt + 1]])

        load(0)
        load(1)
        stage_a(0)
        for tt in range(NT):
            if tt + 2 < NT:
                load(tt + 2)
            if tt + 1 < NT:
                stage_a(tt + 1)
            stage_b(tt)

    def eps_ap(self):
        return self.small[:, 0:1]

    def load_w_cast(self, dst_buf, dst_ap, w_ap, col0, ncols, slot, group=False):
        K = self.K
        src = w_ap.rearrange("(k p) n -> p k n", p=128)[:, :, col0:col0 + ncols]
        K.dma(K.pool, dst_ap, src, slot, writes=[dst_buf], group=group)

    def g2_phase(self, li, l, last):
        K, nc = self.K, self.nc
        I = self.I
        with ExitStack() as st:
            wo = K.sb(st, "g2w", [128, 16, D], BF16)
            wsl = K.slot("wo")
            for j in range(4):
                self.load_w_cast(wo, wo[:, :, j * 512:(j + 1) * 512], I[f"l{l}_w_out"], j * 512, 512, wsl,
                                 group=(j > 0))
            xr = [K.sb(st, f"g2x{i}", [128, D], F32) for i in range(2)]
            xo = [K.sb(st, f"g2o{i}", [128, D], F32) for i in range(2)]
            wt = [K.sb(st, f"g2t{i}", [128, 16, 128], BF16) for i in range(2)]
            junk = K.sb(st, "g2junk", [128, D], BF16)
            stat = [K.sb(st, f"g2s{i}", [128, 4], F32) for i in range(2)]
            xsl = [K.slot("x0"), K.slot("x1")]
            tsl = [K.slot("t0"), K.slot("t1")]
            osl = [K.slot("o0"), K.slot("o1")]
            Xi, Xo = self.X[li], self.X[li + 1]
            steps = [(s, tt) for s in range(self.nseq) for tt in range(NT)]

            def load(i):
                s, tt = steps[i]
                K.dma(K.sp, xr[i % 2][:], Xi[s, tt * 128:(tt + 1) * 128, :], xsl[i % 2],
                      reads=[self.Xb[li][s][tt]], writes=[xr[i % 2]])
                K.dma(K.sp, wt[i % 2][:], self.WT[s, tt], tsl[i % 2], reads=[self.WTb[s]], writes=[wt[i % 2]])

            load(0)
            for i, (s, tt) in enumerate(steps):
                if i + 1 < len(steps):
                    load(i + 1)
                x, o, w = xr[i % 2], xo[i % 2], wt[i % 2]
                for db in range(4):
                    bank = self.P[(i * 4 + db) % 2]
                    for c in range(16):
                        K.op(K.pe, lambda: nc.tensor.matmul(bank[:], lhsT=w[:, c, :],
                                                            rhs=wo[:, c, db * 512:(db + 1) * 512],
                                                            start=(c == 0), stop=(c == 15)),
                             reads=[w, wo], writes=[bank], inc=(c == 15))
                    K.op(K.dve, lambda: nc.vector.tensor_tensor(out=o[:, db * 512:(db + 1) * 512], in0=bank[:],
                                                                in1=x[:, db * 512:(db + 1) * 512], op=ALU.add),
                         reads=[bank, x], writes=[o])
                if last:
                    sm = stat[i % 2]
                    K.op(K.act, lambda: nc.scalar.activation(out=junk[:], in_=o[:], func=AF.Square,
                                                             accum_out=sm[:, 0:1]),
                         reads=[o], writes=[junk, sm])
                    K.op(K.act, lambda: nc.scalar.activation(out=sm[:, 1:2], in_=sm[:, 0:1], func=AF.Sqrt,
                                                             scale=1.0 / D, bias=self.eps_ap()),
                         reads=[sm, self.small], writes=[sm])
                    K.op(K.dve, lambda: nc.vector.reciprocal(out=sm[:, 2:3], in_=sm[:, 1:2]),
                         reads=[sm], writes=[sm])
                    K.op(K.dve, lambda: nc.vector.scalar_tensor_tensor(
                        out=o[:], in0=o[:], scalar=sm[:, 2:3], in1=self.gfin[:], op0=ALU.mult, op1=ALU.mult),
                        reads=[o, sm, self.gfin], writes=[o])
                K.dma(K.sp, Xo[s, tt * 128:(tt + 1) * 128, :], o[:], osl[i % 2], reads=[o],
                      writes=[self.Xb[li + 1][s][tt]])
            K.barrier()

    def conformer_layer(self, li, l):
        K, nc = self.K, self.nc
        I = self.I
        p = f"l{l}_"
        W = I[p + "w_in"]
        for s in range(self.nseq):
            with ExitStack() as st:
                hT = K.sb(st, "hT", [128, 16, S], BF16)
                with ExitStack() as st1:
                    self.p1_phase(st1, li, s, hT)
                    K.barrier()
                dwT = K.sb(st, "dwT", [128, 16, 31], F32)
                dwb = K.sb(st, "dwb", [128, 16], F32)
                lng = K.sb(st, "lng", [128, 16], F32)
                lnb = K.sb(st, "lnb", [128, 16], F32)
                csl = K.slot("const")
                K.dma(K.sp, dwT[:], I[p + "dw_wT"].rearrange("p (c k) -> p c k", c=16), csl, writes=[dwT])
                K.dma(K.sp, dwb[:], I[p + "dw_b"], csl, writes=[dwb])
                K.dma(K.sp, lng[:], I[p + "ln_g"], csl, writes=[lng])
                K.dma(K.sp, lnb[:], I[p + "ln_b"], csl, writes=[lnb])
                S1 = K.sb(st, "S1", [128, S], F32)
                S2 = K.sb(st, "S2", [128, S], F32)
                wa = [K.sb(st, f"wa{i}", [128, 16, 128], BF16) for i in range(2)]
                wb = [K.sb(st, f"wb{i}", [128, 16, 128], BF16) for i in range(2)]
                wasl = [K.slot("wa0"), K.slot("wa1")]
                wbsl = [K.slot("wb0"), K.slot("wb1")]
                with ExitStack() as st2:
                    v = [K.sb(st2, f"v{i}", [128, 30 + S], BF16) for i in range(2)]
                    dg = [K.sb(st2, f"dg{i}", [128, 31, 128], BF16) for i in range(2)]
                    sg = [K.sb(st2, f"sg{i}", [128, 512], F32) for i in range(2)]
                    ys = [K.sb(st2, f"ys{i}", [128, 512], F32) for i in range(2)]
                    yq = [K.sb(st2, f"yq{i}", [128, 512], F32) for i in range(2)]
                    ysl = [K.slot("ys0"), K.slot("ys1")]
                    for i in range(2):
                        K.op(K.pool, lambda: nc.gpsimd.memset(v[i][:, 0:30], 0.0), writes=[v[i]])

                    def loadw(c):
                        self.load_w_cast(wa[c % 2], wa[c % 2][:], W, c * 128, 128, wasl[c % 2])
                        self.load_w_cast(wb[c % 2], wb[c % 2][:], W, D + c * 128, 128, wbsl[c % 2])

                    def gemm_glu(c, tb):
                        j = c * 4 + tb
                        pa, pb = self.P[(j % 2) * 3], self.P[(j % 2) * 3 + 1]
                        for (ps_, w_) in ((pa, wa[c % 2]), (pb, wb[c % 2])):
                            for k in range(16):
                                K.op(K.pe, lambda: nc.tensor.matmul(ps_[:], lhsT=w_[:, k, :],
                                                                    rhs=hT[:, k, tb * 512:(tb + 1) * 512],
                                                                    start=(k == 0), stop=(k == 15)),
                                     reads=[w_, hT], writes=[ps_], inc=(k == 15))
                        g_ = sg[j % 2]
                        K.op(K.act, lambda: nc.scalar.activation(out=g_[:], in_=pb[:], func=AF.Sigmoid),
                             reads=[pb], writes=[g_])
                        vv = v[c % 2]
                        K.op(K.dve, lambda: nc.vector.tensor_tensor(
                            out=vv[:, 30 + tb * 512:30 + (tb + 1) * 512], in0=pa[:], in1=g_[:], op=ALU.mult),
                            reads=[pa, g_], writes=[vv])

                    KD = 8

                    def conv(c, tb):
                        j = c * 4 + tb
                        py = self.P[(j % 2) * 3 + 2]
                        vv, dd = v[c % 2], dg[c % 2]
                        for k in range(KD, 31):
                            K.op(K.pe, lambda: nc.tensor.matmul(py[:], lhsT=dd[:, k, :],
                                                                rhs=vv[:, k + tb * 512:k + (tb + 1) * 512],
                                                                start=(k == KD), stop=(k == 30)),
                                 reads=[dd, vv], writes=[py], inc=(k == 30))
                        y_, q_ = ys[j % 2], yq[j % 2]
                        K.op(K.dve, lambda: nc.vector.tensor_scalar(
                            out=y_[:], in0=vv[:, tb * 512:(tb + 1) * 512], scalar1=dwT[:, c, 0:1],
                            scalar2=dwb[:, c:c + 1], op0=ALU.mult, op1=ALU.add),
                            reads=[vv, dwT, dwb], writes=[y_])
                        for k in range(1, KD):
                            K.op(K.dve, lambda: nc.vector.scalar_tensor_tensor(
                                out=y_[:], in0=vv[:, k + tb * 512:k + (tb + 1) * 512], scalar=dwT[:, c, k:k + 1],
                                in1=y_[:], op0=ALU.mult, op1=ALU.add),
                                reads=[vv, dwT, y_], writes=[y_], inc=(k == KD - 1))
                        K.op(K.dve, lambda: nc.vector.tensor_tensor(out=y_[:], in0=py[:], in1=y_[:], op=ALU.add),
                             reads=[py, y_], writes=[y_])
                        K.op(K.act, lambda: nc.scalar.activation(out=q_[:], in_=y_[:], func=AF.Square),
                             reads=[y_], writes=[q_])
                        sl_ = slice(tb * 512, (tb + 1) * 512)
                        if c == 0:
                            K.op(K.pool, lambda: nc.gpsimd.tensor_copy(out=S1[:, sl_], in_=y_[:]),
                                 reads=[y_], writes=[S1])
                            K.op(K.pool, lambda: nc.gpsimd.tensor_copy(out=S2[:, sl_], in_=q_[:]),
                                 reads=[q_], writes=[S2])
                        else:
                            K.op(K.pool, lambda: nc.gpsimd.tensor_tensor(out=S1[:, sl_], in0=S1[:, sl_],
                                                                         in1=y_[:], op=ALU.add),
                                 reads=[y_, S1], writes=[S1])
                            K.op(K.pool, lambda: nc.gpsimd.tensor_tensor(out=S2[:, sl_], in0=S2[:, sl_],
                                                                         in1=q_[:], op=ALU.add),
                                 reads=[q_, S2], writes=[S2])
                        K.dma(K.sp, self.Ys[s, c, :, sl_], y_[:], ysl[j % 2], reads=[y_], writes=[self.Yb[s][c]])

                    def mkdiag(c):
                        dd = dg[c % 2]
                        K.op(K.pool, lambda: nc.gpsimd.tensor_tensor(
                            out=dd[:],
                            in0=self.ident_b[:].unsqueeze(1).broadcast_to([128, 31, 128]),
                            in1=dwT[:, c, :].unsqueeze(2).broadcast_to([128, 31, 128]),
                            op=ALU.mult),
                            reads=[self.ident_b, dwT], writes=[dd])

                    steps = [(c, tb) for c in range(16) for tb in range(4)]
                    loadw(0)
                    mkdiag(0)
                    gemm_glu(0, 0)
                    for i, (c, tb) in enumerate(steps):
                        if tb == 0 and c + 1 < 16:
                            loadw(c + 1)
                            mkdiag(c + 1)
                        if i + 1 < len(steps):
                            gemm_glu(*steps[i + 1])
                        conv(c, tb)
                    for tb in range(4):
                        sl_ = slice(tb * 512, (tb + 1) * 512)
                        p1, p2 = self.P[6], self.P[7]
                        K.op(K.pe, lambda: nc.tensor.matmul(p1[:], lhsT=self.ones_f[:], rhs=S1[:, sl_],
                                                            start=True, stop=True),
                             reads=[self.ones_f, S1], writes=[p1])
                        K.op(K.pe, lambda: nc.tensor.matmul(p2[:], lhsT=self.ones_f[:], rhs=S2[:, sl_],
                                                            start=True, stop=True),
                             reads=[self.ones_f, S2], writes=[p2])
                        K.op(K.act, lambda: nc.scalar.mul(out=S1[:, sl_], in_=p1[:], mul=1.0 / D),
                             reads=[p1], writes=[S1])
                        t_ = sg[0]
                        K.op(K.dve, lambda: nc.vector.tensor_tensor(out=t_[:], in0=S1[:, sl_], in1=S1[:, sl_],
                                                                    op=ALU.mult),
                             reads=[S1], writes=[t_])
                        K.op(K.dve, lambda: nc.vector.scalar_tensor_tensor(
                            out=t_[:], in0=p2[:], scalar=1.0 / D, in1=t_[:], op0=ALU.mult, op1=ALU.subtract),
                            reads=[p2, t_], writes=[t_])
                        K.op(K.act, lambda: nc.scalar.activation(out=t_[:], in_=t_[:], func=AF.Sqrt,
                                                                 bias=self.eps_ap()),
                             reads=[t_, self.small], writes=[t_])
                        K.op(K.dve, lambda: nc.vector.reciprocal(out=S2[:, sl_], in_=t_[:]),
                             reads=[t_], writes=[S2])
                    K.barrier()
                with ExitStack() as st3:
                    yb = [K.sb(st3, f"yb{i}", [128, S], F32) for i in range(2)]
                    wTb = [K.sb(st3, f"wTb{i}", [128, S], BF16) for i in range(2)]
                    szb = [K.sb(st3, f"szb{i}", [128, 512], F32) for i in range(2)]
                    s1b = [K.sb(st3, f"s1b{i}", [128, 512], F32) for i in range(2)]
                    ybsl = [K.slot("yb0"), K.slot("yb1")]
                    wtsl = [K.slot("wt0"), K.slot("wt1")]

                    def load3(c):
                        self.load_w_cast(wa[c % 2], wa[c % 2][:], W, 2 * D + c * 128, 128, wasl[c % 2])
                        K.dma(K.sp, yb[c % 2][:], self.Ys[s, c], ybsl[c % 2], reads=[self.Yb[s][c]],
                              writes=[yb[c % 2]])

                    load3(0)
                    for c in range(16):
                        if c + 1 < 16:
                            load3(c + 1)
                        y_, w_ = yb[c % 2], wTb[c % 2]
                        for tb in range(4):
                            j = c * 4 + tb
                            sl_ = slice(tb * 512, (tb + 1) * 512)
                            pz = self.P[j % 2]
                            for k in range(16):
                                K.op(K.pe, lambda: nc.tensor.matmul(pz[:], lhsT=wa[c % 2][:, k, :], rhs=hT[:, k, sl_],
                                                                    start=(k == 0), stop=(k == 15)),
                                     reads=[wa[c % 2], hT], writes=[pz], inc=(k == 15))
                            z_, a_ = szb[j % 2], s1b[j % 2]
                            K.op(K.act, lambda: nc.scalar.activation(out=z_[:], in_=pz[:], func=AF.Silu),
                                 reads=[pz], writes=[z_])
                            K.op(K.dve, lambda: nc.vector.tensor_tensor(out=y_[:, sl_], in0=y_[:, sl_],
                                                                        in1=S1[:, sl_], op=ALU.subtract),
                                 reads=[y_, S1], writes=[y_])
                            K.op(K.dve, lambda: nc.vector.tensor_tensor(out=y_[:, sl_], in0=y_[:, sl_],
                                                                        in1=S2[:, sl_], op=ALU.mult),
                                 reads=[y_, S2], writes=[y_])
                            K.op(K.act, lambda: nc.scalar.activation(out=a_[:], in_=y_[:, sl_], func=AF.Silu,
                                                                     scale=lng[:, c:c + 1], bias=lnb[:, c:c + 1]),
                                 reads=[y_, lng, lnb], writes=[a_])
                            K.op(K.dve, lambda: nc.vector.tensor_tensor(out=w_[:, sl_], in0=a_[:], in1=z_[:],
                                                                        op=ALU.mult),
                                 reads=[a_, z_], writes=[w_])
                        dst = self.WT[s].rearrange("t p c j -> p t c j")[:, :, c, :]
                        K.dma(K.sp, dst, w_[:].rearrange("p (t j) -> p t j", t=NT), wtsl[c % 2], reads=[w_],
                              writes=[self.WTb[s]])
                    K.barrier()

    def nsa_setup(self):
        K, nc, I = self.K, self.nc, self.I
        with ExitStack() as st:
            far = K.sb(st, "far", [128, 16], F32)
            sl = K.slot("const")
            K.dma(K.sp, far[:], bass.AP(I["rb_far"].tensor, 0, [[0, 128], [1, 16]]), sl, writes=[far])
            jobs = [("rb_A", None, self.TA, 0, 128), ("rb_B", "m_B", self.TB, 0, 128),
                    ("rb_C", "m_C", self.TC, 0, 128), ("rb_C", "m_C", self.TC, 128, 120)]
            for ji, (rn, mn, dst, r0, nr) in enumerate(jobs):
                raw = K.sb(st, f"raw{ji}", [128, 16, 128], F32)
                K.dma(K.sp, raw[0:nr], I[rn][r0:r0 + nr], K.slot(f"x{ji % 2}"), writes=[raw])
                K.op(K.dve, lambda: nc.vector.tensor_tensor(
                    out=raw[0:nr], in0=raw[0:nr], in1=far[0:nr].unsqueeze(2).broadcast_to([nr, 16, 128]),
                    op=ALU.subtract), reads=[raw, far], writes=[raw])
                if mn is not None:
                    mk = K.sb(st, f"mk{ji}", [128, 128], F32)
                    K.dma(K.sp, mk[0:nr], I[mn][r0:r0 + nr], K.slot(f"t{ji % 2}"), writes=[mk])
                    K.op(K.dve, lambda: nc.vector.tensor_tensor(
                        out=raw[0:nr], in0=raw[0:nr], in1=mk[0:nr].unsqueeze(1).broadcast_to([nr, 16, 128]),
                        op=ALU.add), reads=[raw, mk], writes=[raw])
                K.dma(K.sp, dst[r0:r0 + nr], raw[0:nr], K.slot(f"o{ji % 2}"), reads=[raw], writes=[self.Tb])
            K.barrier()

    def nsa_layer(self, li, l):
        K, nc, I = self.K, self.nc, self.I
        p = f"l{l}_"
        W = I[p + "w_in"]
        for s in range(self.nseq):
            with ExitStack() as so:
                gates = K.sb(so, "gates", [128, NT, 48], F32)
                with ExitStack() as st:
                    hT = K.sb(st, "hT", [128, 16, S], BF16)
                    with ExitStack() as st1:
                        self.p1_phase(st1, li, s, hT)
                        K.barrier()
                    if self.dbg >= 2:
                        self.n2_phase(st, s, W, hT, gates)
                    K.barrier()
                with ExitStack() as st:
                    KcT = K.sb(st, "KcT", [128, 4, 128], BF16)
                    VcO = K.sb(st, "VcO", [128, 4, 162], BF16)
                    res = self.n4_residents(st, s)
                    with ExitStack() as st3:
                        if self.dbg >= 3:
                            self.n3_phase(st3, s, p, KcT, VcO)
                        K.barrier()
                    if self.dbg >= 4:
                        self.n4_phase(st, s, KcT, VcO, gates, res)
                    K.barrier()

    def n2_phase(self, st, s, W, hT, gates):
        K, nc = self.K, self.nc
        wst = [K.sb(st, f"wst{i}", [128, 16, 128], BF16) for i in range(3)]
        wsl = [K.slot(f"wa{i}") for i in range(2)] + [K.slot("wb0")]
        stg = [K.sb(st, f"stg{i}", [128, S], BF16) for i in range(2)]
        gsl = [K.slot("ys0"), K.slot("ys1")]
        chunks = []
        for h in range(16):
            chunks.append((h * 128, "q", h))
        for wi, base in enumerate((2048, 3072, 4096, 2560)):
            for g in range(4):
                chunks.append((base + g * 128, "k", (wi, g)))
        for c in range(16):
            chunks.append((5168 + c * 128, "z", c))

        def loadw(i):
            self.load_w_cast(wst[i % 3], wst[i % 3][:], W, chunks[i][0], 128, wsl[i % 3])

        loadw(0)
        loadw(1)
        j = 0
        for i, (col0, kind, idx) in enumerate(chunks):
            if i + 2 < len(chunks):
                loadw(i + 2)
            w_ = wst[i % 3]
            g_ = stg[i % 2]
            for tb in range(4):
                ps_ = self.P[j % 4]
                j += 1
                sl_ = slice(tb * 512, (tb + 1) * 512)
                for k in range(16):
                    K.op(K.pe, lambda: nc.tensor.matmul(ps_[:], lhsT=w_[:, k, :], rhs=hT[:, k, sl_],
                                                        start=(k == 0), stop=(k == 15)),
                         reads=[w_, hT], writes=[ps_], inc=(k == 15))
                if kind == "q":
                    K.op(K.act, lambda: nc.scalar.mul(out=g_[:, sl_], in_=ps_[:], mul=float(DK) ** -0.5),
                         reads=[ps_], writes=[g_])
                elif kind == "z":
                    K.op(K.act, lambda: nc.scalar.activation(out=g_[:, sl_], in_=ps_[:], func=AF.Silu),
                         reads=[ps_], writes=[g_])
                else:
                    K.op(K.dve, lambda: nc.vector.tensor_copy(out=g_[:, sl_], in_=ps_[:]),
                         reads=[ps_], writes=[g_])
            if kind == "q":
                dst = self.QT[s].rearrange("t p h j -> p t h j")[:, :, idx, :]
                K.dma(K.sp, dst, g_[:].rearrange("p (t j) -> p t j", t=NT), gsl[i % 2], reads=[g_],
                      writes=[self.QTb[s]])
            elif kind == "z":
                dst = self.SZ[s].rearrange("t p h j -> p t h j")[:, :, idx, :]
                K.dma(K.sp, dst, g_[:].rearrange("p (t j) -> p t j", t=NT), gsl[i % 2], reads=[g_],
                      writes=[self.SZb[s]])
            else:
                wi, g = idx
                K.dma(K.sp, self.KT[s, wi, g], g_[:], gsl[i % 2], reads=[g_], writes=[self.KTb[s]])
        wmv = [K.sb(st, f"wmv{i}", [128, 16, 512], BF16) for i in range(2)]
        wg = K.sb(st, "wg", [128, 16, 48], BF16)
        vst = [K.sb(st, f"vst{i}", [128, 4, 130], BF16) for i in range(2)]
        msl = [K.slot("wb1"), K.slot("wo")]
        vsl = [K.slot("yb0"), K.slot("yb1")]
        for i in range(2):
            K.op(K.pool, lambda: nc.gpsimd.memset(vst[i][:, :, 128:129], 1.0), writes=[vst[i]])
            K.op(K.pool, lambda: nc.gpsimd.memset(vst[i][:, :, 129:130], 0.0), writes=[vst[i]])
        for vi, base in enumerate((3584, 4608)):
            self.load_w_cast(wmv[vi], wmv[vi][:], W, base, 512, msl[vi])
        self.load_w_cast(wg, wg[:], W, 5120, 48, K.slot("const"))
        for vi in range(2):
            for tt in range(NT):
                ps_ = self.P[j % 4]
                j += 1
                for k in range(16):
                    K.op(K.pe, lambda: nc.tensor.matmul(ps_[:], lhsT=hT[:, k, tt * 128:(tt + 1) * 128],
                                                        rhs=wmv[vi][:, k, :], start=(k == 0), stop=(k == 15)),
                         reads=[wmv[vi], hT], writes=[ps_], inc=(k == 15))
                v_ = vst[tt % 2]
                eng = K.act if tt % 2 == 0 else K.dve
                src = ps_[:].rearrange("p (g d) -> p g d", g=4)
                if tt % 2 == 0:
                    K.op(K.act, lambda: nc.scalar.copy(out=v_[:, :, 0:128], in_=src), reads=[ps_], writes=[v_])
                else:
                    K.op(K.dve, lambda: nc.vector.tensor_copy(out=v_[:, :, 0:128], in_=src), reads=[ps_],
                         writes=[v_])
                K.dma(K.sp, self.VA[s, vi, tt], v_[:].rearrange("p g d -> p (g d)"), vsl[tt % 2], reads=[v_],
                      writes=[self.VAb[s]])
        for tt in range(NT):
            ps_ = self.P[j % 4]
            j += 1
            for k in range(16):
                K.op(K.pe, lambda: nc.tensor.matmul(ps_[:, 0:48], lhsT=hT[:, k, tt * 128:(tt + 1) * 128],
                                                    rhs=wg[:, k, :], start=(k == 0), stop=(k == 15)),
                     reads=[wg, hT], writes=[ps_], inc=(k == 15))
            K.op(K.act, lambda: nc.scalar.activation(out=gates[:, tt, :], in_=ps_[:, 0:48], func=AF.Sigmoid),
                 reads=[ps_], writes=[gates])

    def n3_phase(self, st, s, p, KcT, VcO):
        K, nc, I = self.K, self.nc, self.I
        srcT = [K.sb(st, f"cT{i}", [128, 4, S], BF16) for i in range(2)]
        w1 = [K.sb(st, f"w1{i}", [128, 32, 128], BF16) for i in range(2)]
        w2 = [K.sb(st, f"w2{i}", [128, 128], BF16) for i in range(2)]
        HT = [K.sb(st, f"HT{i}", [128, 4, 127], BF16) for i in range(2)]
        posb = K.sb(st, "posb", [128, 32], BF16)
        cvec = K.sb(st, "cvec", [128, 2], F32)
        csl = K.slot("const")
        K.dma(K.sp, srcT[0][:], self.KT[s, 0].rearrange("g p t -> p g t"), K.slot("x0"), reads=[self.KTb[s]],
              writes=[srcT[0]])
        K.dma(K.sp, srcT[1][:], self.KT[s, 3].rearrange("g p t -> p g t"), K.slot("x1"), reads=[self.KTb[s]],
              writes=[srcT[1]])
        for i, nm in enumerate(("ck", "cv")):
            K.dma(K.pool, w1[i][:], I[p + nm + "_w1"].rearrange("(l d) j -> d l j", d=128), K.slot(f"wa{i}"),
                  writes=[w1[i]])
            K.dma(K.pool, w2[i][:], I[p + nm + "_w2"], K.slot(f"wb{i}"), writes=[w2[i]])
        K.dma(K.pool, posb[:], I[p + "posT"], csl, writes=[posb])
        K.op(K.pool, lambda: nc.gpsimd.memset(KcT[:], 0.0), writes=[KcT])
        K.op(K.pool, lambda: nc.gpsimd.memset(VcO[:], 0.0), writes=[VcO])
        for g in range(4):
            K.dma(K.pool, VcO[:, g, 128:162], I["c_ov"], csl, writes=[VcO])
        for i in range(2):
            pc = self.P[4 + i]
            for l_ in range(32):
                K.op(K.pe, lambda: nc.tensor.matmul(pc[:, 0:1], lhsT=w1[i][:, l_, :], rhs=posb[:, l_:l_ + 1],
                                                    start=(l_ == 0), stop=(l_ == 31)),
                     reads=[w1[i], posb], writes=[pc], inc=(l_ == 31))
            K.op(K.act, lambda: nc.scalar.copy(out=cvec[:, i:i + 1], in_=pc[:, 0:1]), reads=[pc], writes=[cvec])
            ph = self.P[i]
            for l_ in range(32):
                K.op(K.pe, lambda: nc.tensor.matmul(ph[:, 0:508].rearrange("p (g n) -> p g n", g=4),
                                                    lhsT=w1[i][:, l_, :],
                                                    rhs=srcT[i][:, :, l_:l_ + 16 * 126 + 1:16],
                                                    start=(l_ == 0), stop=(l_ == 31)),
                     reads=[w1[i], srcT[i]], writes=[ph], inc=(l_ == 31))
            K.op(K.act, lambda: nc.scalar.activation(out=HT[i][:], in_=ph[:, 0:508].rearrange("p (g n) -> p g n", g=4),
                                                     func=AF.Silu, bias=cvec[:, i:i + 1]),
                 reads=[ph, cvec], writes=[HT[i]])
        pk = self.P[2]
        K.op(K.pe, lambda: nc.tensor.matmul(pk[:, 0:508], lhsT=w2[0][:], rhs=HT[0][:].rearrange("p g n -> p (g n)"),
                                            start=True, stop=True),
             reads=[w2[0], HT[0]], writes=[pk])
        K.op(K.act, lambda: nc.scalar.copy(out=KcT[:, :, 0:127], in_=pk[:, 0:508].rearrange("p (g n) -> p g n", g=4)),
             reads=[pk], writes=[KcT])
        pv = self.P[3]
        for g in range(4):
            K.op(K.pe, lambda: nc.tensor.matmul(pv[0:127, g * 128:(g + 1) * 128], lhsT=HT[1][:, g, :], rhs=w2[1][:],
                                                start=True, stop=True),
                 reads=[w2[1], HT[1]], writes=[pv], inc=(g == 3))
        K.op(K.dve, lambda: nc.vector.tensor_copy(out=VcO[0:127, :, 0:128],
                                                  in_=pv[0:127, :].rearrange("p (g d) -> p g d", g=4)),
             reads=[pv], writes=[VcO])

    def n4_residents(self, st, s):
        K, nc, I = self.K, self.nc, self.I
        ksT = K.sb(st, "ksT", [128, 4, S], BF16)
        kwT = K.sb(st, "kwT", [128, 4, S], BF16)
        vsA = K.sb(st, "vsA", [128, NT, 520], BF16)
        vwA = K.sb(st, "vwA", [128, NT, 520], BF16)
        Eb = K.sb(st, "Eb", [128, 64, 128], BF16)
        TA = K.sb(st, "TA", [128, 16, 128], F32)
        TB = K.sb(st, "TB", [128, 16, 128], F32)
        keep = K.sb(st, "keep", [128, NT, 32], F32)
        addm = K.sb(st, "addm", [128, NT, 32], F32)
        M0 = K.sb(st, "M0", [128, 128], BF16)
        csl = K.slot("r6")
        K.dma(K.sp, ksT[:], self.KT[s, 1].rearrange("g p t -> p g t"), K.slot("r0"), reads=[self.KTb[s]], writes=[ksT])
        K.dma(K.sp, kwT[:], self.KT[s, 2].rearrange("g p t -> p g t"), K.slot("r1"), reads=[self.KTb[s]], writes=[kwT])
        K.dma(K.sp, vsA[:], self.VA[s, 0].rearrange("t p c -> p t c"), K.slot("r2"), reads=[self.VAb[s]], writes=[vsA])
        K.dma(K.sp, vwA[:], self.VA[s, 1].rearrange("t p c -> p t c"), K.slot("r3"), reads=[self.VAb[s]], writes=[vwA])
        for j in range(4):
            K.dma(K.pool, Eb[:, j * 16:(j + 1) * 16, :],
                  I["c_E"].rearrange("p (e k) -> p e k", k=128)[:, j * 16:(j + 1) * 16, :], K.slot("r4"), writes=[Eb],
                  group=(j > 0))
        K.dma(K.pool, M0[:], I["m_W0"], K.slot("r5"), writes=[M0])
        K.dma(K.sp, TA[:], self.TA, csl, reads=[self.Tb], writes=[TA])
        K.dma(K.sp, TB[:], self.TB, csl, reads=[self.Tb], writes=[TB])
        K.dma(K.sp, keep[:], I["c_keep"].rearrange("p (t s) -> p t s", s=32), csl, writes=[keep])
        K.dma(K.sp, addm[:], I["c_add"].rearrange("p (t s) -> p t s", s=32), csl, writes=[addm])
        return (ksT, kwT, vsA, vwA, Eb, TA, TB, keep, addm, M0)

    def n4_phase(self, st, s, KcT, VcO, gates, res):
        K, nc, I = self.K, self.nc, self.I
        ksT, kwT, vsA, vwA, Eb, TA, TB, keep, addm, M0 = res
        qr = [K.sb(st, f"qr{i}", [128, 16, 128], BF16) for i in range(2)]
        zr = [K.sb(st, f"zr{i}", [128, 16, 128], BF16) for i in range(2)]
        cbr = [K.sb(st, f"cbr{i}", [128, 16, 128], F32) for i in range(2)]
        NPT = 4
        pTr = [K.sb(st, f"pT{i}", [128, 512], BF16) for i in range(NPT)]
        pTc = [K.sb(st, f"pTc{i}", [128, 512], BF16) for i in range(2)]
        scr = [K.sb(st, f"sc{i}", [128, 512], F32) for i in range(2)]
        posb = [K.sb(st, f"posb{i}", [128, 2, 2, 129], F32) for i in range(4)]
        ptmp = K.sb(st, "ptmp", [128, 2, 2, 128], F32)
        otiles = [K.sb(st, f"otile{i}", [128, D], F32) for i in range(2)]
        ob = K.sb(st, "ob", [128, D], BF16)
        oT = [K.sb(st, f"oT{i}", [128, 16, 128], BF16) for i in range(1)]
        impns = [K.sb(st, f"impn{i}", [128, 4, 32], F32) for i in range(2)]
        t8s = [K.sb(st, f"t8{i}", [128, 4, 8], F32) for i in range(2)]
        selms = [K.sb(st, f"selm{i}", [128, 4, 32], F32) for i in range(2)]
        selTs = [K.sb(st, f"selT{i}", [128, 128], BF16) for i in range(2)]
        rcs = [K.sb(st, f"rc{i}", [128, 8], F32) for i in range(6)]
        qsl = [K.slot("ys0"), K.slot("ys1")]
        zsl = [K.slot("yb0"), K.slot("yb1")]
        bsl = [K.slot("wb0"), K.slot("wb1")]
        osl = [K.slot("o0"), K.slot("o1")]
        SCB = [self.P[0], self.P[1], self.P[7]]
        Pm = self.P[6]
        pm = self.pt[:, 6, :]
        po_sel = self.pt[:, 2:4, :]
        po_win = self.pt[:, 4:6, :]
        Psel = [self.P[2], self.P[3]]
        Pwin = [self.P[4], self.P[5]]

        def load_qc(tt):
            K.dma(K.sp, qr[tt % 2][:], self.QT[s, tt], qsl[tt % 2], reads=[self.QTb[s]], writes=[qr[tt % 2]])
            K.dma(K.sp, cbr[tt % 2][:], self.TC[120 - 8 * tt:248 - 8 * tt], bsl[tt % 2], reads=[self.Tb],
                  writes=[cbr[tt % 2]])

        def load_z(tt):
            K.dma(K.sp, zr[tt % 2][:], self.SZ[s, tt], zsl[tt % 2], reads=[self.SZb[s]], writes=[zr[tt % 2]])

        cnt = {"u": 0, "rc": 0, "pb": 0}

        def cmp_half(g, half, T):
            otile = otiles[T % 2]
            impn = impns[T % 2]
            rc = rcs[cnt["rc"] % 6]
            cnt["rc"] += 1
            K.op(K.dve, lambda: nc.vector.tensor_scalar(out=rc[:, 0:2], in0=pm[:, 128:291:162], scalar1=1e-30,
                                                        scalar2=None, op0=ALU.max),
                 reads=[Pm], writes=[rc])
            K.op(K.dve, lambda: nc.vector.reciprocal(out=rc[:, 0:2], in_=rc[:, 0:2]), reads=[rc], writes=[rc])
            h0 = 4 * g + 2 * half
            K.op(K.dve, lambda: nc.vector.tensor_tensor(out=rc[:, 4:6], in0=rc[:, 0:2],
                                                        in1=gates[:, T, h0 * 3:h0 * 3 + 4:3], op=ALU.mult),
                 reads=[rc, gates], writes=[rc])
            for q2 in range(2):
                h = h0 + q2
                K.op(K.act, lambda: nc.scalar.activation(out=otile[:, h * 128:(h + 1) * 128],
                                                         in_=pm[:, q2 * 162:q2 * 162 + 128], func=AF.Copy,
                                                         scale=rc[:, 4 + q2:5 + q2]),
                     reads=[Pm, rc], writes=[otile])
            for q2 in range(2):
                src = pm[:, q2 * 162 + 130:q2 * 162 + 162]
                if half == 0 and q2 == 0:
                    K.op(K.dve, lambda: nc.vector.tensor_scalar(out=impn[:, g, :], in0=src, scalar1=rc[:, 0:1],
                                                                scalar2=None, op0=ALU.mult),
                         reads=[Pm, rc], writes=[impn])
                else:
                    K.op(K.dve, lambda: nc.vector.scalar_tensor_tensor(
                        out=impn[:, g, :], in0=src, scalar=rc[:, q2:q2 + 1], in1=impn[:, g, :],
                        op0=ALU.mult, op1=ALU.add), reads=[Pm, rc, impn], writes=[impn])

        def fin_branch(po_view, Pb, g, T, branch):
            otile = otiles[T % 2]
            pb_ = posb[cnt["pb"] % 4]
            cnt["pb"] += 1
            rc = rcs[cnt["rc"] % 6]
            cnt["rc"] += 1
            K.op(K.dve, lambda: nc.vector.tensor_copy(
                out=pb_[:], in_=po_view[:, :, 0:258].rearrange("p a (b c) -> p a b c", c=129)),
                reads=Pb, writes=[pb_])
            K.op(K.dve, lambda: nc.vector.reciprocal(out=rc[:, 0:4].rearrange("p (a b) -> p a b", a=2),
                                                     in_=pb_[:, :, :, 128]),
                 reads=[pb_], writes=[rc])
            gcol = (4 * g) * 3 + branch
            K.op(K.dve, lambda: nc.vector.tensor_tensor(out=rc[:, 4:8], in0=rc[:, 0:4],
                                                        in1=gates[:, T, gcol:gcol + 10:3], op=ALU.mult),
                 reads=[rc, gates], writes=[rc])
            K.op(K.pool, lambda: nc.gpsimd.tensor_tensor(
                out=ptmp[:], in0=pb_[:, :, :, 0:128],
                in1=rc[:, 4:8].rearrange("p (a b) -> p a b", a=2).unsqueeze(3).broadcast_to([128, 2, 2, 128]),
                op=ALU.mult), reads=[pb_, rc], writes=[ptmp])
            og = otile[:, 4 * g * 128:(4 * g + 4) * 128].rearrange("p (a b d) -> p a b d", a=2, b=2)
            K.op(K.pool, lambda: nc.gpsimd.tensor_tensor(out=og, in0=og, in1=ptmp[:], op=ALU.add),
                 reads=[otile, ptmp], writes=[otile])

        if self.dbg < 5:
            return

        def topk_stages(T):
            impn, t8, selm, selT = impns[T % 2], t8s[T % 2], selms[T % 2], selTs[T % 2]

            def E():
                K.op(K.dve, lambda: nc.vector.tensor_tensor(out=impn[:], in0=impn[:],
                                                            in1=keep[:, T, :].unsqueeze(1).broadcast_to([128, 4, 32]),
                                                            op=ALU.mult),
                     reads=[impn, keep], writes=[impn])
                K.op(K.dve, lambda: nc.vector.tensor_tensor(out=impn[:], in0=impn[:],
                                                            in1=addm[:, T, :].unsqueeze(1).broadcast_to([128, 4, 32]),
                                                            op=ALU.add),
                     reads=[impn, addm], writes=[impn])

            def F():
                for g in range(4):
                    K.op(K.dve, lambda: nc.vector.max(out=t8[:, g, :], in_=impn[:, g, :]), reads=[impn], writes=[t8])

            def G():
                for g in range(4):
                    K.op(K.dve, lambda: nc.vector.tensor_scalar(out=selm[:, g, :], in0=impn[:, g, :],
                                                                scalar1=t8[:, g, 7:8], scalar2=-1.0,
                                                                op0=ALU.is_ge, op1=ALU.add),
                         reads=[impn, t8], writes=[selm])

            def H():
                K.op(K.pe, lambda: nc.tensor.transpose(out=pm[:, 0:128], in_=selm[:].rearrange("p g s -> p (g s)"),
                                                       identity=self.ident_f[:]),
                     reads=[selm, self.ident_f], writes=[Pm])

            def I_():
                K.op(K.act, lambda: nc.scalar.copy(out=selT[:], in_=pm[:, 0:128]), reads=[Pm], writes=[selT])

            return [E, F, G, H, I_]

        def final_stages(T):
            otile = otiles[T % 2]
            z_ = zr[T % 2]
            pv = pm.bitcast(BF16)
            o_ = oT[0]

            def cp():
                K.op(K.act, lambda: nc.scalar.copy(out=ob[:], in_=otile[:]), reads=[otile], writes=[ob])

            def tr(half):
                for c8 in range(8):
                    c = half * 8 + c8
                    K.op(K.pe, lambda: nc.tensor.transpose(out=pv[:, c8 * 128:(c8 + 1) * 128],
                                                           in_=ob[:, c * 128:(c + 1) * 128], identity=self.ident_b[:]),
                         reads=[ob, self.ident_b], writes=[Pm], inc=(c8 == 7))

            def mu(half):
                K.op(K.dve, lambda: nc.vector.tensor_tensor(
                    out=o_[:, half * 8:(half + 1) * 8, :], in0=pv[:].rearrange("p (c t) -> p c t", c=8),
                    in1=z_[:, half * 8:(half + 1) * 8, :], op=ALU.mult),
                    reads=[Pm, z_], writes=[o_])
                if half == 1:
                    K.dma(K.sp, self.WT[s, T], o_[:], osl[T % 2], reads=[o_], writes=[self.WTb[s]])

            return [cp, lambda: tr(0), lambda: mu(0), lambda: tr(1), lambda: mu(1)]

        def run_tile(T, with_units, Tc, fifo):
            allu = []
            if with_units:
                for g in range(4):
                    for kc in range(T + 1):
                        allu.append((g, "s", kc))
                    for kc in range(max(0, T - 4), T + 1):
                        allu.append((g, "w", kc))
            if Tc is not None:
                n0 = len(allu)
                start = min(n0, max(6, n0 // 4))
                gap = max(1, (n0 - start) // 5)
                for g in range(4):
                    allu.insert(min(len(allu), start + g * (gap + 1)), (g, "c", None))
            N = len(allu)
            first_seen = {}
            ustate = {}
            cdone = {"n": 0}

            def qk(n):
                g, br, kc = allu[n]
                u = cnt["u"]
                cnt["u"] += 1
                ustate[n] = u
                ps_ = SCB[u % 3]
                if br == "c":
                    q_ = qr[Tc % 2]
                    K.op(K.pe, lambda: nc.tensor.matmul(ps_[:], lhsT=KcT[:, g, :], rhs=q_[:, 4 * g:4 * g + 4, :],
                                                        start=True, stop=True),
                         reads=[KcT, q_], writes=[ps_])
                    return
                q_ = qr[T % 2]
                selT = selTs[T % 2]
                kT = ksT if br == "s" else kwT
                extra = (br == "s" and kc < T) or (br == "w" and kc == T - 4)
                K.op(K.pe, lambda: nc.tensor.matmul(ps_[:], lhsT=kT[:, g, kc * 128:(kc + 1) * 128],
                                                    rhs=q_[:, 4 * g:4 * g + 4, :], start=True, stop=not extra),
                     reads=[kT, q_], writes=[ps_], inc=not extra)
                if br == "s" and kc < T:
                    K.op(K.pe, lambda: nc.tensor.matmul(ps_[:], lhsT=Eb[:, g * 16 + kc, :],
                                                        rhs=selT[:].unsqueeze(1).broadcast_to([128, 4, 128]),
                                                        start=False, stop=True),
                         reads=[Eb, selT], writes=[ps_])
                elif br == "w" and kc == T - 4:
                    K.op(K.pe, lambda: nc.tensor.matmul(ps_[:], lhsT=self.ident_b[:],
                                                        rhs=M0[:].unsqueeze(1).broadcast_to([128, 4, 128]),
                                                        start=False, stop=True),
                         reads=[self.ident_b, M0], writes=[ps_])

            def rest_cmp(g, u):
                while len(fifo) > 4:
                    fifo.pop(0)()
                ps_, sc_, pT_ = SCB[u % 3], scr[u % 2], pTc[g % 2]
                cb_ = cbr[Tc % 2]
                K.op(K.dve, lambda: nc.vector.tensor_tensor(
                    out=sc_[:], in0=ps_[:], in1=cb_[:, 4 * g:4 * g + 4, :].rearrange("p h j -> p (h j)"),
                    op=ALU.add), reads=[ps_, cb_], writes=[sc_])
                K.op(K.act, lambda: nc.scalar.activation(out=pT_[:], in_=sc_[:], func=AF.Exp),
                     reads=[sc_], writes=[pT_])

                def pv(half):
                    for q2 in range(2):
                        r = half * 2 + q2
                        K.op(K.pe, lambda: nc.tensor.matmul(pm[:, q2 * 162:q2 * 162 + 162],
                                                            lhsT=pT_[:, r * 128:(r + 1) * 128], rhs=VcO[:, g, :],
                                                            start=True, stop=True),
                             reads=[pT_, VcO], writes=[Pm], inc=(q2 == 1))

                fifo.append(lambda: pv(0))
                fifo.append(lambda: cmp_half(g, 0, Tc))
                fifo.append(lambda: pv(1))
                fifo.append(lambda: cmp_half(g, 1, Tc))
                cdone["n"] += 1
                if cdone["n"] == 4:
                    fifo.extend(topk_stages(Tc))

            def rest(n):
                g, br, kc = allu[n]
                u = ustate.pop(n)
                if br == "c":
                    rest_cmp(g, u)
                    return
                ps_ = SCB[u % 3]
                pT_ = pTr[u % NPT]
                tab = TB if kc == T else (TA if kc == T - 1 else None)
                if tab is not None:
                    sc_ = scr[u % 2]
                    K.op(K.dve, lambda: nc.vector.tensor_tensor(
                        out=sc_[:], in0=ps_[:], in1=tab[:, 4 * g:4 * g + 4, :].rearrange("p h j -> p (h j)"),
                        op=ALU.add), reads=[ps_, tab], writes=[sc_])
                    K.op(K.act, lambda: nc.scalar.activation(out=pT_[:], in_=sc_[:], func=AF.Exp),
                         reads=[sc_], writes=[pT_])
                else:
                    K.op(K.act, lambda: nc.scalar.activation(out=pT_[:], in_=ps_[:], func=AF.Exp),
                         reads=[ps_], writes=[pT_])
                po_view, Pb, vA = (po_sel, Psel, vsA) if br == "s" else (po_win, Pwin, vwA)
                first = (g, br) not in first_seen
                first_seen[(g, br)] = True
                last = kc == T
                for r in range(4):
                    K.op(K.pe, lambda: nc.tensor.matmul(
                        po_view[:, r // 2, (r % 2) * 129:(r % 2) * 129 + 129],
                        lhsT=pT_[:, r * 128:(r + 1) * 128], rhs=vA[:, kc, g * 130:g * 130 + 129],
                        start=(first and r % 2 == 0), stop=last, skip_group_check=True),
                        reads=[pT_, vA], writes=[Pb[r // 2]], inc=(r == 3))
                if last:
                    fin_branch(po_view, Pb, g, T, 1 if br == "s" else 2)

            for n in range(N):
                if n == 0:
                    qk(0)
                    if N > 1:
                        qk(1)
                if n + 2 < N:
                    qk(n + 2)
                rest(n)
                if fifo:
                    fifo.pop(0)()
            while fifo:
                fifo.pop(0)()

        load_qc(0)
        run_tile(0, False, 0, [])
        for tt in range(NT):
            if tt + 1 < NT:
                load_qc(tt + 1)
            load_z(tt)
            fifo = final_stages(tt - 1) if tt > 0 else []
            run_tile(tt, True, tt + 1 if tt + 1 < NT else None, fifo)
        for f in final_stages(NT - 1):
            f()


def host_consts():
    c = {}
    c["c_ident"] = np.eye(128, dtype=np.float32)
    return c


def layer_inputs(inputs, layers):
    m = {}
    for l in layers:
        p = f"l{l}_"
        m[p + "norm"] = np.ascontiguousarray(inputs[p + "norm"].reshape(1, D))
        m[p + "w_out"] = inputs[p + "w_out"]
        m[p + "w_in"] = inputs[p + "w_in"]
        if l % 2 == 0:
            dw = inputs[p + "dw_w"].reshape(31, 16, 128).transpose(2, 1, 0)
            m[p + "dw_wT"] = np.ascontiguousarray(dw.reshape(128, 16 * 31))
            for nm in ("dw_b", "ln_g", "ln_b"):
                m[p + nm] = np.ascontiguousarray(inputs[p + nm].reshape(16, 128).T)
        else:
            m[p + "posT"] = np.ascontiguousarray(inputs[p + "cmp_pos"].T)
            for nm in ("ck_w1", "ck_w2", "cv_w1", "cv_w2"):
                m[p + nm] = inputs[p + nm]
    m["final_norm"] = np.ascontiguousarray(inputs["final_norm"].reshape(1, D))
    return m


_PROG_CACHE = {}


def run(inputs, layers=(0, 1, 2, 3), final_norm=True, ncores=NCORES, nseq=NSEQ, x_override=None, trace=False, dbg=99):
    key = (tuple(layers), final_norm, nseq, dbg)
    if key not in _PROG_CACHE:
        _PROG_CACHE[key] = Prog(layers, final_norm, nseq, dbg)
    prog = _PROG_CACHE[key]
    shared = dict(host_consts())
    shared.update(layer_inputs(inputs, layers))
    if any(l % 2 == 1 for l in layers):
        shared.update(nsa_host_tables(inputs["rel_bias"]))
    x = inputs["x"] if x_override is None else x_override
    in_maps = []
    for c in range(ncores):
        m = dict(shared)
        m["x"] = np.ascontiguousarray(x[c * nseq:(c + 1) * nseq])
        in_maps.append(m)
    res = run_bass_kernel_spmd(prog.nc, in_maps, core_ids=list(range(ncores)), **({'trace': True} if trace else {}))
    if trace:
        print('exec_time_ns', res.exec_time_ns)
    return np.concatenate([r["y"] for r in res.results], axis=0)


def _t5_bucket_np(dist):
    dist = np.maximum(dist, 0)
    d = np.maximum(dist, 16).astype(np.float32)
    large = 16 + (np.log(d / np.float32(16)) / np.float32(np.log(8.0)) * np.float32(16)).astype(np.int32)
    large = np.minimum(large, 31)
    return np.where(dist < 16, dist, large)


def nsa_host_tables(rel_bias):
    m = {}
    k = np.arange(128)[:, None]
    t = np.arange(128)[None, :]
    dA = t - k + 128
    dB = t - k
    mp = np.arange(248)[:, None]
    dC = t - 16 * (mp - 120) - 31
    m["rb_A"] = np.ascontiguousarray(rel_bias[_t5_bucket_np(dA)].transpose(0, 2, 1))
    m["rb_B"] = np.ascontiguousarray(rel_bias[_t5_bucket_np(dB)].transpose(0, 2, 1))
    m["rb_C"] = np.ascontiguousarray(rel_bias[_t5_bucket_np(dC)].transpose(0, 2, 1))
    m["rb_far"] = np.ascontiguousarray(rel_bias[31:32, :])
    m["m_B"] = np.where(dB >= 0, 0.0, NEGM).astype(np.float32)
    m["m_C"] = np.where(dC >= 0, 0.0, NEGM).astype(np.float32)
    m["m_W0"] = np.where(k > t, 0.0, NEGM).astype(np.float32)
    E = np.zeros((4, 32, 4, 16, 128), np.float32)
    for g in range(4):
        for kc in range(16):
            E[g, 2 * kc, g, kc, 0:64] = -NEGM
            E[g, 2 * kc + 1, g, kc, 64:128] = -NEGM
    m["c_E"] = np.ascontiguousarray(E.reshape(128, 64 * 128))
    tl = np.arange(128)[:, None, None]
    ti = np.arange(16)[None, :, None]
    sb = np.arange(32)[None, None, :]
    tabs = ti * 128 + tl
    cur = tabs // 64
    forced = (sb == 0) | (sb == cur) | (sb == cur - 1)
    causal = sb * 64 <= tabs
    keep = (causal & ~forced).astype(np.float32)
    add = np.where(causal, np.where(forced, 1e6, 0.0), -1e30).astype(np.float32)
    m["c_keep"] = np.ascontiguousarray(keep.reshape(128, 512))
    m["c_add"] = np.ascontiguousarray(add.reshape(128, 512))
    ov = np.zeros((128, 34), np.float32)
    n = np.arange(127)[:, None]
    s0 = np.arange(32)[None, :] * 64
    j0 = n * 16
    ov[:127, 0] = 1.0
    ov[:127, 2:34] = ((j0 < s0 + 64) & (j0 + 32 > s0)).astype(np.float32)
    m["c_ov"] = ov
    return m


def kernel(**inputs):
    inputs = {k: np.asarray(v) for k, v in inputs.items()}
    return run(inputs)
```

```python
import numpy as np
from contextlib import ExitStack
import concourse.bass as bass
import concourse.mybir as mybir
from concourse.bass_utils import run_bass_kernel_spmd

F32 = mybir.dt.float32
BF16 = mybir.dt.bfloat16
AF = mybir.ActivationFunctionType
ALU = mybir.AluOpType
AX = mybir.AxisListType

D = 2048
S = 2048
NT = S // 128
NSEQ = 2
NCORES = 8
NH = 16
NG = 4
DK = 128
EPS = 1e-6
NEGM = -30000.0
CONF_W = 3 * D
NSA_W = 2048 + 6 * 512 + 48 + 2048


class Owner:
    def __init__(self, K, name):
        self.sem = K.stack.enter_context(K.nc.semaphore(name))
        self.count = 0
        self.name = name


class Eng(Owner):
    def __init__(self, K, name, h):
        super().__init__(K, "e_" + name)
        self.h = h
        self.seen = {}
        self.is_pe = name == "pe"
        self.last_inc = True


class Buf:
    def __init__(self, ap, name=""):
        self.ap = ap
        self.name = name
        self.writers = {}
        self.readers = {}
        self.prev = {}
        self.excl = False

    def __getitem__(self, idx):
        return self.ap[idx]


def _merge(d, own, val):
    if d.get(own, 0) < val:
        d[own] = val


class Kern:
    def __init__(self, nc):
        self.nc = nc
        self.stack = ExitStack()
        self.pe = Eng(self, "pe", nc.tensor)
        self.act = Eng(self, "act", nc.scalar)
        self.dve = Eng(self, "dve", nc.vector)
        self.pool = Eng(self, "pool", nc.gpsimd)
        self.sp = Eng(self, "sp", nc.sync)
        self.engs = [self.pe, self.act, self.dve, self.pool, self.sp]
        self.slots = {}
        self.n_dram = 0

    def slot(self, name):
        if name not in self.slots:
            self.slots[name] = Owner(self, "d_" + name)
        return self.slots[name]

    def sb(self, st, name, shape, dt):
        self.n_dram += 1
        nm = f"{name}_{self.n_dram}"
        return Buf(st.enter_context(self.nc.sbuf_tensor(nm, list(shape), dt)), nm)

    def dram(self, name, shape, dt, kind=None):
        if kind is None:
            t = self.nc.dram_tensor(name, list(shape), dt)
        else:
            t = self.nc.dram_tensor(name, list(shape), dt, kind=kind)
        return t.ap()

    def _deps(self, eng, reads, writes):
        deps = {}

        def add(d, raw):
            for own, val in d.items():
                if own is eng and (eng.is_pe or not raw or val > eng.count):
                    continue
                _merge(deps, own, val)

        for b in reads:
            add(b.writers, True)
            if b.excl:
                add(b.readers, False)
        for b in writes:
            add(b.writers, False)
            add(b.readers, False)
            add(b.prev, False)
        return [(o, v) for o, v in deps.items() if eng.seen.get(o, 0) < v]

    def _emit_waits(self, eng, need, fn):
        for o, v in need[1:]:
            eng.h.wait_ge(o.sem, v)
        ins = fn()
        if need:
            ins._wait_ge(need[0][0].sem, need[0][1])
        for o, v in need:
            eng.seen[o] = v
        return ins

    def _update(self, ev, reads, writes):
        own, val = ev
        for b in reads:
            _merge(b.readers, own, val)
        for b in writes:
            if b.readers:
                prev = dict(b.readers)
                for o, v in b.writers.items():
                    _merge(prev, o, v)
                b.prev = prev
                b.readers = {}
                b.writers = {}
            _merge(b.writers, own, val)

    def op(self, eng, fn, reads=(), writes=(), inc=True):
        need = self._deps(eng, reads, writes)
        ins = self._emit_waits(eng, need, fn)
        if inc:
            eng.count += 1
            ins.then_inc(eng.sem, 1)
            ev = (eng, eng.count)
            eng.last_inc = True
        else:
            ev = (eng, eng.count + 1)
            eng.last_inc = False
        self._update(ev, reads, writes)
        return ins

    def dma(self, q, out, in_, slot, reads=(), writes=(), group=False, **kw):
        need = self._deps(q, reads, writes)
        if not group and slot.count > 0 and q.seen.get(slot, 0) < slot.count:
            need = [(o, v) for o, v in need if o is not slot] + [(slot, slot.count)]
        ins = self._emit_waits(q, need, lambda: q.h.dma_start(out=out, in_=in_, **kw))
        slot.count += 16
        ins.then_inc(slot.sem, 16)
        self._update((slot, slot.count), reads, writes)
        return ins

    def barrier(self):
        assert all(e.last_inc for e in self.engs)
        owners = list(self.engs) + list(self.slots.values())
        for e in self.engs:
            for o in owners:
                if o is e or o.count == 0:
                    continue
                if e.seen.get(o, 0) < o.count:
                    e.h.wait_ge(o.sem, o.count)
                    e.seen[o] = o.count


class Prog:
    def __init__(self, layers=(0, 1, 2, 3), final_norm=True, nseq=NSEQ, dbg=99):
        self.dbg = dbg
        self.layers = list(layers)
        self.final_norm = final_norm
        self.nseq = nseq
        nc = bass.Bass("TRN2", target_bir_lowering=False)
        self.nc = nc
        self.K = Kern(nc)
        self.build()

    def declare_inputs(self):
        K = self.K
        ns = self.nseq
        I = {}

        def inp(name, shape, dt=F32):
            I[name] = K.dram(name, shape, dt, kind="ExternalInput")

        inp("x", [ns, S, D])
        inp("c_ident", [128, 128])
        for l in self.layers:
            p = f"l{l}_"
            inp(p + "norm", [1, D])
            inp(p + "w_out", [D, D])
            if l % 2 == 0:
                inp(p + "w_in", [D, CONF_W])
                inp(p + "dw_wT", [128, 16 * 31])
                inp(p + "dw_b", [128, 16])
                inp(p + "ln_g", [128, 16])
                inp(p + "ln_b", [128, 16])
            else:
                inp(p + "w_in", [D, NSA_W])
                inp(p + "posT", [128, 32])
                inp(p + "ck_w1", [4096, 128])
                inp(p + "ck_w2", [128, 128])
                inp(p + "cv_w1", [4096, 128])
                inp(p + "cv_w2", [128, 128])
        if any(l % 2 == 1 for l in self.layers):
            inp("rb_A", [128, 16, 128])
            inp("rb_B", [128, 16, 128])
            inp("rb_C", [248, 16, 128])
            inp("rb_far", [1, 16])
            inp("m_B", [128, 128])
            inp("m_C", [248, 128])
            inp("m_W0", [128, 128])
            inp("c_E", [128, 64 * 128])
            inp("c_keep", [128, 16 * 32])
            inp("c_add", [128, 16 * 32])
            inp("c_ov", [128, 34])
        inp("final_norm", [1, D])
        self.I = I
        self.out = K.dram("y", [ns, S, D], F32, kind="ExternalOutput")

    def build(self):
        K = self.K
        nc = self.nc
        ns = self.nseq
        self.declare_inputs()
        I = self.I
        with K.stack:
            st = K.stack
            self.ident_f = K.sb(st, "ident_f", [128, 128], F32)
            self.ident_b = K.sb(st, "ident_b", [128, 128], BF16)
            self.ones_f = K.sb(st, "ones_f", [128, 128], F32)
            self.gbc = K.sb(st, "gbc", [128, D], F32)
            self.gfin = K.sb(st, "gfin", [128, D], F32)
            self.small = K.sb(st, "small", [128, 64], F32)
            pt = st.enter_context(nc.psum_tensor("psum", [128, 8, 512], F32))
            self.pt = pt
            self.P = [Buf(pt[:, b, :], f"ps{b}") for b in range(8)]
            for b_ in self.P:
                b_.excl = True
            sl = K.slot("const")
            K.dma(K.sp, self.ident_f[:], I["c_ident"], sl, writes=[self.ident_f])
            K.op(K.act, lambda: nc.scalar.copy(out=self.ident_b[:], in_=self.ident_f[:]),
                 reads=[self.ident_f], writes=[self.ident_b])
            K.op(K.pool, lambda: nc.gpsimd.memset(self.ones_f[:], 1.0), writes=[self.ones_f])
            K.dma(K.sp, self.gfin[:], bass.AP(I["final_norm"].tensor, 0, [[0, 128], [1, D]]), sl,
                  writes=[self.gfin])
            K.op(K.pool, lambda: nc.gpsimd.memset(self.small[:, 0:1], EPS), writes=[self.small])
            nl = len(self.layers)
            self.X = [I["x"]]
            for i in range(nl - 1):
                self.X.append(K.dram(f"xs{i}", [ns, S, D], F32))
            self.X.append(self.out)
            self.Xb = [[[Buf(None, f"X{i}_{s}_{t}") for t in range(NT)] for s in range(ns)]
                       for i in range(nl + 1)]
            self.WT = K.dram("wt_s", [ns, NT, 128, 16, 128], BF16)
            self.WTb = [Buf(None, f"WT{s}") for s in range(ns)]
            self.Ys = K.dram("y_s", [ns, 16, 128, S], F32)
            self.Yb = [[Buf(None, f"Y{s}_{c}") for c in range(16)] for s in range(ns)]
            if any(l % 2 == 1 for l in self.layers):
                self.QT = K.dram("qt_s", [ns, NT, 128, 16, 128], BF16)
                self.SZ = K.dram("sz_s", [ns, NT, 128, 16, 128], BF16)
                self.KT = K.dram("kt_s", [ns, 4, 4, 128, S], BF16)
                self.VA = K.dram("va_s", [ns, 2, NT, 128, 520], BF16)
                self.TA = K.dram("ta_s", [128, 16, 128], F32)
                self.TB = K.dram("tb_s", [128, 16, 128], F32)
                self.TC = K.dram("tc_s", [248, 16, 128], F32)
                self.QTb = [Buf(None, f"QT{s}") for s in range(ns)]
                self.SZb = [Buf(None, f"SZ{s}") for s in range(ns)]
                self.KTb = [Buf(None, f"KT{s}") for s in range(ns)]
                self.VAb = [Buf(None, f"VA{s}") for s in range(ns)]
                self.Tb = Buf(None, "Ttab")
                self.nsa_setup()
            K.barrier()
            for li, l in enumerate(self.layers):
                last = (li == nl - 1) and self.final_norm
                sl = K.slot("const")
                K.dma(K.sp, self.gbc[:], bass.AP(I[f"l{l}_norm"].tensor, 0, [[0, 128], [1, D]]), sl,
                      writes=[self.gbc])
                if l % 2 == 0:
                    self.conformer_layer(li, l)
                else:
                    self.nsa_layer(li, l)
                self.g2_phase(li, l, last)
            K.barrier()

    def p1_phase(self, st, li, s, hT):
        K, nc = self.K, self.nc
        xr = [K.sb(st, f"p1x{i}", [128, D], F32) for i in range(3)]
        hb = [K.sb(st, f"p1h{i}", [128, D], BF16) for i in range(2)]
        junk = K.sb(st, "p1junk", [128, D], BF16)
        stat = [K.sb(st, f"p1s{i}", [128, 4], F32) for i in range(2)]
        X = self.X[li]
        xsl = [K.slot("x0"), K.slot("x1"), K.slot("t0")]
        hTw = [Buf(None, f"hTw{i}") for i in range(2 * NT)]

        def load(tt):
            K.dma(K.sp, xr[tt % 3][:], X[s, tt * 128:(tt + 1) * 128, :], xsl[tt % 3],
                  reads=[self.Xb[li][s][tt]], writes=[xr[tt % 3]])

        def stage_a(tt):
            x, h, sm = xr[tt % 3], hb[tt % 2], stat[tt % 2]
            K.op(K.act, lambda: nc.scalar.activation(out=junk[:], in_=x[:], func=AF.Square,
                                                     accum_out=sm[:, 0:1]),
                 reads=[x], writes=[junk, sm])
            K.op(K.act, lambda: nc.scalar.activation(out=sm[:, 1:2], in_=sm[:, 0:1], func=AF.Sqrt,
                                                     scale=1.0 / D, bias=self.eps_ap()),
                 reads=[sm, self.small], writes=[sm])
            K.op(K.dve, lambda: nc.vector.reciprocal(out=sm[:, 2:3], in_=sm[:, 1:2]),
                 reads=[sm], writes=[sm])
            K.op(K.dve, lambda: nc.vector.scalar_tensor_tensor(
                out=h[:], in0=x[:], scalar=sm[:, 2:3], in1=self.gbc[:], op0=ALU.mult, op1=ALU.mult),
                reads=[x, sm, self.gbc], writes=[h])

        def stage_b(tt):
            h = hb[tt % 2]
            pb = (tt % 2) * 2
            pv = self.pt[:, pb:pb + 2, :].bitcast(BF16)
            for c in range(16):
                bank = self.P[pb + c // 8]
                o = pv[:, c // 8, (c % 8) * 128:(c % 8 + 1) * 128]
                K.op(K.pe, lambda: nc.tensor.transpose(out=o, in_=h[:, c * 128:(c + 1) * 128],
                                                       identity=self.ident_b[:]),
                     reads=[h, self.ident_b], writes=[bank], inc=(c % 8 == 7))
            for half in range(2):
                src = pv[:, half, :].rearrange("p (c t) -> p c t", c=8)
                dst = hT[:, half * 8:(half + 1) * 8, tt * 128:(tt + 1) * 128]
                if half == 0:
                    K.op(K.act, lambda: nc.scalar.copy(out=dst, in_=src), reads=[self.P[pb]],
                         writes=[hTw[2 * tt]])
                else:
                    K.op(K.dve, lambda: nc.vector.tensor_copy(out=dst, in_=src), reads=[self.P[pb + 1]],
                         writes=[hTw[2 * tt + 1]])

        load(0)
        load(1)
        stage_a(0)
        for tt in range(NT):
            if tt + 2 < NT:
                load(tt + 2)
            if tt + 1 < NT:
                stage_a(tt + 1)
            stage_b(tt)

    def eps_ap(self):
        return self.small[:, 0:1]

    def load_w_cast(self, dst_buf, dst_ap, w_ap, col0, ncols, slot, group=False):
        K = self.K
        src = w_ap.rearrange("(k p) n -> p k n", p=128)[:, :, col0:col0 + ncols]
        K.dma(K.pool, dst_ap, src, slot, writes=[dst_buf], group=group)

    def g2_phase(self, li, l, last):
        K, nc = self.K, self.nc
        I = self.I
        with ExitStack() as st:
            wo = K.sb(st, "g2w", [128, 16, D], BF16)
            wsl = K.slot("wo")
            for j in range(4):
                self.load_w_cast(wo, wo[:, :, j * 512:(j + 1) * 512], I[f"l{l}_w_out"], j * 512, 512, wsl,
                                 group=(j > 0))
            xr = [K.sb(st, f"g2x{i}", [128, D], F32) for i in range(2)]
            xo = [K.sb(st, f"g2o{i}", [128, D], F32) for i in range(2)]
            wt = [K.sb(st, f"g2t{i}", [128, 16, 128], BF16) for i in range(2)]
            junk = K.sb(st, "g2junk", [128, D], BF16)
            stat = [K.sb(st, f"g2s{i}", [128, 4], F32) for i in range(2)]
            xsl = [K.slot("x0"), K.slot("x1")]
            tsl = [K.slot("t0"), K.slot("t1")]
            osl = [K.slot("o0"), K.slot("o1")]
            Xi, Xo = self.X[li], self.X[li + 1]
            steps = [(s, tt) for s in range(self.nseq) for tt in range(NT)]

            def load(i):
                s, tt = steps[i]
                K.dma(K.sp, xr[i % 2][:], Xi[s, tt * 128:(tt + 1) * 128, :], xsl[i % 2],
                      reads=[self.Xb[li][s][tt]], writes=[xr[i % 2]])
                K.dma(K.sp, wt[i % 2][:], self.WT[s, tt], tsl[i % 2], reads=[self.WTb[s]], writes=[wt[i % 2]])

            load(0)
            for i, (s, tt) in enumerate(steps):
                if i + 1 < len(steps):
                    load(i + 1)
                x, o, w = xr[i % 2], xo[i % 2], wt[i % 2]
                for db in range(4):
                    bank = self.P[(i * 4 + db) % 2]
                    for c in range(16):
                        K.op(K.pe, lambda: nc.tensor.matmul(bank[:], lhsT=w[:, c, :],
                                                            rhs=wo[:, c, db * 512:(db + 1) * 512],
                                                            start=(c == 0), stop=(c == 15)),
                             reads=[w, wo], writes=[bank], inc=(c == 15))
                    K.op(K.dve, lambda: nc.vector.tensor_tensor(out=o[:, db * 512:(db + 1) * 512], in0=bank[:],
                                                                in1=x[:, db * 512:(db + 1) * 512], op=ALU.add),
                         reads=[bank, x], writes=[o])
                if last:
                    sm = stat[i % 2]
                    K.op(K.act, lambda: nc.scalar.activation(out=junk[:], in_=o[:], func=AF.Square,
                                                             accum_out=sm[:, 0:1]),
                         reads=[o], writes=[junk, sm])
                    K.op(K.act, lambda: nc.scalar.activation(out=sm[:, 1:2], in_=sm[:, 0:1], func=AF.Sqrt,
                                                             scale=1.0 / D, bias=self.eps_ap()),
                         reads=[sm, self.small], writes=[sm])
                    K.op(K.dve, lambda: nc.vector.reciprocal(out=sm[:, 2:3], in_=sm[:, 1:2]),
                         reads=[sm], writes=[sm])
                    K.op(K.dve, lambda: nc.vector.scalar_tensor_tensor(
                        out=o[:], in0=o[:], scalar=sm[:, 2:3], in1=self.gfin[:], op0=ALU.mult, op1=ALU.mult),
                        reads=[o, sm, self.gfin], writes=[o])
                K.dma(K.sp, Xo[s, tt * 128:(tt + 1) * 128, :], o[:], osl[i % 2], reads=[o],
                      writes=[self.Xb[li + 1][s][tt]])
            K.barrier()

    def conformer_layer(self, li, l):
        K, nc = self.K, self.nc
        I = self.I
        p = f"l{l}_"
        W = I[p + "w_in"]
        for s in range(self.nseq):
            with ExitStack() as st:
                hT = K.sb(st, "hT", [128, 16, S], BF16)
                with ExitStack() as st1:
                    self.p1_phase(st1, li, s, hT)
                    K.barrier()
                dwT = K.sb(st, "dwT", [128, 16, 31], F32)
                dwb = K.sb(st, "dwb", [128, 16], F32)
                lng = K.sb(st, "lng", [128, 16], F32)
                lnb = K.sb(st, "lnb", [128, 16], F32)
                csl = K.slot("const")
                K.dma(K.sp, dwT[:], I[p + "dw_wT"].rearrange("p (c k) -> p c k", c=16), csl, writes=[dwT])
                K.dma(K.sp, dwb[:], I[p + "dw_b"], csl, writes=[dwb])
                K.dma(K.sp, lng[:], I[p + "ln_g"], csl, writes=[lng])
                K.dma(K.sp, lnb[:], I[p + "ln_b"], csl, writes=[lnb])
                S1 = K.sb(st, "S1", [128, S], F32)
                S2 = K.sb(st, "S2", [128, S], F32)
                wa = [K.sb(st, f"wa{i}", [128, 16, 128], BF16) for i in range(2)]
                wb = [K.sb(st, f"wb{i}", [128, 16, 128], BF16) for i in range(2)]
                wasl = [K.slot("wa0"), K.slot("wa1")]
                wbsl = [K.slot("wb0"), K.slot("wb1")]
                with ExitStack() as st2:
                    v = [K.sb(st2, f"v{i}", [128, 30 + S], BF16) for i in range(2)]
                    dg = [K.sb(st2, f"dg{i}", [128, 31, 128], BF16) for i in range(2)]
                    sg = [K.sb(st2, f"sg{i}", [128, 512], F32) for i in range(2)]
                    ys = [K.sb(st2, f"ys{i}", [128, 512], F32) for i in range(2)]
                    yq = [K.sb(st2, f"yq{i}", [128, 512], F32) for i in range(2)]
                    ysl = [K.slot("ys0"), K.slot("ys1")]
                    for i in range(2):
                        K.op(K.pool, lambda: nc.gpsimd.memset(v[i][:, 0:30], 0.0), writes=[v[i]])

                    def loadw(c):
                        self.load_w_cast(wa[c % 2], wa[c % 2][:], W, c * 128, 128, wasl[c % 2])
                        self.load_w_cast(wb[c % 2], wb[c % 2][:], W, D + c * 128, 128, wbsl[c % 2])

                    def gemm_glu(c, tb):
                        j = c * 4 + tb
                        pa, pb = self.P[(j % 2) * 3], self.P[(j % 2) * 3 + 1]
                        for (ps_, w_) in ((pa, wa[c % 2]), (pb, wb[c % 2])):
                            for k in range(16):
                                K.op(K.pe, lambda: nc.tensor.matmul(ps_[:], lhsT=w_[:, k, :],
                                                                    rhs=hT[:, k, tb * 512:(tb + 1) * 512],
                                                                    start=(k == 0), stop=(k == 15)),
                                     reads=[w_, hT], writes=[ps_], inc=(k == 15))
                        g_ = sg[j % 2]
                        K.op(K.act, lambda: nc.scalar.activation(out=g_[:], in_=pb[:], func=AF.Sigmoid),
                             reads=[pb], writes=[g_])
                        vv = v[c % 2]
                        K.op(K.dve, lambda: nc.vector.tensor_tensor(
                            out=vv[:, 30 + tb * 512:30 + (tb + 1) * 512], in0=pa[:], in1=g_[:], op=ALU.mult),
                            reads=[pa, g_], writes=[vv])

                    KD = 8

                    def conv(c, tb):
                        j = c * 4 + tb
                        py = self.P[(j % 2) * 3 + 2]
                        vv, dd = v[c % 2], dg[c % 2]
                        for k in range(KD, 31):
                            K.op(K.pe, lambda: nc.tensor.matmul(py[:], lhsT=dd[:, k, :],
                                                                rhs=vv[:, k + tb * 512:k + (tb + 1) * 512],
                                                                start=(k == KD), stop=(k == 30)),
                                 reads=[dd, vv], writes=[py], inc=(k == 30))
                        y_, q_ = ys[j % 2], yq[j % 2]
                        K.op(K.dve, lambda: nc.vector.tensor_scalar(
                            out=y_[:], in0=vv[:, tb * 512:(tb + 1) * 512], scalar1=dwT[:, c, 0:1],
                            scalar2=dwb[:, c:c + 1], op0=ALU.mult, op1=ALU.add),
                            reads=[vv, dwT, dwb], writes=[y_])
                        for k in range(1, KD):
                            K.op(K.dve, lambda: nc.vector.scalar_tensor_tensor(
                                out=y_[:], in0=vv[:, k + tb * 512:k + (tb + 1) * 512], scalar=dwT[:, c, k:k + 1],
                                in1=y_[:], op0=ALU.mult, op1=ALU.add),
                                reads=[vv, dwT, y_], writes=[y_], inc=(k == KD - 1))
                        K.op(K.dve, lambda: nc.vector.tensor_tensor(out=y_[:], in0=py[:], in1=y_[:], op=ALU.add),
                             reads=[py, y_], writes=[y_])
                        K.op(K.act, lambda: nc.scalar.activation(out=q_[:], in_=y_[:], func=AF.Square),
                             reads=[y_], writes=[q_])
                        sl_ = slice(tb * 512, (tb + 1) * 512)
                        if c == 0:
                            K.op(K.pool, lambda: nc.gpsimd.tensor_copy(out=S1[:, sl_], in_=y_[:]),
                                 reads=[y_], writes=[S1])
                            K.op(K.pool, lambda: nc.gpsimd.tensor_copy(out=S2[:, sl_], in_=q_[:]),
                                 reads=[q_], writes=[S2])
                        else:
                            K.op(K.pool, lambda: nc.gpsimd.tensor_tensor(out=S1[:, sl_], in0=S1[:, sl_],
                                                                         in1=y_[:], op=ALU.add),
                                 reads=[y_, S1], writes=[S1])
                            K.op(K.pool, lambda: nc.gpsimd.tensor_tensor(out=S2[:, sl_], in0=S2[:, sl_],
                                                                         in1=q_[:], op=ALU.add),
                                 reads=[q_, S2], writes=[S2])
                        K.dma(K.sp, self.Ys[s, c, :, sl_], y_[:], ysl[j % 2], reads=[y_], writes=[self.Yb[s][c]])

                    def mkdiag(c):
                        dd = dg[c % 2]
                        K.op(K.pool, lambda: nc.gpsimd.tensor_tensor(
                            out=dd[:],
                            in0=self.ident_b[:].unsqueeze(1).broadcast_to([128, 31, 128]),
                            in1=dwT[:, c, :].unsqueeze(2).broadcast_to([128, 31, 128]),
                            op=ALU.mult),
                            reads=[self.ident_b, dwT], writes=[dd])

                    steps = [(c, tb) for c in range(16) for tb in range(4)]
                    loadw(0)
                    mkdiag(0)
                    gemm_glu(0, 0)
                    for i, (c, tb) in enumerate(steps):
                        if tb == 0 and c + 1 < 16:
                            loadw(c + 1)
                            mkdiag(c + 1)
                        if i + 1 < len(steps):
                            gemm_glu(*steps[i + 1])
                        conv(c, tb)
                    for tb in range(4):
                        sl_ = slice(tb * 512, (tb + 1) * 512)
                        p1, p2 = self.P[6], self.P[7]
                        K.op(K.pe, lambda: nc.tensor.matmul(p1[:], lhsT=self.ones_f[:], rhs=S1[:, sl_],
                                                            start=True, stop=True),
                             reads=[self.ones_f, S1], writes=[p1])
                        K.op(K.pe, lambda: nc.tensor.matmul(p2[:], lhsT=self.ones_f[:], rhs=S2[:, sl_],
                                                            start=True, stop=True),
                             reads=[self.ones_f, S2], writes=[p2])
                        K.op(K.act, lambda: nc.scalar.mul(out=S1[:, sl_], in_=p1[:], mul=1.0 / D),
                             reads=[p1], writes=[S1])
                        t_ = sg[0]
                        K.op(K.dve, lambda: nc.vector.tensor_tensor(out=t_[:], in0=S1[:, sl_], in1=S1[:, sl_],
                                                                    op=ALU.mult),
                             reads=[S1], writes=[t_])
                        K.op(K.dve, lambda: nc.vector.scalar_tensor_tensor(
                            out=t_[:], in0=p2[:], scalar=1.0 / D, in1=t_[:], op0=ALU.mult, op1=ALU.subtract),
                            reads=[p2, t_], writes=[t_])
                        K.op(K.act, lambda: nc.scalar.activation(out=t_[:], in_=t_[:], func=AF.Sqrt,
                                                                 bias=self.eps_ap()),
                             reads=[t_, self.small], writes=[t_])
                        K.op(K.dve, lambda: nc.vector.reciprocal(out=S2[:, sl_], in_=t_[:]),
                             reads=[t_], writes=[S2])
                    K.barrier()
                with ExitStack() as st3:
                    yb = [K.sb(st3, f"yb{i}", [128, S], F32) for i in range(2)]
                    wTb = [K.sb(st3, f"wTb{i}", [128, S], BF16) for i in range(2)]
                    szb = [K.sb(st3, f"szb{i}", [128, 512], F32) for i in range(2)]
                    s1b = [K.sb(st3, f"s1b{i}", [128, 512], F32) for i in range(2)]
                    ybsl = [K.slot("yb0"), K.slot("yb1")]
                    wtsl = [K.slot("wt0"), K.slot("wt1")]

                    def load3(c):
                        self.load_w_cast(wa[c % 2], wa[c % 2][:], W, 2 * D + c * 128, 128, wasl[c % 2])
                        K.dma(K.sp, yb[c % 2][:], self.Ys[s, c], ybsl[c % 2], reads=[self.Yb[s][c]],
                              writes=[yb[c % 2]])

                    load3(0)
                    for c in range(16):
                        if c + 1 < 16:
                            load3(c + 1)
                        y_, w_ = yb[c % 2], wTb[c % 2]
                        for tb in range(4):
                            j = c * 4 + tb
                            sl_ = slice(tb * 512, (tb + 1) * 512)
                            pz = self.P[j % 2]
                            for k in range(16):
                                K.op(K.pe, lambda: nc.tensor.matmul(pz[:], lhsT=wa[c % 2][:, k, :], rhs=hT[:, k, sl_],
                                                                    start=(k == 0), stop=(k == 15)),
                                     reads=[wa[c % 2], hT], writes=[pz], inc=(k == 15))
                            z_, a_ = szb[j % 2], s1b[j % 2]
                            K.op(K.act, lambda: nc.scalar.activation(out=z_[:], in_=pz[:], func=AF.Silu),
                                 reads=[pz], writes=[z_])
                            K.op(K.dve, lambda: nc.vector.tensor_tensor(out=y_[:, sl_], in0=y_[:, sl_],
                                                                        in1=S1[:, sl_], op=ALU.subtract),
                                 reads=[y_, S1], writes=[y_])
                            K.op(K.dve, lambda: nc.vector.tensor_tensor(out=y_[:, sl_], in0=y_[:, sl_],
                                                                        in1=S2[:, sl_], op=ALU.mult),
                                 reads=[y_, S2], writes=[y_])
                            K.op(K.act, lambda: nc.scalar.activation(out=a_[:], in_=y_[:, sl_], func=AF.Silu,
                                                                     scale=lng[:, c:c + 1], bias=lnb[:, c:c + 1]),
                                 reads=[y_, lng, lnb], writes=[a_])
                            K.op(K.dve, lambda: nc.vector.tensor_tensor(out=w_[:, sl_], in0=a_[:], in1=z_[:],
                                                                        op=ALU.mult),
                                 reads=[a_, z_], writes=[w_])
                        dst = self.WT[s].rearrange("t p c j -> p t c j")[:, :, c, :]
                        K.dma(K.sp, dst, w_[:].rearrange("p (t j) -> p t j", t=NT), wtsl[c % 2], reads=[w_],
                              writes=[self.WTb[s]])
                    K.barrier()

    def nsa_setup(self):
        K, nc, I = self.K, self.nc, self.I
        with ExitStack() as st:
            far = K.sb(st, "far", [128, 16], F32)
            sl = K.slot("const")
            K.dma(K.sp, far[:], bass.AP(I["rb_far"].tensor, 0, [[0, 128], [1, 16]]), sl, writes=[far])
            jobs = [("rb_A", None, self.TA, 0, 128), ("rb_B", "m_B", self.TB, 0, 128),
                    ("rb_C", "m_C", self.TC, 0, 128), ("rb_C", "m_C", self.TC, 128, 120)]
            for ji, (rn, mn, dst, r0, nr) in enumerate(jobs):
                raw = K.sb(st, f"raw{ji}", [128, 16, 128], F32)
                K.dma(K.sp, raw[0:nr], I[rn][r0:r0 + nr], K.slot(f"x{ji % 2}"), writes=[raw])
                K.op(K.dve, lambda: nc.vector.tensor_tensor(
                    out=raw[0:nr], in0=raw[0:nr], in1=far[0:nr].unsqueeze(2).broadcast_to([nr, 16, 128]),
                    op=ALU.subtract), reads=[raw, far], writes=[raw])
                if mn is not None:
                    mk = K.sb(st, f"mk{ji}", [128, 128], F32)
                    K.dma(K.sp, mk[0:nr], I[mn][r0:r0 + nr], K.slot(f"t{ji % 2}"), writes=[mk])
                    K.op(K.dve, lambda: nc.vector.tensor_tensor(
                        out=raw[0:nr], in0=raw[0:nr], in1=mk[0:nr].unsqueeze(1).broadcast_to([nr, 16, 128]),
                        op=ALU.add), reads=[raw, mk], writes=[raw])
                K.dma(K.sp, dst[r0:r0 + nr], raw[0:nr], K.slot(f"o{ji % 2}"), reads=[raw], writes=[self.Tb])
            K.barrier()

    def nsa_layer(self, li, l):
        K, nc, I = self.K, self.nc, self.I
        p = f"l{l}_"
        W = I[p + "w_in"]
        for s in range(self.nseq):
            with ExitStack() as so:
                gates = K.sb(so, "gates", [128, NT, 48], F32)
                with ExitStack() as st:
                    hT = K.sb(st, "hT", [128, 16, S], BF16)
                    with ExitStack() as st1:
                        self.p1_phase(st1, li, s, hT)
                        K.barrier()
                    if self.dbg >= 2:
                        self.n2_phase(st, s, W, hT, gates)
                    K.barrier()
                with ExitStack() as st:
                    KcT = K.sb(st, "KcT", [128, 4, 128], BF16)
                    VcO = K.sb(st, "VcO", [128, 4, 162], BF16)
                    res = self.n4_residents(st, s)
                    with ExitStack() as st3:
                        if self.dbg >= 3:
                            self.n3_phase(st3, s, p, KcT, VcO)
                        K.barrier()
                    if self.dbg >= 4:
                        self.n4_phase(st, s, KcT, VcO, gates, res)
                    K.barrier()

    def n2_phase(self, st, s, W, hT, gates):
        K, nc = self.K, self.nc
        wst = [K.sb(st, f"wst{i}", [128, 16, 128], BF16) for i in range(3)]
        wsl = [K.slot(f"wa{i}") for i in range(2)] + [K.slot("wb0")]
        stg = [K.sb(st, f"stg{i}", [128, S], BF16) for i in range(2)]
        gsl = [K.slot("ys0"), K.slot("ys1")]
        chunks = []
        for h in range(16):
            chunks.append((h * 128, "q", h))
        for wi, base in enumerate((2048, 3072, 4096, 2560)):
            for g in range(4):
                chunks.append((base + g * 128, "k", (wi, g)))
        for c in range(16):
            chunks.append((5168 + c * 128, "z", c))

        def loadw(i):
            self.load_w_cast(wst[i % 3], wst[i % 3][:], W, chunks[i][0], 128, wsl[i % 3])

        loadw(0)
        loadw(1)
        j = 0
        for i, (col0, kind, idx) in enumerate(chunks):
            if i + 2 < len(chunks):
                loadw(i + 2)
            w_ = wst[i % 3]
            g_ = stg[i % 2]
            for tb in range(4):
                ps_ = self.P[j % 4]
                j += 1
                sl_ = slice(tb * 512, (tb + 1) * 512)
                for k in range(16):
                    K.op(K.pe, lambda: nc.tensor.matmul(ps_[:], lhsT=w_[:, k, :], rhs=hT[:, k, sl_],
                                                        start=(k == 0), stop=(k == 15)),
                         reads=[w_, hT], writes=[ps_], inc=(k == 15))
                if kind == "q":
                    K.op(K.act, lambda: nc.scalar.mul(out=g_[:, sl_], in_=ps_[:], mul=float(DK) ** -0.5),
                         reads=[ps_], writes=[g_])
                elif kind == "z":
                    K.op(K.act, lambda: nc.scalar.activation(out=g_[:, sl_], in_=ps_[:], func=AF.Silu),
                         reads=[ps_], writes=[g_])
                else:
                    K.op(K.dve, lambda: nc.vector.tensor_copy(out=g_[:, sl_], in_=ps_[:]),
                         reads=[ps_], writes=[g_])
            if kind == "q":
                dst = self.QT[s].rearrange("t p h j -> p t h j")[:, :, idx, :]
                K.dma(K.sp, dst, g_[:].rearrange("p (t j) -> p t j", t=NT), gsl[i % 2], reads=[g_],
                      writes=[self.QTb[s]])
            elif kind == "z":
                dst = self.SZ[s].rearrange("t p h j -> p t h j")[:, :, idx, :]
                K.dma(K.sp, dst, g_[:].rearrange("p (t j) -> p t j", t=NT), gsl[i % 2], reads=[g_],
                      writes=[self.SZb[s]])
            else:
                wi, g = idx
                K.dma(K.sp, self.KT[s, wi, g], g_[:], gsl[i % 2], reads=[g_], writes=[self.KTb[s]])
        wmv = [K.sb(st, f"wmv{i}", [128, 16, 512], BF16) for i in range(2)]
        wg = K.sb(st, "wg", [128, 16, 48], BF16)
        vst = [K.sb(st, f"vst{i}", [128, 4, 130], BF16) for i in range(2)]
        msl = [K.slot("wb1"), K.slot("wo")]
        vsl = [K.slot("yb0"), K.slot("yb1")]
        for i in range(2):
            K.op(K.pool, lambda: nc.gpsimd.memset(vst[i][:, :, 128:129], 1.0), writes=[vst[i]])
            K.op(K.pool, lambda: nc.gpsimd.memset(vst[i][:, :, 129:130], 0.0), writes=[vst[i]])
        for vi, base in enumerate((3584, 4608)):
            self.load_w_cast(wmv[vi], wmv[vi][:], W, base, 512, msl[vi])
        self.load_w_cast(wg, wg[:], W, 5120, 48, K.slot("const"))
        for vi in range(2):
            for tt in range(NT):
                ps_ = self.P[j % 4]
                j += 1
                for k in range(16):
                    K.op(K.pe, lambda: nc.tensor.matmul(ps_[:], lhsT=hT[:, k, tt * 128:(tt + 1) * 128],
                                                        rhs=wmv[vi][:, k, :], start=(k == 0), stop=(k == 15)),
                         reads=[wmv[vi], hT], writes=[ps_], inc=(k == 15))
                v_ = vst[tt % 2]
                eng = K.act if tt % 2 == 0 else K.dve
                src = ps_[:].rearrange("p (g d) -> p g d", g=4)
                if tt % 2 == 0:
                    K.op(K.act, lambda: nc.scalar.copy(out=v_[:, :, 0:128], in_=src), reads=[ps_], writes=[v_])
                else:
                    K.op(K.dve, lambda: nc.vector.tensor_copy(out=v_[:, :, 0:128], in_=src), reads=[ps_],
                         writes=[v_])
                K.dma(K.sp, self.VA[s, vi, tt], v_[:].rearrange("p g d -> p (g d)"), vsl[tt % 2], reads=[v_],
                      writes=[self.VAb[s]])
        for tt in range(NT):
            ps_ = self.P[j % 4]
            j += 1
            for k in range(16):
                K.op(K.pe, lambda: nc.tensor.matmul(ps_[:, 0:48], lhsT=hT[:, k, tt * 128:(tt + 1) * 128],
                                                    rhs=wg[:, k, :], start=(k == 0), stop=(k == 15)),
                     reads=[wg, hT], writes=[ps_], inc=(k == 15))
            K.op(K.act, lambda: nc.scalar.activation(out=gates[:, tt, :], in_=ps_[:, 0:48], func=AF.Sigmoid),
                 reads=[ps_], writes=[gates])

    def n3_phase(self, st, s, p, KcT, VcO):
        K, nc, I = self.K, self.nc, self.I
        srcT = [K.sb(st, f"cT{i}", [128, 4, S], BF16) for i in range(2)]
        w1 = [K.sb(st, f"w1{i}", [128, 32, 128], BF16) for i in range(2)]
        w2 = [K.sb(st, f"w2{i}", [128, 128], BF16) for i in range(2)]
        HT = [K.sb(st, f"HT{i}", [128, 4, 127], BF16) for i in range(2)]
        posb = K.sb(st, "posb", [128, 32], BF16)
        cvec = K.sb(st, "cvec", [128, 2], F32)
        csl = K.slot("const")
        K.dma(K.sp, srcT[0][:], self.KT[s, 0].rearrange("g p t -> p g t"), K.slot("x0"), reads=[self.KTb[s]],
              writes=[srcT[0]])
        K.dma(K.sp, srcT[1][:], self.KT[s, 3].rearrange("g p t -> p g t"), K.slot("x1"), reads=[self.KTb[s]],
              writes=[srcT[1]])
        for i, nm in enumerate(("ck", "cv")):
            K.dma(K.pool, w1[i][:], I[p + nm + "_w1"].rearrange("(l d) j -> d l j", d=128), K.slot(f"wa{i}"),
                  writes=[w1[i]])
            K.dma(K.pool, w2[i][:], I[p + nm + "_w2"], K.slot(f"wb{i}"), writes=[w2[i]])
        K.dma(K.pool, posb[:], I[p + "posT"], csl, writes=[posb])
        K.op(K.pool, lambda: nc.gpsimd.memset(KcT[:], 0.0), writes=[KcT])
        K.op(K.pool, lambda: nc.gpsimd.memset(VcO[:], 0.0), writes=[VcO])
        for g in range(4):
            K.dma(K.pool, VcO[:, g, 128:162], I["c_ov"], csl, writes=[VcO])
        for i in range(2):
            pc = self.P[4 + i]
            for l_ in range(32):
                K.op(K.pe, lambda: nc.tensor.matmul(pc[:, 0:1], lhsT=w1[i][:, l_, :], rhs=posb[:, l_:l_ + 1],
                                                    start=(l_ == 0), stop=(l_ == 31)),
                     reads=[w1[i], posb], writes=[pc], inc=(l_ == 31))
            K.op(K.act, lambda: nc.scalar.copy(out=cvec[:, i:i + 1], in_=pc[:, 0:1]), reads=[pc], writes=[cvec])
            ph = self.P[i]
            for l_ in range(32):
                K.op(K.pe, lambda: nc.tensor.matmul(ph[:, 0:508].rearrange("p (g n) -> p g n", g=4),
                                                    lhsT=w1[i][:, l_, :],
                                                    rhs=srcT[i][:, :, l_:l_ + 16 * 126 + 1:16],
                                                    start=(l_ == 0), stop=(l_ == 31)),
                     reads=[w1[i], srcT[i]], writes=[ph], inc=(l_ == 31))
            K.op(K.act, lambda: nc.scalar.activation(out=HT[i][:], in_=ph[:, 0:508].rearrange("p (g n) -> p g n", g=4),
                                                     func=AF.Silu, bias=cvec[:, i:i + 1]),
                 reads=[ph, cvec], writes=[HT[i]])
        pk = self.P[2]
        K.op(K.pe, lambda: nc.tensor.matmul(pk[:, 0:508], lhsT=w2[0][:], rhs=HT[0][:].rearrange("p g n -> p (g n)"),
                                            start=True, stop=True),
             reads=[w2[0], HT[0]], writes=[pk])
        K.op(K.act, lambda: nc.scalar.copy(out=KcT[:, :, 0:127], in_=pk[:, 0:508].rearrange("p (g n) -> p g n", g=4)),
             reads=[pk], writes=[KcT])
        pv = self.P[3]
        for g in range(4):
            K.op(K.pe, lambda: nc.tensor.matmul(pv[0:127, g * 128:(g + 1) * 128], lhsT=HT[1][:, g, :], rhs=w2[1][:],
                                                start=True, stop=True),
                 reads=[w2[1], HT[1]], writes=[pv], inc=(g == 3))
        K.op(K.dve, lambda: nc.vector.tensor_copy(out=VcO[0:127, :, 0:128],
                                                  in_=pv[0:127, :].rearrange("p (g d) -> p g d", g=4)),
             reads=[pv], writes=[VcO])

    def n4_residents(self, st, s):
        K, nc, I = self.K, self.nc, self.I
        ksT = K.sb(st, "ksT", [128, 4, S], BF16)
        kwT = K.sb(st, "kwT", [128, 4, S], BF16)
        vsA = K.sb(st, "vsA", [128, NT, 520], BF16)
        vwA = K.sb(st, "vwA", [128, NT, 520], BF16)
        Eb = K.sb(st, "Eb", [128, 64, 128], BF16)
        TA = K.sb(st, "TA", [128, 16, 128], F32)
        TB = K.sb(st, "TB", [128, 16, 128], F32)
        keep = K.sb(st, "keep", [128, NT, 32], F32)
        addm = K.sb(st, "addm", [128, NT, 32], F32)
        M0 = K.sb(st, "M0", [128, 128], BF16)
        csl = K.slot("r6")
        K.dma(K.sp, ksT[:], self.KT[s, 1].rearrange("g p t -> p g t"), K.slot("r0"), reads=[self.KTb[s]], writes=[ksT])
        K.dma(K.sp, kwT[:], self.KT[s, 2].rearrange("g p t -> p g t"), K.slot("r1"), reads=[self.KTb[s]], writes=[kwT])
        K.dma(K.sp, vsA[:], self.VA[s, 0].rearrange("t p c -> p t c"), K.slot("r2"), reads=[self.VAb[s]], writes=[vsA])
        K.dma(K.sp, vwA[:], self.VA[s, 1].rearrange("t p c -> p t c"), K.slot("r3"), reads=[self.VAb[s]], writes=[vwA])
        for j in range(4):
            K.dma(K.pool, Eb[:, j * 16:(j + 1) * 16, :],
                  I["c_E"].rearrange("p (e k) -> p e k", k=128)[:, j * 16:(j + 1) * 16, :], K.slot("r4"), writes=[Eb],
                  group=(j > 0))
        K.dma(K.pool, M0[:], I["m_W0"], K.slot("r5"), writes=[M0])
        K.dma(K.sp, TA[:], self.TA, csl, reads=[self.Tb], writes=[TA])
        K.dma(K.sp, TB[:], self.TB, csl, reads=[self.Tb], writes=[TB])
        K.dma(K.sp, keep[:], I["c_keep"].rearrange("p (t s) -> p t s", s=32), csl, writes=[keep])
        K.dma(K.sp, addm[:], I["c_add"].rearrange("p (t s) -> p t s", s=32), csl, writes=[addm])
        return (ksT, kwT, vsA, vwA, Eb, TA, TB, keep, addm, M0)

    def n4_phase(self, st, s, KcT, VcO, gates, res):
        K, nc, I = self.K, self.nc, self.I
        ksT, kwT, vsA, vwA, Eb, TA, TB, keep, addm, M0 = res
        qr = [K.sb(st, f"qr{i}", [128, 16, 128], BF16) for i in range(2)]
        zr = [K.sb(st, f"zr{i}", [128, 16, 128], BF16) for i in range(2)]
        cbr = [K.sb(st, f"cbr{i}", [128, 16, 128], F32) for i in range(2)]
        NPT = 4
        pTr = [K.sb(st, f"pT{i}", [128, 512], BF16) for i in range(NPT)]
        pTc = [K.sb(st, f"pTc{i}", [128, 512], BF16) for i in range(4)]
        scr = [K.sb(st, f"sc{i}", [128, 512], F32) for i in range(2)]
        posb = [K.sb(st, f"posb{i}", [128, 2, 2, 129], F32) for i in range(4)]
        ptmp = K.sb(st, "ptmp", [128, 2, 2, 128], F32)
        csb = [K.sb(st, f"csb{i}", [128, 2, 162], F32) for i in range(2)]
        otiles = [K.sb(st, f"otile{i}", [128, D], F32) for i in range(2)]
        ob = K.sb(st, "ob", [128, D], BF16)
        oT = [K.sb(st, f"oT{i}", [128, 16, 128], BF16) for i in range(1)]
        impns = [K.sb(st, f"impn{i}", [128, 4, 32], F32) for i in range(2)]
        t8s = [K.sb(st, f"t8{i}", [128, 4, 8], F32) for i in range(2)]
        selms = [K.sb(st, f"selm{i}", [128, 4, 32], F32) for i in range(2)]
        selTs = [K.sb(st, f"selT{i}", [128, 128], BF16) for i in range(2)]
        rcs = [K.sb(st, f"rc{i}", [128, 8], F32) for i in range(6)]
        qsl = [K.slot("ys0"), K.slot("ys1")]
        zsl = [K.slot("yb0"), K.slot("yb1")]
        bsl = [K.slot("wb0"), K.slot("wb1")]
        osl = [K.slot("o0"), K.slot("o1")]
        SCB = [self.P[0], self.P[1], self.P[7]]
        Pm = self.P[6]
        pm = self.pt[:, 6, :]
        po_sel = self.pt[:, 2:4, :]
        po_win = self.pt[:, 4:6, :]
        Psel = [self.P[2], self.P[3]]
        Pwin = [self.P[4], self.P[5]]

        def load_qc(tt):
            K.dma(K.sp, qr[tt % 2][:], self.QT[s, tt], qsl[tt % 2], reads=[self.QTb[s]], writes=[qr[tt % 2]])
            K.dma(K.sp, cbr[tt % 2][:], self.TC[120 - 8 * tt:248 - 8 * tt], bsl[tt % 2], reads=[self.Tb],
                  writes=[cbr[tt % 2]])

        def load_z(tt):
            K.dma(K.sp, zr[tt % 2][:], self.SZ[s, tt], zsl[tt % 2], reads=[self.SZb[s]], writes=[zr[tt % 2]])

        cnt = {"u": 0, "rc": 0, "pb": 0, "cs": 0}

        def cmp_half(g, half, T):
            otile = otiles[T % 2]
            impn = impns[T % 2]
            rc = rcs[cnt["rc"] % 6]
            cnt["rc"] += 1
            cs = csb[cnt["cs"] % 2]
            cnt["cs"] += 1
            K.op(K.dve, lambda: nc.vector.tensor_copy(out=cs[:], in_=pm[:, 0:324].rearrange("p (a c) -> p a c", a=2)),
                 reads=[Pm], writes=[cs])
            K.op(K.dve, lambda: nc.vector.tensor_scalar(out=rc[:, 0:2], in0=cs[:, :, 128], scalar1=1e-30,
                                                        scalar2=None, op0=ALU.max),
                 reads=[cs], writes=[rc])
            K.op(K.dve, lambda: nc.vector.reciprocal(out=rc[:, 0:2], in_=rc[:, 0:2]), reads=[rc], writes=[rc])
            h0 = 4 * g + 2 * half
            K.op(K.dve, lambda: nc.vector.tensor_tensor(out=rc[:, 4:6], in0=rc[:, 0:2],
                                                        in1=gates[:, T, h0 * 3:h0 * 3 + 4:3], op=ALU.mult),
                 reads=[rc, gates], writes=[rc])
            K.op(K.pool, lambda: nc.gpsimd.tensor_tensor(
                out=otile[:, h0 * 128:(h0 + 2) * 128].rearrange("p (a d) -> p a d", a=2), in0=cs[:, :, 0:128],
                in1=rc[:, 4:6].unsqueeze(2).broadcast_to([128, 2, 128]), op=ALU.mult),
                reads=[cs, rc], writes=[otile])
            for q2 in range(2):
                src = cs[:, q2, 130:162]
                if half == 0 and q2 == 0:
                    K.op(K.dve, lambda: nc.vector.tensor_scalar(out=impn[:, g, :], in0=src, scalar1=rc[:, 0:1],
                                                                scalar2=None, op0=ALU.mult),
                         reads=[cs, rc], writes=[impn])
                else:
                    K.op(K.dve, lambda: nc.vector.scalar_tensor_tensor(
                        out=impn[:, g, :], in0=src, scalar=rc[:, q2:q2 + 1], in1=impn[:, g, :],
                        op0=ALU.mult, op1=ALU.add), reads=[cs, rc, impn], writes=[impn])

        def fin_branch(po_view, Pb, g, T, branch):
            otile = otiles[T % 2]
            pb_ = posb[cnt["pb"] % 4]
            cnt["pb"] += 1
            rc = rcs[cnt["rc"] % 6]
            cnt["rc"] += 1
            K.op(K.dve, lambda: nc.vector.tensor_copy(
                out=pb_[:], in_=po_view[:, :, 0:258].rearrange("p a (b c) -> p a b c", c=129)),
                reads=Pb, writes=[pb_])
            K.op(K.dve, lambda: nc.vector.reciprocal(out=rc[:, 0:4].rearrange("p (a b) -> p a b", a=2),
                                                     in_=pb_[:, :, :, 128]),
                 reads=[pb_], writes=[rc])
            gcol = (4 * g) * 3 + branch
            K.op(K.dve, lambda: nc.vector.tensor_tensor(out=rc[:, 4:8], in0=rc[:, 0:4],
                                                        in1=gates[:, T, gcol:gcol + 10:3], op=ALU.mult),
                 reads=[rc, gates], writes=[rc])
            K.op(K.pool, lambda: nc.gpsimd.tensor_tensor(
                out=ptmp[:], in0=pb_[:, :, :, 0:128],
                in1=rc[:, 4:8].rearrange("p (a b) -> p a b", a=2).unsqueeze(3).broadcast_to([128, 2, 2, 128]),
                op=ALU.mult), reads=[pb_, rc], writes=[ptmp])
            og = otile[:, 4 * g * 128:(4 * g + 4) * 128].rearrange("p (a b d) -> p a b d", a=2, b=2)
            K.op(K.pool, lambda: nc.gpsimd.tensor_tensor(out=og, in0=og, in1=ptmp[:], op=ALU.add),
                 reads=[otile, ptmp], writes=[otile])

        if self.dbg < 5:
            return

        def topk_stages(T):
            impn, t8, selm, selT = impns[T % 2], t8s[T % 2], selms[T % 2], selTs[T % 2]

            def E():
                K.op(K.dve, lambda: nc.vector.tensor_tensor(out=impn[:], in0=impn[:],
                                                            in1=keep[:, T, :].unsqueeze(1).broadcast_to([128, 4, 32]),
                                                            op=ALU.mult),
                     reads=[impn, keep], writes=[impn])
                K.op(K.dve, lambda: nc.vector.tensor_tensor(out=impn[:], in0=impn[:],
                                                            in1=addm[:, T, :].unsqueeze(1).broadcast_to([128, 4, 32]),
                                                            op=ALU.add),
                     reads=[impn, addm], writes=[impn])

            def F():
                for g in range(4):
                    K.op(K.dve, lambda: nc.vector.max(out=t8[:, g, :], in_=impn[:, g, :]), reads=[impn], writes=[t8])

            def G():
                for g in range(4):
                    K.op(K.dve, lambda: nc.vector.tensor_scalar(out=selm[:, g, :], in0=impn[:, g, :],
                                                                scalar1=t8[:, g, 7:8], scalar2=-1.0,
                                                                op0=ALU.is_ge, op1=ALU.add),
                         reads=[impn, t8], writes=[selm])

            def H():
                K.op(K.pe, lambda: nc.tensor.transpose(out=pm[:, 0:128], in_=selm[:].rearrange("p g s -> p (g s)"),
                                                       identity=self.ident_f[:]),
                     reads=[selm, self.ident_f], writes=[Pm])

            def I_():
                K.op(K.dve, lambda: nc.vector.tensor_copy(out=selT[:], in_=pm[:, 0:128]), reads=[Pm], writes=[selT])

            return [E, F, G, H, I_]

        def final_stages(T):
            otile = otiles[T % 2]
            z_ = zr[T % 2]
            pv = pm.bitcast(BF16)
            o_ = oT[0]

            def cp():
                K.op(K.dve, lambda: nc.vector.tensor_copy(out=ob[:], in_=otile[:]), reads=[otile], writes=[ob])

            def tr(half):
                for c8 in range(8):
                    c = half * 8 + c8
                    K.op(K.pe, lambda: nc.tensor.transpose(out=pv[:, c8 * 128:(c8 + 1) * 128],
                                                           in_=ob[:, c * 128:(c + 1) * 128], identity=self.ident_b[:]),
                         reads=[ob, self.ident_b], writes=[Pm], inc=(c8 == 7))

            def mu(half):
                K.op(K.dve, lambda: nc.vector.tensor_tensor(
                    out=o_[:, half * 8:(half + 1) * 8, :], in0=pv[:].rearrange("p (c t) -> p c t", c=8),
                    in1=z_[:, half * 8:(half + 1) * 8, :], op=ALU.mult),
                    reads=[Pm, z_], writes=[o_])
                if half == 1:
                    K.dma(K.sp, self.WT[s, T], o_[:], osl[T % 2], reads=[o_], writes=[self.WTb[s]])

            return [(6, cp), (4, lambda: tr(0)), (2, lambda: mu(0)), (2, lambda: tr(1)), (2, lambda: mu(1))]

        def run_tile(T, with_units, Tc, fifo):
            allu = []
            if with_units:
                def spread(kcs):
                    rest_ = [k for k in kcs if k not in (T, T - 1)]
                    out = rest_[:1] + [T] + rest_[1:3] + ([T - 1] if T - 1 in kcs else []) + rest_[3:]
                    return out

                for g in range(4):
                    for br, kcs in (("s", list(range(T + 1))), ("w", list(range(max(0, T - 4), T + 1)))):
                        order = spread(kcs)
                        assert sorted(order) == kcs
                        for i_, kc in enumerate(order):
                            allu.append((g, br, kc, i_ == 0, i_ == len(order) - 1))
            if Tc is not None:
                n0 = len(allu)
                start = min(n0, max(6, n0 // 4))
                gap = max(1, (n0 - start) // 5)
                for g in range(4):
                    allu.insert(min(len(allu), start + g * (gap + 1)), (g, "c", None, True, True))
            N = len(allu)
            ustate = {}
            cdone = {"n": 0}
            since = {"n": 0}
            pvq = []

            def qk(n):
                g, br, kc, _f, _l = allu[n]
                u = cnt["u"]
                cnt["u"] += 1
                ustate[n] = u
                ps_ = SCB[u % 3]
                if br == "c":
                    q_ = qr[Tc % 2]
                    K.op(K.pe, lambda: nc.tensor.matmul(ps_[:], lhsT=KcT[:, g, :], rhs=q_[:, 4 * g:4 * g + 4, :],
                                                        start=True, stop=True),
                         reads=[KcT, q_], writes=[ps_])
                    return
                q_ = qr[T % 2]
                selT = selTs[T % 2]
                kT = ksT if br == "s" else kwT
                extra = (br == "s" and kc < T) or (br == "w" and kc == T - 4)
                K.op(K.pe, lambda: nc.tensor.matmul(ps_[:], lhsT=kT[:, g, kc * 128:(kc + 1) * 128],
                                                    rhs=q_[:, 4 * g:4 * g + 4, :], start=True, stop=not extra),
                     reads=[kT, q_], writes=[ps_], inc=not extra)
                if br == "s" and kc < T:
                    K.op(K.pe, lambda: nc.tensor.matmul(ps_[:], lhsT=Eb[:, g * 16 + kc, :],
                                                        rhs=selT[:].unsqueeze(1).broadcast_to([128, 4, 128]),
                                                        start=False, stop=True),
                         reads=[Eb, selT], writes=[ps_])
                elif br == "w" and kc == T - 4:
                    K.op(K.pe, lambda: nc.tensor.matmul(ps_[:], lhsT=self.ident_b[:],
                                                        rhs=M0[:].unsqueeze(1).broadcast_to([128, 4, 128]),
                                                        start=False, stop=True),
                         reads=[self.ident_b, M0], writes=[ps_])

            def rest_cmp(g, u):
                sc_, pT_ = scr[u % 2], pTc[g]
                K.op(K.act, lambda: nc.scalar.activation(out=pT_[:], in_=sc_[:], func=AF.Exp),
                     reads=[sc_], writes=[pT_])

                def pv(half):
                    for q2 in range(2):
                        r = half * 2 + q2
                        K.op(K.pe, lambda: nc.tensor.matmul(pm[:, q2 * 162:q2 * 162 + 162],
                                                            lhsT=pT_[:, r * 128:(r + 1) * 128], rhs=VcO[:, g, :],
                                                            start=True, stop=True),
                             reads=[pT_, VcO], writes=[Pm], inc=(q2 == 1))

                fifo.append((1, lambda: pv(0)))
                fifo.append((1, lambda: cmp_half(g, 0, Tc)))
                fifo.append((5, lambda: pv(1)))
                fifo.append((1, lambda: cmp_half(g, 1, Tc)))
                cdone["n"] += 1
                if cdone["n"] == 4:
                    fifo.extend(zip((4, 1, 1, 2, 2), topk_stages(Tc)))

            def pre(n):
                g, br, kc, _f, _l = allu[n]
                u = ustate[n]
                if br == "c":
                    tab = cbr[Tc % 2]
                else:
                    tab = TB if kc == T else (TA if kc == T - 1 else None)
                if tab is None:
                    return
                ps_, sc_ = SCB[u % 3], scr[u % 2]
                K.op(K.dve, lambda: nc.vector.tensor_tensor(
                    out=sc_[:], in0=ps_[:], in1=tab[:, 4 * g:4 * g + 4, :].rearrange("p h j -> p (h j)"),
                    op=ALU.add), reads=[ps_, tab], writes=[sc_])

            def rest(n):
                g, br, kc, first, last = allu[n]
                u = ustate.pop(n)
                if br == "c":
                    rest_cmp(g, u)
                    return
                ps_ = SCB[u % 3]
                pT_ = pTr[u % NPT]
                tab = TB if kc == T else (TA if kc == T - 1 else None)
                if tab is not None:
                    sc_ = scr[u % 2]
                    K.op(K.act, lambda: nc.scalar.activation(out=pT_[:], in_=sc_[:], func=AF.Exp),
                         reads=[sc_], writes=[pT_])
                else:
                    K.op(K.act, lambda: nc.scalar.activation(out=pT_[:], in_=ps_[:], func=AF.Exp),
                         reads=[ps_], writes=[pT_])
                pvq.append((g, br, kc, first, last, pT_))

            def pv_emit():
                if not pvq:
                    return
                g, br, kc, first, last, pT_ = pvq.pop(0)
                po_view, Pb, vA = (po_sel, Psel, vsA) if br == "s" else (po_win, Pwin, vwA)
                for r in range(4):
                    K.op(K.pe, lambda: nc.tensor.matmul(
                        po_view[:, r // 2, (r % 2) * 129:(r % 2) * 129 + 129],
                        lhsT=pT_[:, r * 128:(r + 1) * 128], rhs=vA[:, kc, g * 130:g * 130 + 129],
                        start=(first and r % 2 == 0), stop=last, skip_group_check=True),
                        reads=[pT_, vA], writes=[Pb[r // 2]], inc=(r == 3))
                if last:
                    fin_branch(po_view, Pb, g, T, 1 if br == "s" else 2)

            for n in range(N):
                if n == 0:
                    qk(0)
                    if N > 1:
                        qk(1)
                    pre(0)
                if n + 2 < N:
                    qk(n + 2)
                if n + 1 < N:
                    pre(n + 1)
                pv_emit()
                rest(n)
                since["n"] += 1
                if fifo and since["n"] >= fifo[0][0]:
                    fifo.pop(0)[1]()
                    since["n"] = 0
            pv_emit()
            while fifo:
                fifo.pop(0)[1]()

        load_qc(0)
        run_tile(0, False, 0, [])
        for tt in range(NT):
            if tt + 1 < NT:
                load_qc(tt + 1)
            load_z(tt)
            fifo = final_stages(tt - 1) if tt > 0 else []
            run_tile(tt, True, tt + 1 if tt + 1 < NT else None, fifo)
        for _d, f in final_stages(NT - 1):
            f()


def host_consts():
    c = {}
    c["c_ident"] = np.eye(128, dtype=np.float32)
    return c


def layer_inputs(inputs, layers):
    m = {}
    for l in layers:
        p = f"l{l}_"
        m[p + "norm"] = np.ascontiguousarray(inputs[p + "norm"].reshape(1, D))
        m[p + "w_out"] = inputs[p + "w_out"]
        m[p + "w_in"] = inputs[p + "w_in"]
        if l % 2 == 0:
            dw = inputs[p + "dw_w"].reshape(31, 16, 128).transpose(2, 1, 0)
            m[p + "dw_wT"] = np.ascontiguousarray(dw.reshape(128, 16 * 31))
            for nm in ("dw_b", "ln_g", "ln_b"):
                m[p + nm] = np.ascontiguousarray(inputs[p + nm].reshape(16, 128).T)
        else:
            m[p + "posT"] = np.ascontiguousarray(inputs[p + "cmp_pos"].T)
            for nm in ("ck_w1", "ck_w2", "cv_w1", "cv_w2"):
                m[p + nm] = inputs[p + nm]
    m["final_norm"] = np.ascontiguousarray(inputs["final_norm"].reshape(1, D))
    return m


_PROG_CACHE = {}


def run(inputs, layers=(0, 1, 2, 3), final_norm=True, ncores=NCORES, nseq=NSEQ, x_override=None, trace=False, dbg=99):
    key = (tuple(layers), final_norm, nseq, dbg)
    if key not in _PROG_CACHE:
        _PROG_CACHE[key] = Prog(layers, final_norm, nseq, dbg)
    prog = _PROG_CACHE[key]
    shared = dict(host_consts())
    shared.update(layer_inputs(inputs, layers))
    if any(l % 2 == 1 for l in layers):
        shared.update(nsa_host_tables(inputs["rel_bias"]))
    x = inputs["x"] if x_override is None else x_override
    in_maps = []
    for c in range(ncores):
        m = dict(shared)
        m["x"] = np.ascontiguousarray(x[c * nseq:(c + 1) * nseq])
        in_maps.append(m)
    res = run_bass_kernel_spmd(prog.nc, in_maps, core_ids=list(range(ncores)), **({'trace': True} if trace else {}))
    if trace:
        print('exec_time_ns', res.exec_time_ns)
    return np.concatenate([r["y"] for r in res.results], axis=0)


def _t5_bucket_np(dist):
    dist = np.maximum(dist, 0)
    d = np.maximum(dist, 16).astype(np.float32)
    large = 16 + (np.log(d / np.float32(16)) / np.float32(np.log(8.0)) * np.float32(16)).astype(np.int32)
    large = np.minimum(large, 31)
    return np.where(dist < 16, dist, large)


def nsa_host_tables(rel_bias):
    m = {}
    k = np.arange(128)[:, None]
    t = np.arange(128)[None, :]
    dA = t - k + 128
    dB = t - k
    mp = np.arange(248)[:, None]
    dC = t - 16 * (mp - 120) - 31
    m["rb_A"] = np.ascontiguousarray(rel_bias[_t5_bucket_np(dA)].transpose(0, 2, 1))
    m["rb_B"] = np.ascontiguousarray(rel_bias[_t5_bucket_np(dB)].transpose(0, 2, 1))
    m["rb_C"] = np.ascontiguousarray(rel_bias[_t5_bucket_np(dC)].transpose(0, 2, 1))
    m["rb_far"] = np.ascontiguousarray(rel_bias[31:32, :])
    m["m_B"] = np.where(dB >= 0, 0.0, NEGM).astype(np.float32)
    m["m_C"] = np.where(dC >= 0, 0.0, NEGM).astype(np.float32)
    m["m_W0"] = np.where(k > t, 0.0, NEGM).astype(np.float32)
    E = np.zeros((4, 32, 4, 16, 128), np.float32)
    for g in range(4):
        for kc in range(16):
            E[g, 2 * kc, g, kc, 0:64] = -NEGM
            E[g, 2 * kc + 1, g, kc, 64:128] = -NEGM
    m["c_E"] = np.ascontiguousarray(E.reshape(128, 64 * 128))
    tl = np.arange(128)[:, None, None]
    ti = np.arange(16)[None, :, None]
    sb = np.arange(32)[None, None, :]
    tabs = ti * 128 + tl
    cur = tabs // 64
    forced = (sb == 0) | (sb == cur) | (sb == cur - 1)
    causal = sb * 64 <= tabs
    keep = (causal & ~forced).astype(np.float32)
    add = np.where(causal, np.where(forced, 1e6, 0.0), -1e30).astype(np.float32)
    m["c_keep"] = np.ascontiguousarray(keep.reshape(128, 512))
    m["c_add"] = np.ascontiguousarray(add.reshape(128, 512))
    ov = np.zeros((128, 34), np.float32)
    n = np.arange(127)[:, None]
    s0 = np.arange(32)[None, :] * 64
    j0 = n * 16
    ov[:127, 0] = 1.0
    ov[:127, 2:34] = ((j0 < s0 + 64) & (j0 + 32 > s0)).astype(np.float32)
    m["c_ov"] = ov
    return m


def kernel(**inputs):
    inputs = {k: np.asarray(v) for k, v in inputs.items()}
    return run(inputs)
```

```python
import numpy as np
from contextlib import ExitStack
import concourse.bass as bass
import concourse.mybir as mybir
from concourse.bass_utils import run_bass_kernel_spmd

F32 = mybir.dt.float32
BF16 = mybir.dt.bfloat16
AF = mybir.ActivationFunctionType
ALU = mybir.AluOpType
AX = mybir.AxisListType

D = 2048
S = 2048
NT = S // 128
NSEQ = 2
NCORES = 8
NH = 16
NG = 4
DK = 128
EPS = 1e-6
NEGM = -30000.0
CONF_W = 3 * D
NSA_W = 2048 + 6 * 512 + 48 + 2048


class Owner:
    def __init__(self, K, name):
        self.sem = K.stack.enter_context(K.nc.semaphore(name))
        self.count = 0
        self.name = name


class Eng(Owner):
    def __init__(self, K, name, h):
        super().__init__(K, "e_" + name)
        self.h = h
        self.seen = {}
        self.is_pe = name == "pe"
        self.last_inc = True


class Buf:
    def __init__(self, ap, name=""):
        self.ap = ap
        self.name = name
        self.writers = {}
        self.readers = {}
        self.prev = {}
        self.excl = False

    def __getitem__(self, idx):
        return self.ap[idx]


def _merge(d, own, val):
    if d.get(own, 0) < val:
        d[own] = val


class Kern:
    def __init__(self, nc):
        self.nc = nc
        self.stack = ExitStack()
        self.pe = Eng(self, "pe", nc.tensor)
        self.act = Eng(self, "act", nc.scalar)
        self.dve = Eng(self, "dve", nc.vector)
        self.pool = Eng(self, "pool", nc.gpsimd)
        self.sp = Eng(self, "sp", nc.sync)
        self.engs = [self.pe, self.act, self.dve, self.pool, self.sp]
        self.slots = {}
        self.n_dram = 0

    def slot(self, name):
        if name not in self.slots:
            self.slots[name] = Owner(self, "d_" + name)
        return self.slots[name]

    def sb(self, st, name, shape, dt):
        self.n_dram += 1
        nm = f"{name}_{self.n_dram}"
        return Buf(st.enter_context(self.nc.sbuf_tensor(nm, list(shape), dt)), nm)

    def dram(self, name, shape, dt, kind=None):
        if kind is None:
            t = self.nc.dram_tensor(name, list(shape), dt)
        else:
            t = self.nc.dram_tensor(name, list(shape), dt, kind=kind)
        return t.ap()

    def _deps(self, eng, reads, writes):
        deps = {}

        def add(d, raw):
            for own, val in d.items():
                if own is eng and (eng.is_pe or not raw or val > eng.count):
                    continue
                _merge(deps, own, val)

        for b in reads:
            add(b.writers, True)
            if b.excl:
                add(b.readers, False)
        for b in writes:
            add(b.writers, False)
            add(b.readers, False)
            add(b.prev, False)
        return [(o, v) for o, v in deps.items() if eng.seen.get(o, 0) < v]

    def _emit_waits(self, eng, need, fn):
        for o, v in need[1:]:
            eng.h.wait_ge(o.sem, v)
        ins = fn()
        if need:
            ins._wait_ge(need[0][0].sem, need[0][1])
        for o, v in need:
            eng.seen[o] = v
        return ins

    def _update(self, ev, reads, writes):
        own, val = ev
        for b in reads:
            _merge(b.readers, own, val)
        for b in writes:
            if b.readers:
                prev = dict(b.readers)
                for o, v in b.writers.items():
                    _merge(prev, o, v)
                b.prev = prev
                b.readers = {}
                b.writers = {}
            _merge(b.writers, own, val)

    def op(self, eng, fn, reads=(), writes=(), inc=True):
        need = self._deps(eng, reads, writes)
        ins = self._emit_waits(eng, need, fn)
        if inc:
            eng.count += 1
            ins.then_inc(eng.sem, 1)
            ev = (eng, eng.count)
            eng.last_inc = True
        else:
            ev = (eng, eng.count + 1)
            eng.last_inc = False
        self._update(ev, reads, writes)
        return ins

    def dma(self, q, out, in_, slot, reads=(), writes=(), group=False, **kw):
        need = self._deps(q, reads, writes)
        if not group and slot.count > 0 and q.seen.get(slot, 0) < slot.count:
            need = [(o, v) for o, v in need if o is not slot] + [(slot, slot.count)]
        ins = self._emit_waits(q, need, lambda: q.h.dma_start(out=out, in_=in_, **kw))
        slot.count += 16
        ins.then_inc(slot.sem, 16)
        self._update((slot, slot.count), reads, writes)
        return ins

    def barrier(self):
        assert all(e.last_inc for e in self.engs)
        owners = list(self.engs) + list(self.slots.values())
        for e in self.engs:
            for o in owners:
                if o is e or o.count == 0:
                    continue
                if e.seen.get(o, 0) < o.count:
                    e.h.wait_ge(o.sem, o.count)
                    e.seen[o] = o.count


class Prog:
    def __init__(self, layers=(0, 1, 2, 3), final_norm=True, nseq=NSEQ, dbg=99):
        self.dbg = dbg
        self.layers = list(layers)
        self.final_norm = final_norm
        self.nseq = nseq
        nc = bass.Bass("TRN2", target_bir_lowering=False)
        self.nc = nc
        self.K = Kern(nc)
        self.build()

    def declare_inputs(self):
        K = self.K
        ns = self.nseq
        I = {}

        def inp(name, shape, dt=F32):
            I[name] = K.dram(name, shape, dt, kind="ExternalInput")

        inp("x", [ns, S, D])
        inp("c_ident", [128, 128])
        for l in self.layers:
            p = f"l{l}_"
            inp(p + "norm", [1, D])
            inp(p + "w_out", [D, D])
            if l % 2 == 0:
                inp(p + "w_in", [D, CONF_W])
                inp(p + "dw_wT", [128, 16 * 31])
                inp(p + "dw_b", [128, 16])
                inp(p + "ln_g", [128, 16])
                inp(p + "ln_b", [128, 16])
            else:
                inp(p + "w_in", [D, NSA_W])
                inp(p + "posT", [128, 32])
                inp(p + "ck_w1", [4096, 128])
                inp(p + "ck_w2", [128, 128])
                inp(p + "cv_w1", [4096, 128])
                inp(p + "cv_w2", [128, 128])
        if any(l % 2 == 1 for l in self.layers):
            inp("rb_A", [128, 16, 128])
            inp("rb_B", [128, 16, 128])
            inp("rb_C", [248, 16, 128])
            inp("rb_far", [1, 16])
            inp("m_B", [128, 128])
            inp("m_C", [248, 128])
            inp("m_W0", [128, 128])
            inp("c_E", [128, 64 * 128])
            inp("c_keep", [128, 16 * 32])
            inp("c_add", [128, 16 * 32])
            inp("c_ov", [128, 34])
        inp("final_norm", [1, D])
        self.I = I
        self.out = K.dram("y", [ns, S, D], F32, kind="ExternalOutput")

    def build(self):
        K = self.K
        nc = self.nc
        ns = self.nseq
        self.declare_inputs()
        I = self.I
        with K.stack:
            st = K.stack
            self.ident_f = K.sb(st, "ident_f", [128, 128], F32)
            self.ident_b = K.sb(st, "ident_b", [128, 128], BF16)
            self.ones_f = K.sb(st, "ones_f", [128, 128], F32)
            self.gbc = K.sb(st, "gbc", [128, D], F32)
            self.gfin = K.sb(st, "gfin", [128, D], F32)
            self.small = K.sb(st, "small", [128, 64], F32)
            pt = st.enter_context(nc.psum_tensor("psum", [128, 8, 512], F32))
            self.pt = pt
            self.P = [Buf(pt[:, b, :], f"ps{b}") for b in range(8)]
            for b_ in self.P:
                b_.excl = True
            sl = K.slot("const")
            K.dma(K.sp, self.ident_f[:], I["c_ident"], sl, writes=[self.ident_f])
            K.op(K.act, lambda: nc.scalar.copy(out=self.ident_b[:], in_=self.ident_f[:]),
                 reads=[self.ident_f], writes=[self.ident_b])
            K.op(K.pool, lambda: nc.gpsimd.memset(self.ones_f[:], 1.0), writes=[self.ones_f])
            K.dma(K.sp, self.gfin[:], bass.AP(I["final_norm"].tensor, 0, [[0, 128], [1, D]]), sl,
                  writes=[self.gfin])
            K.op(K.pool, lambda: nc.gpsimd.memset(self.small[:, 0:1], EPS), writes=[self.small])
            nl = len(self.layers)
            self.X = [I["x"]]
            for i in range(nl - 1):
                self.X.append(K.dram(f"xs{i}", [ns, S, D], F32))
            self.X.append(self.out)
            self.Xb = [[[Buf(None, f"X{i}_{s}_{t}") for t in range(NT)] for s in range(ns)]
                       for i in range(nl + 1)]
            self.WT = K.dram("wt_s", [ns, NT, 128, 16, 128], BF16)
            self.WTb = [Buf(None, f"WT{s}") for s in range(ns)]
            self.Ys = K.dram("y_s", [ns, 16, 128, S], F32)
            self.Yb = [[Buf(None, f"Y{s}_{c}") for c in range(16)] for s in range(ns)]
            if any(l % 2 == 1 for l in self.layers):
                self.QT = K.dram("qt_s", [ns, NT, 128, 16, 128], BF16)
                self.SZ = K.dram("sz_s", [ns, NT, 128, 16, 128], BF16)
                self.KT = K.dram("kt_s", [ns, 4, 4, 128, S], BF16)
                self.VA = K.dram("va_s", [ns, 2, NT, 128, 520], BF16)
                self.TA = K.dram("ta_s", [128, 16, 128], F32)
                self.TB = K.dram("tb_s", [128, 16, 128], F32)
                self.TC = K.dram("tc_s", [248, 16, 128], F32)
                self.QTb = [Buf(None, f"QT{s}") for s in range(ns)]
                self.SZb = [Buf(None, f"SZ{s}") for s in range(ns)]
                self.KTb = [Buf(None, f"KT{s}") for s in range(ns)]
                self.VAb = [Buf(None, f"VA{s}") for s in range(ns)]
                self.Tb = Buf(None, "Ttab")
                self.nsa_setup()
            K.barrier()
            for li, l in enumerate(self.layers):
                last = (li == nl - 1) and self.final_norm
                sl = K.slot("const")
                K.dma(K.sp, self.gbc[:], bass.AP(I[f"l{l}_norm"].tensor, 0, [[0, 128], [1, D]]), sl,
                      writes=[self.gbc])
                if l % 2 == 0:
                    self.conformer_layer(li, l)
                else:
                    self.nsa_layer(li, l)
                self.g2_phase(li, l, last)
            K.barrier()

    def p1_phase(self, st, li, s, hT):
        K, nc = self.K, self.nc
        xr = [K.sb(st, f"p1x{i}", [128, D], F32) for i in range(3)]
        hb = [K.sb(st, f"p1h{i}", [128, D], BF16) for i in range(2)]
        junk = K.sb(st, "p1junk", [128, D], BF16)
        stat = [K.sb(st, f"p1s{i}", [128, 4], F32) for i in range(2)]
        X = self.X[li]
        xsl = [K.slot("x0"), K.slot("x1"), K.slot("t0")]
        hTw = [Buf(None, f"hTw{i}") for i in range(2 * NT)]

        def load(tt):
            K.dma(K.sp, xr[tt % 3][:], X[s, tt * 128:(tt + 1) * 128, :], xsl[tt % 3],
                  reads=[self.Xb[li][s][tt]], writes=[xr[tt % 3]])

        def stage_a(tt):
            x, h, sm = xr[tt % 3], hb[tt % 2], stat[tt % 2]
            K.op(K.act, lambda: nc.scalar.activation(out=junk[:], in_=x[:], func=AF.Square,
                                                     accum_out=sm[:, 0:1]),
                 reads=[x], writes=[junk, sm])
            K.op(K.act, lambda: nc.scalar.activation(out=sm[:, 1:2], in_=sm[:, 0:1], func=AF.Sqrt,
                                                     scale=1.0 / D, bias=self.eps_ap()),
                 reads=[sm, self.small], writes=[sm])
            K.op(K.dve, lambda: nc.vector.reciprocal(out=sm[:, 2:3], in_=sm[:, 1:2]),
                 reads=[sm], writes=[sm])
            K.op(K.dve, lambda: nc.vector.scalar_tensor_tensor(
                out=h[:], in0=x[:], scalar=sm[:, 2:3], in1=self.gbc[:], op0=ALU.mult, op1=ALU.mult),
                reads=[x, sm, self.gbc], writes=[h])

        def stage_b(tt):
            h = hb[tt % 2]
            pb = (tt % 2) * 2
            pv = self.pt[:, pb:pb + 2, :].bitcast(BF16)
            for c in range(16):
                bank = self.P[pb + c // 8]
                o = pv[:, c // 8, (c % 8) * 128:(c % 8 + 1) * 128]
                K.op(K.pe, lambda: nc.tensor.transpose(out=o, in_=h[:, c * 128:(c + 1) * 128],
                                                       identity=self.ident_b[:]),
                     reads=[h, self.ident_b], writes=[bank], inc=(c % 8 == 7))
            for half in range(2):
                src = pv[:, half, :].rearrange("p (c t) -> p c t", c=8)
                dst = hT[:, half * 8:(half + 1) * 8, tt * 128:(tt + 1) * 128]
                if half == 0:
                    K.op(K.act, lambda: nc.scalar.copy(out=dst, in_=src), reads=[self.P[pb]],
                         writes=[hTw[2 * tt]])
                else:
                    K.op(K.dve, lambda: nc.vector.tensor_copy(out=dst, in_=src), reads=[self.P[pb + 1]],
                         writes=[hTw[2 * tt + 1]])

        load(0)
        load(1)
        stage_a(0)
        for tt in range(NT):
            if tt + 2 < NT:
                load(tt + 2)
            if tt + 1 < NT:
                stage_a(tt + 1)
            stage_b(tt)

    def eps_ap(self):
        return self.small[:, 0:1]

    def load_w_cast(self, dst_buf, dst_ap, w_ap, col0, ncols, slot, group=False):
        K = self.K
        src = w_ap.rearrange("(k p) n -> p k n", p=128)[:, :, col0:col0 + ncols]
        K.dma(K.pool, dst_ap, src, slot, writes=[dst_buf], group=group)

    def g2_phase(self, li, l, last):
        K, nc = self.K, self.nc
        I = self.I
        with ExitStack() as st:
            wo = K.sb(st, "g2w", [128, 16, D], BF16)
            wsl = K.slot("wo")
            for j in range(4):
                self.load_w_cast(wo, wo[:, :, j * 512:(j + 1) * 512], I[f"l{l}_w_out"], j * 512, 512, wsl,
                                 group=(j > 0))
            xr = [K.sb(st, f"g2x{i}", [128, D], F32) for i in range(2)]
            xo = [K.sb(st, f"g2o{i}", [128, D], F32) for i in range(2)]
            wt = [K.sb(st, f"g2t{i}", [128, 16, 128], BF16) for i in range(2)]
            junk = K.sb(st, "g2junk", [128, D], BF16)
            stat = [K.sb(st, f"g2s{i}", [128, 4], F32) for i in range(2)]
            xsl = [K.slot("x0"), K.slot("x1")]
            tsl = [K.slot("t0"), K.slot("t1")]
            osl = [K.slot("o0"), K.slot("o1")]
            Xi, Xo = self.X[li], self.X[li + 1]
            steps = [(s, tt) for s in range(self.nseq) for tt in range(NT)]

            def load(i):
                s, tt = steps[i]
                K.dma(K.sp, xr[i % 2][:], Xi[s, tt * 128:(tt + 1) * 128, :], xsl[i % 2],
                      reads=[self.Xb[li][s][tt]], writes=[xr[i % 2]])
                K.dma(K.sp, wt[i % 2][:], self.WT[s, tt], tsl[i % 2], reads=[self.WTb[s]], writes=[wt[i % 2]])

            load(0)
            for i, (s, tt) in enumerate(steps):
                if i + 1 < len(steps):
                    load(i + 1)
                x, o, w = xr[i % 2], xo[i % 2], wt[i % 2]
                for db in range(4):
                    bank = self.P[(i * 4 + db) % 2]
                    for c in range(16):
                        K.op(K.pe, lambda: nc.tensor.matmul(bank[:], lhsT=w[:, c, :],
                                                            rhs=wo[:, c, db * 512:(db + 1) * 512],
                                                            start=(c == 0), stop=(c == 15)),
                             reads=[w, wo], writes=[bank], inc=(c == 15))
                    K.op(K.dve, lambda: nc.vector.tensor_tensor(out=o[:, db * 512:(db + 1) * 512], in0=bank[:],
                                                                in1=x[:, db * 512:(db + 1) * 512], op=ALU.add),
                         reads=[bank, x], writes=[o])
                if last:
                    sm = stat[i % 2]
                    K.op(K.act, lambda: nc.scalar.activation(out=junk[:], in_=o[:], func=AF.Square,
                                                             accum_out=sm[:, 0:1]),
                         reads=[o], writes=[junk, sm])
                    K.op(K.act, lambda: nc.scalar.activation(out=sm[:, 1:2], in_=sm[:, 0:1], func=AF.Sqrt,
                                                             scale=1.0 / D, bias=self.eps_ap()),
                         reads=[sm, self.small], writes=[sm])
                    K.op(K.dve, lambda: nc.vector.reciprocal(out=sm[:, 2:3], in_=sm[:, 1:2]),
                         reads=[sm], writes=[sm])
                    K.op(K.dve, lambda: nc.vector.scalar_tensor_tensor(
                        out=o[:], in0=o[:], scalar=sm[:, 2:3], in1=self.gfin[:], op0=ALU.mult, op1=ALU.mult),
                        reads=[o, sm, self.gfin], writes=[o])
                K.dma(K.sp, Xo[s, tt * 128:(tt + 1) * 128, :], o[:], osl[i % 2], reads=[o],
                      writes=[self.Xb[li + 1][s][tt]])
            K.barrier()

    def conformer_layer(self, li, l):
        K, nc = self.K, self.nc
        I = self.I
        p = f"l{l}_"
        W = I[p + "w_in"]
        for s in range(self.nseq):
            with ExitStack() as st:
                hT = K.sb(st, "hT", [128, 16, S], BF16)
                with ExitStack() as st1:
                    self.p1_phase(st1, li, s, hT)
                    K.barrier()
                dwT = K.sb(st, "dwT", [128, 16, 31], F32)
                dwb = K.sb(st, "dwb", [128, 16], F32)
                lng = K.sb(st, "lng", [128, 16], F32)
                lnb = K.sb(st, "lnb", [128, 16], F32)
                csl = K.slot("const")
                K.dma(K.sp, dwT[:], I[p + "dw_wT"].rearrange("p (c k) -> p c k", c=16), csl, writes=[dwT])
                K.dma(K.sp, dwb[:], I[p + "dw_b"], csl, writes=[dwb])
                K.dma(K.sp, lng[:], I[p + "ln_g"], csl, writes=[lng])
                K.dma(K.sp, lnb[:], I[p + "ln_b"], csl, writes=[lnb])
                S1 = K.sb(st, "S1", [128, S], F32)
                S2 = K.sb(st, "S2", [128, S], F32)
                wa = [K.sb(st, f"wa{i}", [128, 16, 128], BF16) for i in range(2)]
                wb = [K.sb(st, f"wb{i}", [128, 16, 128], BF16) for i in range(2)]
                wasl = [K.slot("wa0"), K.slot("wa1")]
                wbsl = [K.slot("wb0"), K.slot("wb1")]
                with ExitStack() as st2:
                    v = [K.sb(st2, f"v{i}", [128, 30 + S], BF16) for i in range(2)]
                    dg = [K.sb(st2, f"dg{i}", [128, 31, 128], BF16) for i in range(2)]
                    sg = [K.sb(st2, f"sg{i}", [128, 512], F32) for i in range(2)]
                    ys = [K.sb(st2, f"ys{i}", [128, 512], F32) for i in range(2)]
                    yq = [K.sb(st2, f"yq{i}", [128, 512], F32) for i in range(2)]
                    ysl = [K.slot("ys0"), K.slot("ys1")]
                    for i in range(2):
                        K.op(K.pool, lambda: nc.gpsimd.memset(v[i][:, 0:30], 0.0), writes=[v[i]])

                    def loadw(c):
                        self.load_w_cast(wa[c % 2], wa[c % 2][:], W, c * 128, 128, wasl[c % 2])
                        self.load_w_cast(wb[c % 2], wb[c % 2][:], W, D + c * 128, 128, wbsl[c % 2])

                    def gemm_glu(c, tb):
                        j = c * 4 + tb
                        pa, pb = self.P[(j % 2) * 3], self.P[(j % 2) * 3 + 1]
                        for (ps_, w_) in ((pa, wa[c % 2]), (pb, wb[c % 2])):
                            for k in range(16):
                                K.op(K.pe, lambda: nc.tensor.matmul(ps_[:], lhsT=w_[:, k, :],
                                                                    rhs=hT[:, k, tb * 512:(tb + 1) * 512],
                                                                    start=(k == 0), stop=(k == 15)),
                                     reads=[w_, hT], writes=[ps_], inc=(k == 15))
                        g_ = sg[j % 2]
                        K.op(K.act, lambda: nc.scalar.activation(out=g_[:], in_=pb[:], func=AF.Sigmoid),
                             reads=[pb], writes=[g_])
                        vv = v[c % 2]
                        K.op(K.dve, lambda: nc.vector.tensor_tensor(
                            out=vv[:, 30 + tb * 512:30 + (tb + 1) * 512], in0=pa[:], in1=g_[:], op=ALU.mult),
                            reads=[pa, g_], writes=[vv])

                    KD = 8

                    def conv(c, tb):
                        j = c * 4 + tb
                        py = self.P[(j % 2) * 3 + 2]
                        vv, dd = v[c % 2], dg[c % 2]
                        for k in range(KD, 31):
                            K.op(K.pe, lambda: nc.tensor.matmul(py[:], lhsT=dd[:, k, :],
                                                                rhs=vv[:, k + tb * 512:k + (tb + 1) * 512],
                                                                start=(k == KD), stop=(k == 30)),
                                 reads=[dd, vv], writes=[py], inc=(k == 30))
                        y_, q_ = ys[j % 2], yq[j % 2]
                        K.op(K.dve, lambda: nc.vector.tensor_scalar(
                            out=y_[:], in0=vv[:, tb * 512:(tb + 1) * 512], scalar1=dwT[:, c, 0:1],
                            scalar2=dwb[:, c:c + 1], op0=ALU.mult, op1=ALU.add),
                            reads=[vv, dwT, dwb], writes=[y_])
                        for k in range(1, KD):
                            K.op(K.dve, lambda: nc.vector.scalar_tensor_tensor(
                                out=y_[:], in0=vv[:, k + tb * 512:k + (tb + 1) * 512], scalar=dwT[:, c, k:k + 1],
                                in1=y_[:], op0=ALU.mult, op1=ALU.add),
                                reads=[vv, dwT, y_], writes=[y_], inc=(k == KD - 1))
                        K.op(K.dve, lambda: nc.vector.tensor_tensor(out=y_[:], in0=py[:], in1=y_[:], op=ALU.add),
                             reads=[py, y_], writes=[y_])
                        K.op(K.act, lambda: nc.scalar.activation(out=q_[:], in_=y_[:], func=AF.Square),
                             reads=[y_], writes=[q_])
                        sl_ = slice(tb * 512, (tb + 1) * 512)
                        if c == 0:
                            K.op(K.pool, lambda: nc.gpsimd.tensor_copy(out=S1[:, sl_], in_=y_[:]),
                                 reads=[y_], writes=[S1])
                            K.op(K.pool, lambda: nc.gpsimd.tensor_copy(out=S2[:, sl_], in_=q_[:]),
                                 reads=[q_], writes=[S2])
                        else:
                            K.op(K.pool, lambda: nc.gpsimd.tensor_tensor(out=S1[:, sl_], in0=S1[:, sl_],
                                                                         in1=y_[:], op=ALU.add),
                                 reads=[y_, S1], writes=[S1])
                            K.op(K.pool, lambda: nc.gpsimd.tensor_tensor(out=S2[:, sl_], in0=S2[:, sl_],
                                                                         in1=q_[:], op=ALU.add),
                                 reads=[q_, S2], writes=[S2])
                        K.dma(K.sp, self.Ys[s, c, :, sl_], y_[:], ysl[j % 2], reads=[y_], writes=[self.Yb[s][c]])

                    def mkdiag(c):
                        dd = dg[c % 2]
                        K.op(K.pool, lambda: nc.gpsimd.tensor_tensor(
                            out=dd[:],
                            in0=self.ident_b[:].unsqueeze(1).broadcast_to([128, 31, 128]),
                            in1=dwT[:, c, :].unsqueeze(2).broadcast_to([128, 31, 128]),
                            op=ALU.mult),
                            reads=[self.ident_b, dwT], writes=[dd])

                    steps = [(c, tb) for c in range(16) for tb in range(4)]
                    loadw(0)
                    mkdiag(0)
                    gemm_glu(0, 0)
                    for i, (c, tb) in enumerate(steps):
                        if tb == 0 and c + 1 < 16:
                            loadw(c + 1)
                            mkdiag(c + 1)
                        if i + 1 < len(steps):
                            gemm_glu(*steps[i + 1])
                        conv(c, tb)
                    for tb in range(4):
                        sl_ = slice(tb * 512, (tb + 1) * 512)
                        p1, p2 = self.P[6], self.P[7]
                        K.op(K.pe, lambda: nc.tensor.matmul(p1[:], lhsT=self.ones_f[:], rhs=S1[:, sl_],
                                                            start=True, stop=True),
                             reads=[self.ones_f, S1], writes=[p1])
                        K.op(K.pe, lambda: nc.tensor.matmul(p2[:], lhsT=self.ones_f[:], rhs=S2[:, sl_],
                                                            start=True, stop=True),
                             reads=[self.ones_f, S2], writes=[p2])
                        K.op(K.act, lambda: nc.scalar.mul(out=S1[:, sl_], in_=p1[:], mul=1.0 / D),
                             reads=[p1], writes=[S1])
                        t_ = sg[0]
                        K.op(K.dve, lambda: nc.vector.tensor_tensor(out=t_[:], in0=S1[:, sl_], in1=S1[:, sl_],
                                                                    op=ALU.mult),
                             reads=[S1], writes=[t_])
                        K.op(K.dve, lambda: nc.vector.scalar_tensor_tensor(
                            out=t_[:], in0=p2[:], scalar=1.0 / D, in1=t_[:], op0=ALU.mult, op1=ALU.subtract),
                            reads=[p2, t_], writes=[t_])
                        K.op(K.act, lambda: nc.scalar.activation(out=t_[:], in_=t_[:], func=AF.Sqrt,
                                                                 bias=self.eps_ap()),
                             reads=[t_, self.small], writes=[t_])
                        K.op(K.dve, lambda: nc.vector.reciprocal(out=S2[:, sl_], in_=t_[:]),
                             reads=[t_], writes=[S2])
                    K.barrier()
                with ExitStack() as st3:
                    yb = [K.sb(st3, f"yb{i}", [128, S], F32) for i in range(2)]
                    wTb = [K.sb(st3, f"wTb{i}", [128, S], BF16) for i in range(2)]
                    szb = [K.sb(st3, f"szb{i}", [128, 512], F32) for i in range(2)]
                    s1b = [K.sb(st3, f"s1b{i}", [128, 512], F32) for i in range(2)]
                    ybsl = [K.slot("yb0"), K.slot("yb1")]
                    wtsl = [K.slot("wt0"), K.slot("wt1")]

                    def load3(c):
                        self.load_w_cast(wa[c % 2], wa[c % 2][:], W, 2 * D + c * 128, 128, wasl[c % 2])
                        K.dma(K.sp, yb[c % 2][:], self.Ys[s, c], ybsl[c % 2], reads=[self.Yb[s][c]],
                              writes=[yb[c % 2]])

                    load3(0)
                    for c in range(16):
                        if c + 1 < 16:
                            load3(c + 1)
                        y_, w_ = yb[c % 2], wTb[c % 2]
                        for tb in range(4):
                            j = c * 4 + tb
                            sl_ = slice(tb * 512, (tb + 1) * 512)
                            pz = self.P[j % 2]
                            for k in range(16):
                                K.op(K.pe, lambda: nc.tensor.matmul(pz[:], lhsT=wa[c % 2][:, k, :], rhs=hT[:, k, sl_],
                                                                    start=(k == 0), stop=(k == 15)),
                                     reads=[wa[c % 2], hT], writes=[pz], inc=(k == 15))
                            z_, a_ = szb[j % 2], s1b[j % 2]
                            K.op(K.act, lambda: nc.scalar.activation(out=z_[:], in_=pz[:], func=AF.Silu),
                                 reads=[pz], writes=[z_])
                            K.op(K.dve, lambda: nc.vector.tensor_tensor(out=y_[:, sl_], in0=y_[:, sl_],
                                                                        in1=S1[:, sl_], op=ALU.subtract),
                                 reads=[y_, S1], writes=[y_])
                            K.op(K.dve, lambda: nc.vector.tensor_tensor(out=y_[:, sl_], in0=y_[:, sl_],
                                                                        in1=S2[:, sl_], op=ALU.mult),
                                 reads=[y_, S2], writes=[y_])
                            K.op(K.act, lambda: nc.scalar.activation(out=a_[:], in_=y_[:, sl_], func=AF.Silu,
                                                                     scale=lng[:, c:c + 1], bias=lnb[:, c:c + 1]),
                                 reads=[y_, lng, lnb], writes=[a_])
                            K.op(K.dve, lambda: nc.vector.tensor_tensor(out=w_[:, sl_], in0=a_[:], in1=z_[:],
                                                                        op=ALU.mult),
                                 reads=[a_, z_], writes=[w_])
                        dst = self.WT[s].rearrange("t p c j -> p t c j")[:, :, c, :]
                        K.dma(K.sp, dst, w_[:].rearrange("p (t j) -> p t j", t=NT), wtsl[c % 2], reads=[w_],
                              writes=[self.WTb[s]])
                    K.barrier()

    def nsa_setup(self):
        K, nc, I = self.K, self.nc, self.I
        with ExitStack() as st:
            far = K.sb(st, "far", [128, 16], F32)
            sl = K.slot("const")
            K.dma(K.sp, far[:], bass.AP(I["rb_far"].tensor, 0, [[0, 128], [1, 16]]), sl, writes=[far])
            jobs = [("rb_A", None, self.TA, 0, 128), ("rb_B", "m_B", self.TB, 0, 128),
                    ("rb_C", "m_C", self.TC, 0, 128), ("rb_C", "m_C", self.TC, 128, 120)]
            for ji, (rn, mn, dst, r0, nr) in enumerate(jobs):
                raw = K.sb(st, f"raw{ji}", [128, 16, 128], F32)
                K.dma(K.sp, raw[0:nr], I[rn][r0:r0 + nr], K.slot(f"x{ji % 2}"), writes=[raw])
                K.op(K.dve, lambda: nc.vector.tensor_tensor(
                    out=raw[0:nr], in0=raw[0:nr], in1=far[0:nr].unsqueeze(2).broadcast_to([nr, 16, 128]),
                    op=ALU.subtract), reads=[raw, far], writes=[raw])
                if mn is not None:
                    mk = K.sb(st, f"mk{ji}", [128, 128], F32)
                    K.dma(K.sp, mk[0:nr], I[mn][r0:r0 + nr], K.slot(f"t{ji % 2}"), writes=[mk])
                    K.op(K.dve, lambda: nc.vector.tensor_tensor(
                        out=raw[0:nr], in0=raw[0:nr], in1=mk[0:nr].unsqueeze(1).broadcast_to([nr, 16, 128]),
                        op=ALU.add), reads=[raw, mk], writes=[raw])
                K.dma(K.sp, dst[r0:r0 + nr], raw[0:nr], K.slot(f"o{ji % 2}"), reads=[raw], writes=[self.Tb])
            K.barrier()

    def nsa_layer(self, li, l):
        K, nc, I = self.K, self.nc, self.I
        p = f"l{l}_"
        W = I[p + "w_in"]
        for s in range(self.nseq):
            with ExitStack() as so:
                gates = K.sb(so, "gates", [128, NT, 48], F32)
                KcT = K.sb(so, "KcT", [128, 4, 128], BF16)
                VcO = K.sb(so, "VcO", [128, 4, 162], BF16)
                with ExitStack() as st:
                    hT = K.sb(st, "hT", [128, 16, S], BF16)
                    with ExitStack() as st1:
                        self.p1_phase(st1, li, s, hT)
                        K.barrier()
                    if self.dbg >= 2:
                        self.n2_phase(st, s, W, hT, gates, (p, KcT, VcO))
                    K.barrier()
                with ExitStack() as st:
                    res = self.n4_residents(st, s)
                    if self.dbg >= 4:
                        self.n4_phase(st, s, KcT, VcO, gates, res)
                    K.barrier()

    def n2_phase(self, st, s, W, hT, gates, n3args):
        K, nc = self.K, self.nc
        wst = [K.sb(st, f"wst{i}", [128, 16, 128], BF16) for i in range(3)]
        wsl = [K.slot(f"wa{i}") for i in range(2)] + [K.slot("wb0")]
        stg = [K.sb(st, f"stg{i}", [128, S], BF16) for i in range(2)]
        gsl = [K.slot("ys0"), K.slot("ys1")]
        chunks = []
        for wi, base in ((0, 2048), (3, 2560)):
            for g in range(4):
                chunks.append((base + g * 128, "k", (wi, g)))
        for h in range(16):
            chunks.append((h * 128, "q", h))
        for wi, base in ((1, 3072), (2, 4096)):
            for g in range(4):
                chunks.append((base + g * 128, "k", (wi, g)))
        for c in range(16):
            chunks.append((5168 + c * 128, "z", c))
        n3h = None

        def loadw(i):
            self.load_w_cast(wst[i % 3], wst[i % 3][:], W, chunks[i][0], 128, wsl[i % 3])

        loadw(0)
        loadw(1)
        j = 0
        for i, (col0, kind, idx) in enumerate(chunks):
            if i + 2 < len(chunks):
                loadw(i + 2)
            w_ = wst[i % 3]
            g_ = stg[i % 2]
            for tb in range(4):
                ps_ = self.P[j % 4]
                j += 1
                sl_ = slice(tb * 512, (tb + 1) * 512)
                for k in range(16):
                    K.op(K.pe, lambda: nc.tensor.matmul(ps_[:], lhsT=w_[:, k, :], rhs=hT[:, k, sl_],
                                                        start=(k == 0), stop=(k == 15)),
                         reads=[w_, hT], writes=[ps_], inc=(k == 15))
                if kind == "q":
                    K.op(K.act, lambda: nc.scalar.mul(out=g_[:, sl_], in_=ps_[:], mul=float(DK) ** -0.5),
                         reads=[ps_], writes=[g_])
                elif kind == "z":
                    K.op(K.act, lambda: nc.scalar.activation(out=g_[:, sl_], in_=ps_[:], func=AF.Silu),
                         reads=[ps_], writes=[g_])
                else:
                    K.op(K.dve, lambda: nc.vector.tensor_copy(out=g_[:, sl_], in_=ps_[:]),
                         reads=[ps_], writes=[g_])
            if kind == "q":
                dst = self.QT[s].rearrange("t p h j -> p t h j")[:, :, idx, :]
                K.dma(K.sp, dst, g_[:].rearrange("p (t j) -> p t j", t=NT), gsl[i % 2], reads=[g_],
                      writes=[self.QTb[s]])
            elif kind == "z":
                dst = self.SZ[s].rearrange("t p h j -> p t h j")[:, :, idx, :]
                K.dma(K.sp, dst, g_[:].rearrange("p (t j) -> p t j", t=NT), gsl[i % 2], reads=[g_],
                      writes=[self.SZb[s]])
            else:
                wi, g = idx
                K.dma(K.sp, self.KT[s, wi, g], g_[:], gsl[i % 2], reads=[g_], writes=[self.KTb[s]])
            if i == 7:
                n3h = self.n3_load(st, s, n3args[0], n3args[1], n3args[2])
        wmv = [K.sb(st, f"wmv{i}", [128, 16, 512], BF16) for i in range(2)]
        wg = K.sb(st, "wg", [128, 16, 48], BF16)
        vst = [K.sb(st, f"vst{i}", [128, 4, 130], BF16) for i in range(2)]
        msl = [K.slot("wb1"), K.slot("wo")]
        vsl = [K.slot("yb0"), K.slot("yb1")]
        for i in range(2):
            K.op(K.pool, lambda: nc.gpsimd.memset(vst[i][:, :, 128:129], 1.0), writes=[vst[i]])
            K.op(K.pool, lambda: nc.gpsimd.memset(vst[i][:, :, 129:130], 0.0), writes=[vst[i]])
        for vi, base in enumerate((3584, 4608)):
            self.load_w_cast(wmv[vi], wmv[vi][:], W, base, 512, msl[vi])
        self.load_w_cast(wg, wg[:], W, 5120, 48, K.slot("const"))
        for vi in range(2):
            for tt in range(NT):
                ps_ = self.P[j % 4]
                j += 1
                for k in range(16):
                    K.op(K.pe, lambda: nc.tensor.matmul(ps_[:], lhsT=hT[:, k, tt * 128:(tt + 1) * 128],
                                                        rhs=wmv[vi][:, k, :], start=(k == 0), stop=(k == 15)),
                         reads=[wmv[vi], hT], writes=[ps_], inc=(k == 15))
                v_ = vst[tt % 2]
                eng = K.act if tt % 2 == 0 else K.dve
                src = ps_[:].rearrange("p (g d) -> p g d", g=4)
                if tt % 2 == 0:
                    K.op(K.act, lambda: nc.scalar.copy(out=v_[:, :, 0:128], in_=src), reads=[ps_], writes=[v_])
                else:
                    K.op(K.dve, lambda: nc.vector.tensor_copy(out=v_[:, :, 0:128], in_=src), reads=[ps_],
                         writes=[v_])
                K.dma(K.sp, self.VA[s, vi, tt], v_[:].rearrange("p g d -> p (g d)"), vsl[tt % 2], reads=[v_],
                      writes=[self.VAb[s]])
        for tt in range(NT):
            ps_ = self.P[j % 4]
            j += 1
            for k in range(16):
                K.op(K.pe, lambda: nc.tensor.matmul(ps_[:, 0:48], lhsT=hT[:, k, tt * 128:(tt + 1) * 128],
                                                    rhs=wg[:, k, :], start=(k == 0), stop=(k == 15)),
                     reads=[wg, hT], writes=[ps_], inc=(k == 15))
            K.op(K.act, lambda: nc.scalar.activation(out=gates[:, tt, :], in_=ps_[:, 0:48], func=AF.Sigmoid),
                 reads=[ps_], writes=[gates])
        self.n3_compute(n3h, n3args[1], n3args[2])

    def n3_load(self, st, s, p, KcT, VcO):
        K, nc, I = self.K, self.nc, self.I
        srcT = [K.sb(st, f"cT{i}", [128, 4, S], BF16) for i in range(2)]
        w1 = [K.sb(st, f"w1{i}", [128, 32, 128], BF16) for i in range(2)]
        w2 = [K.sb(st, f"w2{i}", [128, 128], BF16) for i in range(2)]
        HT = [K.sb(st, f"HT{i}", [128, 4, 127], BF16) for i in range(2)]
        posb = K.sb(st, "posb", [128, 32], BF16)
        cvec = K.sb(st, "cvec", [128, 2], F32)
        K.dma(K.sp, srcT[0][:], self.KT[s, 0].rearrange("g p t -> p g t"), K.slot("r0"), reads=[self.KTb[s]],
              writes=[srcT[0]])
        K.dma(K.sp, srcT[1][:], self.KT[s, 3].rearrange("g p t -> p g t"), K.slot("r1"), reads=[self.KTb[s]],
              writes=[srcT[1]])
        for i, nm in enumerate(("ck", "cv")):
            K.dma(K.pool, w1[i][:], I[p + nm + "_w1"].rearrange("(l d) j -> d l j", d=128), K.slot(f"r{2 + i}"),
                  writes=[w1[i]])
            K.dma(K.pool, w2[i][:], I[p + nm + "_w2"], K.slot(f"r{4 + i}"), writes=[w2[i]])
        K.dma(K.pool, posb[:], I[p + "posT"], K.slot("r6"), writes=[posb])
        K.op(K.pool, lambda: nc.gpsimd.memset(KcT[:], 0.0), writes=[KcT])
        K.op(K.pool, lambda: nc.gpsimd.memset(VcO[:], 0.0), writes=[VcO])
        for g in range(4):
            K.dma(K.pool, VcO[:, g, 128:162], I["c_ov"], K.slot("r6"), writes=[VcO])
        return (srcT, w1, w2, HT, posb, cvec)

    def n3_compute(self, h, KcT, VcO):
        K, nc, I = self.K, self.nc, self.I
        srcT, w1, w2, HT, posb, cvec = h
        for i in range(2):
            pc = self.P[4 + i]
            for l_ in range(32):
                K.op(K.pe, lambda: nc.tensor.matmul(pc[:, 0:1], lhsT=w1[i][:, l_, :], rhs=posb[:, l_:l_ + 1],
                                                    start=(l_ == 0), stop=(l_ == 31)),
                     reads=[w1[i], posb], writes=[pc], inc=(l_ == 31))
            K.op(K.act, lambda: nc.scalar.copy(out=cvec[:, i:i + 1], in_=pc[:, 0:1]), reads=[pc], writes=[cvec])
            ph = self.P[6 + i]
            for l_ in range(32):
                K.op(K.pe, lambda: nc.tensor.matmul(ph[:, 0:508].rearrange("p (g n) -> p g n", g=4),
                                                    lhsT=w1[i][:, l_, :],
                                                    rhs=srcT[i][:, :, l_:l_ + 16 * 126 + 1:16],
                                                    start=(l_ == 0), stop=(l_ == 31)),
                     reads=[w1[i], srcT[i]], writes=[ph], inc=(l_ == 31))
            K.op(K.act, lambda: nc.scalar.activation(out=HT[i][:], in_=ph[:, 0:508].rearrange("p (g n) -> p g n", g=4),
                                                     func=AF.Silu, bias=cvec[:, i:i + 1]),
                 reads=[ph, cvec], writes=[HT[i]])
        pk = self.P[4]
        K.op(K.pe, lambda: nc.tensor.matmul(pk[:, 0:508], lhsT=w2[0][:], rhs=HT[0][:].rearrange("p g n -> p (g n)"),
                                            start=True, stop=True),
             reads=[w2[0], HT[0]], writes=[pk])
        K.op(K.act, lambda: nc.scalar.copy(out=KcT[:, :, 0:127], in_=pk[:, 0:508].rearrange("p (g n) -> p g n", g=4)),
             reads=[pk], writes=[KcT])
        pv = self.P[5]
        for g in range(4):
            K.op(K.pe, lambda: nc.tensor.matmul(pv[0:127, g * 128:(g + 1) * 128], lhsT=HT[1][:, g, :], rhs=w2[1][:],
                                                start=True, stop=True),
                 reads=[w2[1], HT[1]], writes=[pv], inc=(g == 3))
        K.op(K.dve, lambda: nc.vector.tensor_copy(out=VcO[0:127, :, 0:128],
                                                  in_=pv[0:127, :].rearrange("p (g d) -> p g d", g=4)),
             reads=[pv], writes=[VcO])

    def n4_residents(self, st, s):
        K, nc, I = self.K, self.nc, self.I
        ksT = K.sb(st, "ksT", [128, 4, S], BF16)
        kwT = K.sb(st, "kwT", [128, 4, S], BF16)
        vsA = K.sb(st, "vsA", [128, NT, 520], BF16)
        vwA = K.sb(st, "vwA", [128, NT, 520], BF16)
        Eb = K.sb(st, "Eb", [128, 64, 128], BF16)
        TA = K.sb(st, "TA", [128, 16, 128], F32)
        TB = K.sb(st, "TB", [128, 16, 128], F32)
        keep = K.sb(st, "keep", [128, NT, 32], F32)
        addm = K.sb(st, "addm", [128, NT, 32], F32)
        M0 = K.sb(st, "M0", [128, 128], BF16)
        csl = K.slot("r6")
        K.dma(K.sp, ksT[:], self.KT[s, 1].rearrange("g p t -> p g t"), K.slot("r0"), reads=[self.KTb[s]], writes=[ksT])
        K.dma(K.sp, kwT[:], self.KT[s, 2].rearrange("g p t -> p g t"), K.slot("r1"), reads=[self.KTb[s]], writes=[kwT])
        K.dma(K.sp, vsA[:], self.VA[s, 0].rearrange("t p c -> p t c"), K.slot("r2"), reads=[self.VAb[s]], writes=[vsA])
        K.dma(K.sp, vwA[:], self.VA[s, 1].rearrange("t p c -> p t c"), K.slot("r3"), reads=[self.VAb[s]], writes=[vwA])
        for j in range(4):
            K.dma(K.pool, Eb[:, j * 16:(j + 1) * 16, :],
                  I["c_E"].rearrange("p (e k) -> p e k", k=128)[:, j * 16:(j + 1) * 16, :], K.slot("r4"), writes=[Eb],
                  group=(j > 0))
        K.dma(K.pool, M0[:], I["m_W0"], K.slot("r5"), writes=[M0])
        K.dma(K.sp, TA[:], self.TA, csl, reads=[self.Tb], writes=[TA])
        K.dma(K.sp, TB[:], self.TB, csl, reads=[self.Tb], writes=[TB])
        K.dma(K.sp, keep[:], I["c_keep"].rearrange("p (t s) -> p t s", s=32), csl, writes=[keep])
        K.dma(K.sp, addm[:], I["c_add"].rearrange("p (t s) -> p t s", s=32), csl, writes=[addm])
        return (ksT, kwT, vsA, vwA, Eb, TA, TB, keep, addm, M0)

    def n4_phase(self, st, s, KcT, VcO, gates, res):
        K, nc, I = self.K, self.nc, self.I
        ksT, kwT, vsA, vwA, Eb, TA, TB, keep, addm, M0 = res
        qr = [K.sb(st, f"qr{i}", [128, 16, 128], BF16) for i in range(2)]
        zr = [K.sb(st, f"zr{i}", [128, 16, 128], BF16) for i in range(2)]
        cbr = [K.sb(st, f"cbr{i}", [128, 16, 128], F32) for i in range(2)]
        NPT = 4
        pTr = [K.sb(st, f"pT{i}", [128, 512], BF16) for i in range(NPT)]
        pTc = [K.sb(st, f"pTc{i}", [128, 512], BF16) for i in range(4)]
        scr = [K.sb(st, f"sc{i}", [128, 512], F32) for i in range(2)]
        posb = [K.sb(st, f"posb{i}", [128, 2, 2, 129], F32) for i in range(4)]
        ptmp = K.sb(st, "ptmp", [128, 2, 2, 128], F32)
        csb = [K.sb(st, f"csb{i}", [128, 2, 162], F32) for i in range(2)]
        otiles = [K.sb(st, f"otile{i}", [128, D], F32) for i in range(2)]
        ob = K.sb(st, "ob", [128, D], BF16)
        oT = [K.sb(st, f"oT{i}", [128, 16, 128], BF16) for i in range(1)]
        impns = [K.sb(st, f"impn{i}", [128, 4, 32], F32) for i in range(2)]
        t8s = [K.sb(st, f"t8{i}", [128, 4, 8], F32) for i in range(2)]
        selms = [K.sb(st, f"selm{i}", [128, 4, 32], F32) for i in range(2)]
        selTs = [K.sb(st, f"selT{i}", [128, 128], BF16) for i in range(2)]
        rcs = [K.sb(st, f"rc{i}", [128, 8], F32) for i in range(6)]
        qsl = [K.slot("ys0"), K.slot("ys1")]
        zsl = [K.slot("yb0"), K.slot("yb1")]
        bsl = [K.slot("wb0"), K.slot("wb1")]
        osl = [K.slot("o0"), K.slot("o1")]
        SCB = [self.P[0], self.P[1], self.P[7]]
        Pm = self.P[6]
        pm = self.pt[:, 6, :]
        po_sel = self.pt[:, 2:4, :]
        po_win = self.pt[:, 4:6, :]
        Psel = [self.P[2], self.P[3]]
        Pwin = [self.P[4], self.P[5]]

        def load_qc(tt):
            K.dma(K.sp, qr[tt % 2][:], self.QT[s, tt], qsl[tt % 2], reads=[self.QTb[s]], writes=[qr[tt % 2]])
            K.dma(K.sp, cbr[tt % 2][:], self.TC[120 - 8 * tt:248 - 8 * tt], bsl[tt % 2], reads=[self.Tb],
                  writes=[cbr[tt % 2]])

        def load_z(tt):
            K.dma(K.sp, zr[tt % 2][:], self.SZ[s, tt], zsl[tt % 2], reads=[self.SZb[s]], writes=[zr[tt % 2]])

        cnt = {"u": 0, "rc": 0, "pb": 0, "cs": 0}

        def cmp_half(g, half, T):
            otile = otiles[T % 2]
            impn = impns[T % 2]
            rc = rcs[cnt["rc"] % 6]
            cnt["rc"] += 1
            cs = csb[cnt["cs"] % 2]
            cnt["cs"] += 1
            K.op(K.dve, lambda: nc.vector.tensor_copy(out=cs[:], in_=pm[:, 0:324].rearrange("p (a c) -> p a c", a=2)),
                 reads=[Pm], writes=[cs])
            K.op(K.dve, lambda: nc.vector.tensor_scalar(out=rc[:, 0:2], in0=cs[:, :, 128], scalar1=1e-30,
                                                        scalar2=None, op0=ALU.max),
                 reads=[cs], writes=[rc])
            K.op(K.dve, lambda: nc.vector.reciprocal(out=rc[:, 0:2], in_=rc[:, 0:2]), reads=[rc], writes=[rc])
            h0 = 4 * g + 2 * half
            K.op(K.dve, lambda: nc.vector.tensor_tensor(out=rc[:, 4:6], in0=rc[:, 0:2],
                                                        in1=gates[:, T, h0 * 3:h0 * 3 + 4:3], op=ALU.mult),
                 reads=[rc, gates], writes=[rc])
            K.op(K.pool, lambda: nc.gpsimd.tensor_tensor(
                out=otile[:, h0 * 128:(h0 + 2) * 128].rearrange("p (a d) -> p a d", a=2), in0=cs[:, :, 0:128],
                in1=rc[:, 4:6].unsqueeze(2).broadcast_to([128, 2, 128]), op=ALU.mult),
                reads=[cs, rc], writes=[otile])
            for q2 in range(2):
                src = cs[:, q2, 130:162]
                if half == 0 and q2 == 0:
                    K.op(K.dve, lambda: nc.vector.tensor_scalar(out=impn[:, g, :], in0=src, scalar1=rc[:, 0:1],
                                                                scalar2=None, op0=ALU.mult),
                         reads=[cs, rc], writes=[impn])
                else:
                    K.op(K.dve, lambda: nc.vector.scalar_tensor_tensor(
                        out=impn[:, g, :], in0=src, scalar=rc[:, q2:q2 + 1], in1=impn[:, g, :],
                        op0=ALU.mult, op1=ALU.add), reads=[cs, rc, impn], writes=[impn])

        def fin_branch(po_view, Pb, g, T, branch):
            otile = otiles[T % 2]
            pb_ = posb[cnt["pb"] % 4]
            cnt["pb"] += 1
            rc = rcs[cnt["rc"] % 6]
            cnt["rc"] += 1
            K.op(K.dve, lambda: nc.vector.tensor_copy(
                out=pb_[:], in_=po_view[:, :, 0:258].rearrange("p a (b c) -> p a b c", c=129)),
                reads=Pb, writes=[pb_])
            K.op(K.dve, lambda: nc.vector.reciprocal(out=rc[:, 0:4].rearrange("p (a b) -> p a b", a=2),
                                                     in_=pb_[:, :, :, 128]),
                 reads=[pb_], writes=[rc])
            gcol = (4 * g) * 3 + branch
            K.op(K.dve, lambda: nc.vector.tensor_tensor(out=rc[:, 4:8], in0=rc[:, 0:4],
                                                        in1=gates[:, T, gcol:gcol + 10:3], op=ALU.mult),
                 reads=[rc, gates], writes=[rc])
            K.op(K.pool, lambda: nc.gpsimd.tensor_tensor(
                out=ptmp[:], in0=pb_[:, :, :, 0:128],
                in1=rc[:, 4:8].rearrange("p (a b) -> p a b", a=2).unsqueeze(3).broadcast_to([128, 2, 2, 128]),
                op=ALU.mult), reads=[pb_, rc], writes=[ptmp])
            og = otile[:, 4 * g * 128:(4 * g + 4) * 128].rearrange("p (a b d) -> p a b d", a=2, b=2)
            K.op(K.pool, lambda: nc.gpsimd.tensor_tensor(out=og, in0=og, in1=ptmp[:], op=ALU.add),
                 reads=[otile, ptmp], writes=[otile])

        if self.dbg < 5:
            return

        def topk_stages(T):
            impn, t8, selm, selT = impns[T % 2], t8s[T % 2], selms[T % 2], selTs[T % 2]

            def E():
                K.op(K.dve, lambda: nc.vector.tensor_tensor(out=impn[:], in0=impn[:],
                                                            in1=keep[:, T, :].unsqueeze(1).broadcast_to([128, 4, 32]),
                                                            op=ALU.mult),
                     reads=[impn, keep], writes=[impn])
                K.op(K.dve, lambda: nc.vector.tensor_tensor(out=impn[:], in0=impn[:],
                                                            in1=addm[:, T, :].unsqueeze(1).broadcast_to([128, 4, 32]),
                                                            op=ALU.add),
                     reads=[impn, addm], writes=[impn])

            def F():
                for g in range(4):
                    K.op(K.dve, lambda: nc.vector.max(out=t8[:, g, :], in_=impn[:, g, :]), reads=[impn], writes=[t8])

            def G():
                for g in range(4):
                    K.op(K.dve, lambda: nc.vector.tensor_scalar(out=selm[:, g, :], in0=impn[:, g, :],
                                                                scalar1=t8[:, g, 7:8], scalar2=-1.0,
                                                                op0=ALU.is_ge, op1=ALU.add),
                         reads=[impn, t8], writes=[selm])

            def H():
                K.op(K.pe, lambda: nc.tensor.transpose(out=pm[:, 0:128], in_=selm[:].rearrange("p g s -> p (g s)"),
                                                       identity=self.ident_f[:]),
                     reads=[selm, self.ident_f], writes=[Pm])

            def I_():
                K.op(K.dve, lambda: nc.vector.tensor_copy(out=selT[:], in_=pm[:, 0:128]), reads=[Pm], writes=[selT])

            return [E, F, G, H, I_]

        def final_stages(T):
            otile = otiles[T % 2]
            z_ = zr[T % 2]
            pv = pm.bitcast(BF16)
            o_ = oT[0]

            def cp():
                K.op(K.dve, lambda: nc.vector.tensor_copy(out=ob[:], in_=otile[:]), reads=[otile], writes=[ob])

            def tr(half):
                for c8 in range(8):
                    c = half * 8 + c8
                    K.op(K.pe, lambda: nc.tensor.transpose(out=pv[:, c8 * 128:(c8 + 1) * 128],
                                                           in_=ob[:, c * 128:(c + 1) * 128], identity=self.ident_b[:]),
                         reads=[ob, self.ident_b], writes=[Pm], inc=(c8 == 7))

            def mu(half):
                K.op(K.dve, lambda: nc.vector.tensor_tensor(
                    out=o_[:, half * 8:(half + 1) * 8, :], in0=pv[:].rearrange("p (c t) -> p c t", c=8),
                    in1=z_[:, half * 8:(half + 1) * 8, :], op=ALU.mult),
                    reads=[Pm, z_], writes=[o_])
                if half == 1:
                    K.dma(K.sp, self.WT[s, T], o_[:], osl[T % 2], reads=[o_], writes=[self.WTb[s]])

            return [(6, cp), (4, lambda: tr(0)), (2, lambda: mu(0)), (2, lambda: tr(1)), (2, lambda: mu(1))]

        def run_tile(T, with_units, Tc, fifo):
            allu = []
            if with_units:
                def spread(kcs):
                    rest_ = [k for k in kcs if k not in (T, T - 1)]
                    out = rest_[:1] + [T] + rest_[1:3] + ([T - 1] if T - 1 in kcs else []) + rest_[3:]
                    return out

                for g in range(4):
                    for br, kcs in (("s", list(range(T + 1))), ("w", list(range(max(0, T - 4), T + 1)))):
                        order = spread(kcs)
                        assert sorted(order) == kcs
                        for i_, kc in enumerate(order):
                            allu.append((g, br, kc, i_ == 0, i_ == len(order) - 1))
            if Tc is not None:
                n0 = len(allu)
                start = min(n0, max(6, n0 // 4))
                gap = max(1, (n0 - start) // 5)
                for g in range(4):
                    allu.insert(min(len(allu), start + g * (gap + 1)), (g, "c", None, True, True))
            N = len(allu)
            ustate = {}
            cdone = {"n": 0}
            since = {"n": 0}
            pvq = []

            def qk(n):
                g, br, kc, _f, _l = allu[n]
                u = cnt["u"]
                cnt["u"] += 1
                ustate[n] = u
                ps_ = SCB[u % 3]
                if br == "c":
                    q_ = qr[Tc % 2]
                    K.op(K.pe, lambda: nc.tensor.matmul(ps_[:], lhsT=KcT[:, g, :], rhs=q_[:, 4 * g:4 * g + 4, :],
                                                        start=True, stop=True),
                         reads=[KcT, q_], writes=[ps_])
                    return
                q_ = qr[T % 2]
                selT = selTs[T % 2]
                kT = ksT if br == "s" else kwT
                extra = (br == "s" and kc < T) or (br == "w" and kc == T - 4)
                K.op(K.pe, lambda: nc.tensor.matmul(ps_[:], lhsT=kT[:, g, kc * 128:(kc + 1) * 128],
                                                    rhs=q_[:, 4 * g:4 * g + 4, :], start=True, stop=not extra),
                     reads=[kT, q_], writes=[ps_], inc=not extra)
                if br == "s" and kc < T:
                    K.op(K.pe, lambda: nc.tensor.matmul(ps_[:], lhsT=Eb[:, g * 16 + kc, :],
                                                        rhs=selT[:].unsqueeze(1).broadcast_to([128, 4, 128]),
                                                        start=False, stop=True),
                         reads=[Eb, selT], writes=[ps_])
                elif br == "w" and kc == T - 4:
                    K.op(K.pe, lambda: nc.tensor.matmul(ps_[:], lhsT=self.ident_b[:],
                                                        rhs=M0[:].unsqueeze(1).broadcast_to([128, 4, 128]),
                                                        start=False, stop=True),
                         reads=[self.ident_b, M0], writes=[ps_])

            def rest_cmp(g, u):
                sc_, pT_ = scr[u % 2], pTc[g]
                K.op(K.act, lambda: nc.scalar.activation(out=pT_[:], in_=sc_[:], func=AF.Exp),
                     reads=[sc_], writes=[pT_])

                def pv(half):
                    for q2 in range(2):
                        r = half * 2 + q2
                        K.op(K.pe, lambda: nc.tensor.matmul(pm[:, q2 * 162:q2 * 162 + 162],
                                                            lhsT=pT_[:, r * 128:(r + 1) * 128], rhs=VcO[:, g, :],
                                                            start=True, stop=True),
                             reads=[pT_, VcO], writes=[Pm], inc=(q2 == 1))

                fifo.append((1, lambda: pv(0)))
                fifo.append((1, lambda: cmp_half(g, 0, Tc)))
                fifo.append((5, lambda: pv(1)))
                fifo.append((1, lambda: cmp_half(g, 1, Tc)))
                cdone["n"] += 1
                if cdone["n"] == 4:
                    fifo.extend(zip((4, 1, 1, 2, 2), topk_stages(Tc)))

            def pre(n):
                g, br, kc, _f, _l = allu[n]
                u = ustate[n]
                if br == "c":
                    tab = cbr[Tc % 2]
                else:
                    tab = TB if kc == T else (TA if kc == T - 1 else None)
                if tab is None:
                    return
                ps_, sc_ = SCB[u % 3], scr[u % 2]
                K.op(K.dve, lambda: nc.vector.tensor_tensor(
                    out=sc_[:], in0=ps_[:], in1=tab[:, 4 * g:4 * g + 4, :].rearrange("p h j -> p (h j)"),
                    op=ALU.add), reads=[ps_, tab], writes=[sc_])

            def rest(n):
                g, br, kc, first, last = allu[n]
                u = ustate.pop(n)
                if br == "c":
                    rest_cmp(g, u)
                    return
                ps_ = SCB[u % 3]
                pT_ = pTr[u % NPT]
                tab = TB if kc == T else (TA if kc == T - 1 else None)
                if tab is not None:
                    sc_ = scr[u % 2]
                    K.op(K.act, lambda: nc.scalar.activation(out=pT_[:], in_=sc_[:], func=AF.Exp),
                         reads=[sc_], writes=[pT_])
                else:
                    K.op(K.act, lambda: nc.scalar.activation(out=pT_[:], in_=ps_[:], func=AF.Exp),
                         reads=[ps_], writes=[pT_])
                pvq.append((g, br, kc, first, last, pT_))

            def pv_emit():
                if not pvq:
                    return
                g, br, kc, first, last, pT_ = pvq.pop(0)
                po_view, Pb, vA = (po_sel, Psel, vsA) if br == "s" else (po_win, Pwin, vwA)
                for r in range(4):
                    K.op(K.pe, lambda: nc.tensor.matmul(
                        po_view[:, r // 2, (r % 2) * 129:(r % 2) * 129 + 129],
                        lhsT=pT_[:, r * 128:(r + 1) * 128], rhs=vA[:, kc, g * 130:g * 130 + 129],
                        start=(first and r % 2 == 0), stop=last, skip_group_check=True),
                        reads=[pT_, vA], writes=[Pb[r // 2]], inc=(r == 3))
                if last:
                    fin_branch(po_view, Pb, g, T, 1 if br == "s" else 2)

            for n in range(N):
                if n == 0:
                    qk(0)
                    if N > 1:
                        qk(1)
                    pre(0)
                if n + 2 < N:
                    qk(n + 2)
                if n + 1 < N:
                    pre(n + 1)
                pv_emit()
                rest(n)
                since["n"] += 1
                if fifo and since["n"] >= fifo[0][0]:
                    fifo.pop(0)[1]()
                    since["n"] = 0
            pv_emit()
            while fifo:
                fifo.pop(0)[1]()

        load_qc(0)
        run_tile(0, False, 0, [])
        for tt in range(NT):
            if tt + 1 < NT:
                load_qc(tt + 1)
            load_z(tt)
            fifo = final_stages(tt - 1) if tt > 0 else []
            run_tile(tt, True, tt + 1 if tt + 1 < NT else None, fifo)
        for _d, f in final_stages(NT - 1):
            f()


def host_consts():
    c = {}
    c["c_ident"] = np.eye(128, dtype=np.float32)
    return c


def layer_inputs(inputs, layers):
    m = {}
    for l in layers:
        p = f"l{l}_"
        m[p + "norm"] = np.ascontiguousarray(inputs[p + "norm"].reshape(1, D))
        m[p + "w_out"] = inputs[p + "w_out"]
        m[p + "w_in"] = inputs[p + "w_in"]
        if l % 2 == 0:
            dw = inputs[p + "dw_w"].reshape(31, 16, 128).transpose(2, 1, 0)
            m[p + "dw_wT"] = np.ascontiguousarray(dw.reshape(128, 16 * 31))
            for nm in ("dw_b", "ln_g", "ln_b"):
                m[p + nm] = np.ascontiguousarray(inputs[p + nm].reshape(16, 128).T)
        else:
            m[p + "posT"] = np.ascontiguousarray(inputs[p + "cmp_pos"].T)
            for nm in ("ck_w1", "ck_w2", "cv_w1", "cv_w2"):
                m[p + nm] = inputs[p + nm]
    m["final_norm"] = np.ascontiguousarray(inputs["final_norm"].reshape(1, D))
    return m


_PROG_CACHE = {}


def run(inputs, layers=(0, 1, 2, 3), final_norm=True, ncores=NCORES, nseq=NSEQ, x_override=None, trace=False, dbg=99):
    key = (tuple(layers), final_norm, nseq, dbg)
    if key not in _PROG_CACHE:
        _PROG_CACHE[key] = Prog(layers, final_norm, nseq, dbg)
    prog = _PROG_CACHE[key]
    shared = dict(host_consts())
    shared.update(layer_inputs(inputs, layers))
    if any(l % 2 == 1 for l in layers):
        shared.update(nsa_host_tables(inputs["rel_bias"]))
    x = inputs["x"] if x_override is None else x_override
    in_maps = []
    for c in range(ncores):
        m = dict(shared)
        m["x"] = np.ascontiguousarray(x[c * nseq:(c + 1) * nseq])
        in_maps.append(m)
    res = run_bass_kernel_spmd(prog.nc, in_maps, core_ids=list(range(ncores)), **({'trace': True} if trace else {}))
    if trace:
        print('exec_time_ns', res.exec_time_ns)
    return np.concatenate([r["y"] for r in res.results], axis=0)


def _t5_bucket_np(dist):
    dist = np.maximum(dist, 0)
    d = np.maximum(dist, 16).astype(np.float32)
    large = 16 + (np.log(d / np.float32(16)) / np.float32(np.log(8.0)) * np.float32(16)).astype(np.int32)
    large = np.minimum(large, 31)
    return np.where(dist < 16, dist, large)


def nsa_host_tables(rel_bias):
    m = {}
    k = np.arange(128)[:, None]
    t = np.arange(128)[None, :]
    dA = t - k + 128
    dB = t - k
    mp = np.arange(248)[:, None]
    dC = t - 16 * (mp - 120) - 31
    m["rb_A"] = np.ascontiguousarray(rel_bias[_t5_bucket_np(dA)].transpose(0, 2, 1))
    m["rb_B"] = np.ascontiguousarray(rel_bias[_t5_bucket_np(dB)].transpose(0, 2, 1))
    m["rb_C"] = np.ascontiguousarray(rel_bias[_t5_bucket_np(dC)].transpose(0, 2, 1))
    m["rb_far"] = np.ascontiguousarray(rel_bias[31:32, :])
    m["m_B"] = np.where(dB >= 0, 0.0, NEGM).astype(np.float32)
    m["m_C"] = np.where(dC >= 0, 0.0, NEGM).astype(np.float32)
    m["m_W0"] = np.where(k > t, 0.0, NEGM).astype(np.float32)
    E = np.zeros((4, 32, 4, 16, 128), np.float32)
    for g in range(4):
        for kc in range(16):
            E[g, 2 * kc, g, kc, 0:64] = -NEGM
            E[g, 2 * kc + 1, g, kc, 64:128] = -NEGM
    m["c_E"] = np.ascontiguousarray(E.reshape(128, 64 * 128))
    tl = np.arange(128)[:, None, None]
    ti = np.arange(16)[None, :, None]
    sb = np.arange(32)[None, None, :]
    tabs = ti * 128 + tl
    cur = tabs // 64
    forced = (sb == 0) | (sb == cur) | (sb == cur - 1)
    causal = sb * 64 <= tabs
    keep = (causal & ~forced).astype(np.float32)
    add = np.where(causal, np.where(forced, 1e6, 0.0), -1e30).astype(np.float32)
    m["c_keep"] = np.ascontiguousarray(keep.reshape(128, 512))
    m["c_add"] = np.ascontiguousarray(add.reshape(128, 512))
    ov = np.zeros((128, 34), np.float32)
    n = np.arange(127)[:, None]
    s0 = np.arange(32)[None, :] * 64
    j0 = n * 16
    ov[:127, 0] = 1.0
    ov[:127, 2:34] = ((j0 < s0 + 64) & (j0 + 32 > s0)).astype(np.float32)
    m["c_ov"] = ov
    return m


def kernel(**inputs):
    inputs = {k: np.asarray(v) for k, v in inputs.items()}
    return run(inputs)
```
